# Optimizing a Trainium2 kernel written in Bass

```python
import math
import jax, jax.numpy as jnp
from jax import lax
import numpy as np

D_MODEL = 2048
BATCH = 2
SEQ = 8192
DEPTH = 1

MEM_LEN = 256
LRU_WIDTH = D_MODEL // 2
LRU_BLOCKS = 8
LRU_BLOCK_DIM = LRU_WIDTH // LRU_BLOCKS
LRU_CONV_WIDTH = 4
LRU_C = 8.0
NSA_HEADS = 8
NSA_KV_HEADS = 2
NSA_HEAD_DIM = (D_MODEL // 4) // NSA_HEADS
CMP_STRIDE = 16
CMP_BLOCK = 2 * CMP_STRIDE
SLC_BLOCK = 64
N_SELECT = 16
WINDOW = 512
Q_BLOCK = 128
MEM_HEADS = 4
MEM_HEAD_DIM = (D_MODEL // 4) // MEM_HEADS
D_FF = ((8 * D_MODEL // 3) + 255) // 256 * 256
FFN_CONV_WIDTH = 3

LN_EPS = 1e-5
NEG_INF = -1e30
FORCE_SCORE = 1e9

kernel_name = 'hymba_style_lru_nsa_memxattn_convffn_deepnorm'


def layer_norm(x, g, b):
    xf = x.astype(jnp.float32)
    mu = jnp.mean(xf, axis=-1, keepdims=True)
    var = jnp.mean(jnp.square(xf - mu), axis=-1, keepdims=True)
    y = (xf - mu) * lax.rsqrt(var + LN_EPS)
    return (y * g + b).astype(x.dtype)


def causal_dwconv(x, w, b):
    K = w.shape[0]
    S = x.shape[1]
    xp = jnp.pad(x, ((0, 0), (K - 1, 0), (0, 0)))
    y = b
    for k in range(K):
        y = y + xp[:, k:k + S] * w[k]
    return y


def masked_softmax(s, mask):
    p = jax.nn.softmax(jnp.where(mask, s, NEG_INF), axis=-1)
    return jnp.where(mask, p, 0.0)


def rg_lru_group(xr, yg, conv_w, conv_b, wa, ba, wx, bx, lam):
    B, S, C = xr.shape
    f32 = jnp.float32
    xc = causal_dwconv(xr, conv_w, conv_b).astype(f32)
    xb = xc.reshape(B, S, LRU_BLOCKS, LRU_BLOCK_DIM)
    r = jax.nn.sigmoid(jnp.einsum('bsnc,ncd->bsnd', xb, wa.astype(f32)) + ba).reshape(B, S, C)
    i = jax.nn.sigmoid(jnp.einsum('bsnc,ncd->bsnd', xb, wx.astype(f32)) + bx).reshape(B, S, C)
    log_a = -LRU_C * r * jax.nn.softplus(-lam.astype(f32))
    a = jnp.exp(log_a)
    u = jnp.sqrt(-jnp.expm1(2.0 * log_a)) * (i * xc)

    def combine(left, right):
        a1, b1 = left
        a2, b2 = right
        return a1 * a2, a2 * b1 + b2

    _, h = lax.associative_scan(combine, (a, u), axis=1)
    return (jax.nn.gelu(yg.astype(f32)) * h).astype(xr.dtype)


def compress_blocks(kv, pe, w1, w2):
    B, S, G, Dh = kv.shape
    sub = kv.reshape(B, S // CMP_STRIDE, CMP_STRIDE, G, Dh)
    blocks = jnp.concatenate([sub[:, :-1], sub[:, 1:]], axis=2)
    blocks = blocks + pe[None, None, :, None, :]
    hid = jax.nn.gelu(jnp.einsum('bnlgd,lde->bgne', blocks, w1))
    return jnp.einsum('bgne,ef->bgnf', hid, w2)


def nsa_group(q, kv6, gate_logits, pe_k, w1_k, w2_k, pe_v, w1_v, w2_v):
    B, S, H, Dh = q.shape
    G = NSA_KV_HEADS
    HPG = H // G
    f32 = jnp.float32
    kc_raw, vc_raw, ks, vs, kw, vw = [t.reshape(B, S, G, Dh) for t in kv6]
    k_cmp = compress_blocks(kc_raw, pe_k, w1_k, w2_k)
    v_cmp = compress_blocks(vc_raw, pe_v, w1_v, w2_v)
    n_cmp = k_cmp.shape[2]
    n_slc = S // SLC_BLOCK
    n_top = min(N_SELECT, n_slc)
    k_blk = ks.reshape(B, n_slc, SLC_BLOCK, G, Dh).transpose(0, 3, 1, 2, 4)
    v_blk = vs.reshape(B, n_slc, SLC_BLOCK, G, Dh).transpose(0, 3, 1, 2, 4)
    k_win = jnp.pad(kw, ((0, 0), (WINDOW, 0), (0, 0), (0, 0)))
    v_win = jnp.pad(vw, ((0, 0), (WINDOW, 0), (0, 0), (0, 0)))
    n_q = S // Q_BLOCK
    q_chunks = q.reshape(B, n_q, Q_BLOCK, G, HPG, Dh).transpose(1, 0, 3, 4, 2, 5)
    g_chunks = jax.nn.sigmoid(gate_logits.astype(f32)).reshape(B, n_q, Q_BLOCK, G, HPG, 3).transpose(1, 0, 3, 4, 2, 5)

    cmp_start = jnp.arange(n_cmp) * CMP_STRIDE
    cmp_end = cmp_start + CMP_BLOCK - 1
    blk = jnp.arange(n_slc)
    slc_start = blk * SLC_BLOCK
    cover = ((cmp_start[:, None] <= slc_start[None, :] + SLC_BLOCK - 1)
             & (cmp_end[:, None] >= slc_start[None, :])).astype(f32)
    b_idx = jnp.arange(B)[:, None, None, None]
    g_idx = jnp.arange(G)[None, :, None, None]
    scale = Dh ** -0.5

    def block_fn(args):
        ci, qc, gc = args
        q0 = ci * Q_BLOCK
        t = q0 + jnp.arange(Q_BLOCK)
        s_c = jnp.einsum('bghqd,bgnd->bghqn', qc, k_cmp).astype(f32) * scale
        p_c = masked_softmax(s_c, cmp_end[None, :] <= t[:, None])
        o_c = jnp.einsum('bghqn,bgnd->bghqd', p_c.astype(v_cmp.dtype), v_cmp)
        imp = jnp.einsum('bghqn,ns->bgqs', p_c, cover)
        cur = (t // SLC_BLOCK)[:, None]
        forced = (blk[None, :] == 0) | (blk[None, :] == cur) | (blk[None, :] == cur - 1)
        visible = slc_start[None, :] <= t[:, None]
        imp = jnp.where(forced, FORCE_SCORE, jnp.where(visible, imp, NEG_INF))
        _, idx = lax.top_k(imp, n_top)
        k_sel = k_blk[b_idx, g_idx, idx]
        v_sel = v_blk[b_idx, g_idx, idx]
        s_s = jnp.einsum('bghqd,bgqnkd->bghqnk', qc, k_sel).astype(f32) * scale
        kpos = idx[..., None] * SLC_BLOCK + jnp.arange(SLC_BLOCK)
        m_s = (kpos <= t[None, None, :, None, None]).reshape(B, G, 1, Q_BLOCK, -1)
        p_s = masked_softmax(s_s.reshape(B, G, HPG, Q_BLOCK, -1), m_s)
        o_s = jnp.einsum('bghqm,bgqmd->bghqd', p_s.astype(v_sel.dtype),
                         v_sel.reshape(B, G, Q_BLOCK, -1, Dh))
        k_w = lax.dynamic_slice_in_dim(k_win, q0, Q_BLOCK + WINDOW, axis=1)
        v_w = lax.dynamic_slice_in_dim(v_win, q0, Q_BLOCK + WINDOW, axis=1)
        s_w = jnp.einsum('bghqd,bkgd->bghqk', qc, k_w).astype(f32) * scale
        kp = q0 - WINDOW + jnp.arange(Q_BLOCK + WINDOW)
        m_w = (kp[None, :] <= t[:, None]) & (kp[None, :] > t[:, None] - WINDOW) & (kp[None, :] >= 0)
        p_w = masked_softmax(s_w, m_w)
        o_w = jnp.einsum('bghqk,bkgd->bghqd', p_w.astype(v_w.dtype), v_w)
        gc = gc.astype(f32)
        o = gc[..., 0:1] * o_c + gc[..., 1:2] * o_s + gc[..., 2:3] * o_w
        return o.astype(qc.dtype)

    out = lax.map(block_fn, (jnp.arange(n_q), q_chunks, g_chunks))
    return out.transpose(1, 0, 4, 2, 3, 5).reshape(B, S, H * Dh)


def memory_group(q, mem, w_mem_kv):
    B, S, _ = q.shape
    k, v = jnp.split(mem @ w_mem_kv, 2, axis=-1)
    k = k.reshape(B, -1, MEM_HEADS, MEM_HEAD_DIM)
    v = v.reshape(B, -1, MEM_HEADS, MEM_HEAD_DIM)
    qh = q.reshape(B, S, MEM_HEADS, MEM_HEAD_DIM)
    s = jnp.einsum('bshd,bmhd->bhsm', qh, k).astype(jnp.float32) * (MEM_HEAD_DIM ** -0.5)
    p = jax.nn.softmax(s, axis=-1).astype(v.dtype)
    return jnp.einsum('bhsm,bmhd->bshd', p, v).reshape(B, S, MEM_HEADS * MEM_HEAD_DIM)


def conv_ffn(h, w_up, conv_w, conv_b, w_down):
    up = causal_dwconv(h @ w_up, conv_w, conv_b)
    g, u = jnp.split(up, 2, axis=-1)
    return (jax.nn.gelu(g) * u) @ w_down


def hybrid_layer(h, mem, w_in, lru_conv_w, lru_conv_b, lru_wa, lru_ba, lru_wx, lru_bx, lru_lam,
                 cmp_pe_k, cmp_w1_k, cmp_w2_k, cmp_pe_v, cmp_w1_v, cmp_w2_v, w_mem_kv, w_out,
                 ln1_g, ln1_b, ffn_w_up, ffn_conv_w, ffn_conv_b, ffn_w_down, ln2_g, ln2_b):
    alpha = (2.0 * DEPTH) ** 0.25
    B, S, _ = h.shape
    G, Dh = NSA_KV_HEADS, NSA_HEAD_DIM
    sizes = [LRU_WIDTH, LRU_WIDTH, NSA_HEADS * Dh] + [G * Dh] * 6 + [3 * NSA_HEADS, MEM_HEADS * MEM_HEAD_DIM]
    parts = jnp.split(h @ w_in, np.cumsum(sizes)[:-1].tolist(), axis=-1)
    lru_x, lru_y, nsa_q = parts[0], parts[1], parts[2]
    nsa_kv = parts[3:9]
    nsa_gate, mem_q = parts[9], parts[10]
    o_lru = rg_lru_group(lru_x, lru_y, lru_conv_w, lru_conv_b, lru_wa, lru_ba, lru_wx, lru_bx, lru_lam)
    o_nsa = nsa_group(nsa_q.reshape(B, S, NSA_HEADS, Dh), nsa_kv, nsa_gate.reshape(B, S, NSA_HEADS, 3),
                      cmp_pe_k, cmp_w1_k, cmp_w2_k, cmp_pe_v, cmp_w1_v, cmp_w2_v)
    o_mem = memory_group(mem_q, mem, w_mem_kv)
    mixed = jnp.concatenate([o_lru, o_nsa.astype(o_lru.dtype), o_mem.astype(o_lru.dtype)], axis=-1) @ w_out
    h = layer_norm(alpha * h + mixed, ln1_g, ln1_b)
    h = layer_norm(alpha * h + conv_ffn(h, ffn_w_up, ffn_conv_w, ffn_conv_b, ffn_w_down), ln2_g, ln2_b)
    return h


def setup_inputs(seed: int = 0) -> dict:
    key = jax.random.key(seed)
    ks = jax.random.split(key, 32)
    f32 = jnp.float32
    L = DEPTH
    beta = (8.0 * DEPTH) ** -0.25
    G, Dh = NSA_KV_HEADS, NSA_HEAD_DIM
    n_in = 2 * LRU_WIDTH + NSA_HEADS * Dh + 6 * G * Dh + 3 * NSA_HEADS + MEM_HEADS * MEM_HEAD_DIM
    mix_width = LRU_WIDTH + NSA_HEADS * Dh + MEM_HEADS * MEM_HEAD_DIM

    def nrm(k, shape, scale):
        return jax.random.normal(k, shape, f32) * scale

    u = jax.random.uniform(ks[7], (L, LRU_WIDTH), f32, 0.9, 0.999)
    s = u ** (1.0 / LRU_C)
    lru_lam = jnp.log(s) - jnp.log1p(-s)
    return {
        'x': nrm(ks[0], (BATCH, SEQ, D_MODEL), 1.0),
        'mem': nrm(ks[1], (BATCH, MEM_LEN, D_MODEL), 1.0),
        'ln_in_g': 1.0 + nrm(ks[2], (D_MODEL,), 0.02),
        'ln_in_b': nrm(ks[3], (D_MODEL,), 0.02),
        'w_in': nrm(ks[4], (L, D_MODEL, n_in), D_MODEL ** -0.5),
        'lru_conv_w': nrm(ks[5], (L, LRU_CONV_WIDTH, LRU_WIDTH), LRU_CONV_WIDTH ** -0.5),
        'lru_conv_b': nrm(ks[6], (L, LRU_WIDTH), 0.01),
        'lru_wa': nrm(ks[8], (L, LRU_BLOCKS, LRU_BLOCK_DIM, LRU_BLOCK_DIM), LRU_BLOCK_DIM ** -0.5),
        'lru_ba': nrm(ks[9], (L, LRU_BLOCKS, LRU_BLOCK_DIM), 0.01),
        'lru_wx': nrm(ks[10], (L, LRU_BLOCKS, LRU_BLOCK_DIM, LRU_BLOCK_DIM), LRU_BLOCK_DIM ** -0.5),
        'lru_bx': nrm(ks[11], (L, LRU_BLOCKS, LRU_BLOCK_DIM), 0.01),
        'lru_lam': lru_lam,
        'cmp_pe_k': nrm(ks[12], (L, CMP_BLOCK, Dh), 0.02),
        'cmp_w1_k': nrm(ks[13], (L, CMP_BLOCK, Dh, Dh), (CMP_BLOCK * Dh) ** -0.5),
        'cmp_w2_k': nrm(ks[14], (L, Dh, Dh), Dh ** -0.5),
        'cmp_pe_v': nrm(ks[15], (L, CMP_BLOCK, Dh), 0.02),
        'cmp_w1_v': nrm(ks[16], (L, CMP_BLOCK, Dh, Dh), (CMP_BLOCK * Dh) ** -0.5),
        'cmp_w2_v': nrm(ks[17], (L, Dh, Dh), Dh ** -0.5),
        'w_mem_kv': nrm(ks[18], (L, D_MODEL, 2 * MEM_HEADS * MEM_HEAD_DIM), D_MODEL ** -0.5),
        'w_out': nrm(ks[19], (L, mix_width, D_MODEL), beta * mix_width ** -0.5),
        'ln1_g': 1.0 + nrm(ks[20], (L, D_MODEL), 0.02),
        'ln1_b': nrm(ks[21], (L, D_MODEL), 0.02),
        'ffn_w_up': nrm(ks[22], (L, D_MODEL, 2 * D_FF), D_MODEL ** -0.5),
        'ffn_conv_w': nrm(ks[23], (L, FFN_CONV_WIDTH, 2 * D_FF), FFN_CONV_WIDTH ** -0.5),
        'ffn_conv_b': nrm(ks[24], (L, 2 * D_FF), 0.01),
        'ffn_w_down': nrm(ks[25], (L, D_FF, D_MODEL), beta * D_FF ** -0.5),
        'ln2_g': 1.0 + nrm(ks[26], (L, D_MODEL), 0.02),
        'ln2_b': nrm(ks[27], (L, D_MODEL), 0.02),
    }


def reference(x, mem, ln_in_g, ln_in_b, w_in, lru_conv_w, lru_conv_b, lru_wa, lru_ba, lru_wx, lru_bx,
              lru_lam, cmp_pe_k, cmp_w1_k, cmp_w2_k, cmp_pe_v, cmp_w1_v, cmp_w2_v, w_mem_kv, w_out,
              ln1_g, ln1_b, ffn_w_up, ffn_conv_w, ffn_conv_b, ffn_w_down, ln2_g, ln2_b):
    h = layer_norm(x, ln_in_g, ln_in_b)
    for l in range(DEPTH):
        h = hybrid_layer(h, mem, w_in[l], lru_conv_w[l], lru_conv_b[l], lru_wa[l], lru_ba[l],
                         lru_wx[l], lru_bx[l], lru_lam[l], cmp_pe_k[l], cmp_w1_k[l], cmp_w2_k[l],
                         cmp_pe_v[l], cmp_w1_v[l], cmp_w2_v[l], w_mem_kv[l], w_out[l],
                         ln1_g[l], ln1_b[l], ffn_w_up[l], ffn_conv_w[l], ffn_conv_b[l],
                         ffn_w_down[l], ln2_g[l], ln2_b[l])
    return h
```

```python
import numpy as np
import ml_dtypes
import concourse.bass as bass
import concourse.mybir as mybir
from concourse.bass_utils import run_bass_kernel_spmd

F32 = mybir.dt.float32
BF16 = mybir.dt.bfloat16
AF = mybir.ActivationFunctionType
ALU = mybir.AluOpType
AX = mybir.AxisListType
NPBF = ml_dtypes.bfloat16

D = 2048
KC = 16
DFF = 5632
NFF = 44
LN_EPS = 1e-5
ALPHA = 2.0 ** 0.25
GT = 512
TINY = 1e-30


class Region:
    __slots__ = ("name", "w", "r", "excl")

    def __init__(self, name, excl=False):
        self.name = name
        self.w = None
        self.r = []
        self.excl = excl


class Eng:
    def __init__(self, name, h, sem):
        self.name = name
        self.h = h
        self.sem = sem
        self.key = name
        self.cnt = 0
        self.known = {}


class Tracker:
    def __init__(self, nc, n_dma_sems=40):
        self.nc = nc
        self.sems = {}
        self.eng = {}
        for name, h in (("pe", nc.tensor), ("act", nc.scalar), ("dve", nc.vector),
                        ("pool", nc.gpsimd), ("sp", nc.sync)):
            sem = nc.alloc_semaphore("cnt_" + name)
            self.sems[name] = sem
            self.eng[name] = Eng(name, h, sem)
        self.dma_sems = []
        self.dma_sets = {}
        for sname, cnt in (("main", n_dma_sems), ("cast", 24)):
            lst = []
            for i in range(cnt):
                key = "dma_%s%d" % (sname, i)
                sem = nc.alloc_semaphore(key)
                self.sems[key] = sem
                slot = [key, sem, 0, None]
                lst.append(slot)
                self.dma_sems.append(slot)
            self.dma_sets[sname] = [lst, 0]
        self.n_wait = 0
        self.n_ins = 0

    def _need(self, e, deps):
        best = {}
        for ev in deps:
            if ev is None:
                continue
            k, v, hist = ev
            if k == e.key and e.name == "pe":
                continue
            if e.known.get(k, 0) >= v:
                continue
            if k not in best or best[k][1] < v:
                best[k] = ev
        if not best:
            return
        newk = dict(e.known)
        for k, (kk, v, hist) in best.items():
            e.h.wait_ge(self.sems[k], v)
            self.n_wait += 1
            if hist is not None:
                for k2, v2 in hist.items():
                    if newk.get(k2, 0) < v2:
                        newk[k2] = v2
            if newk.get(k, 0) < v:
                newk[k] = v
        e.known = newk

    def _collect(self, e, reads, writes):
        deps = []
        for r in reads:
            deps.append(r.w)
            if r.excl:
                deps.extend(x for x in r.r if x[0] != e.key)
        for w in writes:
            if w.w is not None:
                deps.append(w.w)
            deps.extend(x for x in w.r if x[0] != e.key)
        return deps

    def _update(self, ev, reads, writes):
        for r in reads:
            r.r = [x for x in r.r if x[0] != ev[0]]
            r.r.append(ev)
        for w in writes:
            w.w = ev
            w.r = []

    def op(self, en, fn, reads=(), writes=()):
        e = self.eng[en]
        self._need(e, self._collect(e, reads, writes))
        ins = fn(e.h)
        e.cnt += 1
        ins.then_inc(e.sem, 1)
        self.n_ins += 1
        ev = (e.key, e.cnt, e.known)
        self._update(ev, reads, writes)
        return ev

    def dma(self, qn, out, in_, reads=(), writes=(), slots="main", **kw):
        e = self.eng[qn]
        st = self.dma_sets[slots]
        slot = st[0][st[1]]
        st[1] = (st[1] + 1) % len(st[0])
        deps = self._collect(e, reads, writes)
        if slot[3] is not None:
            deps.append(slot[3])
        self._need(e, deps)
        slot[2] += 16
        ins = e.h.dma_start(out=out, in_=in_, **kw)
        ins.then_inc(slot[1], 16)
        self.n_ins += 1
        ev = (slot[0], slot[2], e.known)
        slot[3] = ev
        self._update(ev, reads, writes)
        return ev

    def wait_events(self, en, events):
        self._need(self.eng[en], events)

    def barrier(self, names=("pe", "act", "dve", "pool")):
        evs = [(self.eng[n].key, self.eng[n].cnt, self.eng[n].known) for n in names
               if self.eng[n].cnt > 0]
        for n in names:
            self._need(self.eng[n], evs)


class Cfg:
    def __init__(self, SEQ=8192, debug=False):
        self.SEQ = SEQ
        self.CH = SEQ // 4
        self.NG = SEQ // GT
        self.OWN_G = self.CH // GT
        self.G0 = self.NG - self.OWN_G
        self.NT = SEQ // 128
        self.NBLK = SEQ // 64
        self.NCMP = SEQ // 16
        self.NCT = max(1, self.NCMP // 128)
        self.NCP = min(128, self.NCMP)
        self.T_HALO = self.G0 * 4 - 1
        self.WT0 = self.T_HALO - 4
        self.NWT = self.NT - self.WT0
        self.NOT = self.NT - self.T_HALO
        self.debug = debug


def small_layout(cfg):
    lay = {}
    off = 0
    for name, n in (("g_in", 16), ("b_in", 16), ("g1", 16), ("b1", 16), ("lcw", 32), ("lcb", 8),
                    ("ba", 8), ("bx", 8), ("lam", 8), ("fcw", 264), ("fcb", 88),
                    ("w2k", 128), ("w2v", 128), ("pek", 32), ("pev", 32), ("hm", 2),
                    ("gflag", cfg.NG), ("fbias", cfg.NBLK)):
        lay[name] = (off, n)
        off += n
    return lay, off


W_IN_SPLIT = dict(lx=0, ly=1024, q=2048, kc=2560, vc=2688, ks=2816, vs=2944, kw=3072, vw=3200,
                  gate=3328, mq=3352)


def win_chunk_cols():
    cols = []
    for c in range(8):
        cols.append(np.arange(c * 128, (c + 1) * 128))
    for c in range(8):
        cols.append(1024 + np.arange(c * 128, (c + 1) * 128))
    for c in range(4):
        cols.append(np.concatenate([2048 + c * 64 + np.arange(64), 2048 + (4 + c) * 64 + np.arange(64)]))
    for k in range(6):
        cols.append(2560 + k * 128 + np.arange(128))
    for h in range(4):
        cols.append(3352 + h * 128 + np.arange(128))
    return cols


def prep_shared(cfg, inp):
    f = np.float32
    sh = {}
    w_in = np.asarray(inp["w_in"][0], f)
    cols = win_chunk_cols()
    wt = np.stack([w_in[:, c] for c in cols])
    sh["w_in_t"] = np.ascontiguousarray(wt.reshape(30, KC, 128, 128).transpose(0, 2, 1, 3))
    sh["w_gate"] = np.ascontiguousarray(w_in[:, 3328:3352].reshape(KC, 128, 24).transpose(1, 0, 2))
    w_up = np.asarray(inp["ffn_w_up"][0], f)
    sh["w_up_t"] = np.ascontiguousarray(w_up.reshape(KC, 128, 88, 128).transpose(2, 1, 0, 3))
    w_out = np.asarray(inp["w_out"][0], f)
    sh["w_out_t"] = np.ascontiguousarray(w_out.reshape(4, 4, 128, 4, 512).transpose(3, 0, 2, 1, 4))
    w_dn = np.asarray(inp["ffn_w_down"][0], f)
    sh["w_dn_t"] = np.ascontiguousarray(w_dn.reshape(11, 4, 128, 4, 512).transpose(3, 0, 2, 1, 4))
    wm = np.asarray(inp["w_mem_kv"][0], f)
    sh["w_mk_t"] = np.ascontiguousarray(wm[:, 0:512].reshape(KC, 128, 4, 128).transpose(2, 1, 0, 3))
    sh["w_mv_t"] = np.ascontiguousarray(wm[:, 512:1024].reshape(4, 4, 128, 512).transpose(0, 2, 1, 3))
    wa = np.asarray(inp["lru_wa"][0], f)
    wx = np.asarray(inp["lru_wx"][0], f)
    sh["w_ax"] = np.ascontiguousarray(np.stack([wa.transpose(1, 0, 2), wx.transpose(1, 0, 2)], axis=1))
    for nm, key in (("w1k", "cmp_w1_k"), ("w1v", "cmp_w1_v")):
        w1 = np.asarray(inp[key][0], f)
        t = w1.transpose(1, 0, 2)
        bd = np.zeros((128, 32, 128), f)
        bd[0:64, :, 0:64] = t
        bd[64:128, :, 64:128] = t
        sh[nm] = np.ascontiguousarray(bd.reshape(128, 2, 16, 128).transpose(1, 0, 2, 3))
    lay, ns = small_layout(cfg)
    sp = np.zeros((128, ns), f)

    def put(name, arr):
        o, n = lay[name]
        assert arr.shape == (128, n), (name, arr.shape, n)
        sp[:, o:o + n] = arr

    col = lambda v: np.asarray(v, f).reshape(-1, 128).T
    put("g_in", col(inp["ln_in_g"]))
    put("b_in", col(inp["ln_in_b"]))
    put("g1", col(inp["ln1_g"][0]))
    put("b1", col(inp["ln1_b"][0]))
    cw = np.asarray(inp["lru_conv_w"][0], f)
    put("lcw", cw.reshape(4, 8, 128).transpose(2, 1, 0).reshape(128, 32))
    put("lcb", col(inp["lru_conv_b"][0]))
    put("ba", np.asarray(inp["lru_ba"][0], f).T)
    put("bx", np.asarray(inp["lru_bx"][0], f).T)
    put("lam", col(inp["lru_lam"][0]))
    fw = np.asarray(inp["ffn_conv_w"][0], f)
    put("fcw", fw.reshape(3, 88, 128).transpose(2, 1, 0).reshape(128, 264))
    put("fcb", col(inp["ffn_conv_b"][0]))
    w2k = np.asarray(inp["cmp_w2_k"][0], f)
    w2v = np.asarray(inp["cmp_w2_v"][0], f)
    def bdiag(w):
        o_ = np.zeros((128, 128), f)
        o_[0:64, 0:64] = w
        o_[64:128, 64:128] = w
        return o_
    put("w2k", bdiag(w2k))
    put("w2v", bdiag(w2v))
    hm = np.zeros((128, 2), f)
    hm[0:64, 0] = 1.0
    hm[64:128, 1] = 1.0
    put("hm", hm)
    pk = np.asarray(inp["cmp_pe_k"][0], f).T
    pv = np.asarray(inp["cmp_pe_v"][0], f).T
    put("pek", np.concatenate([pk, pk], axis=0))
    put("pev", np.concatenate([pv, pv], axis=0))
    sh["_small"] = sp
    gb = np.stack([np.stack([np.asarray(inp["ln_in_g"], f), np.asarray(inp["ln_in_b"], f)]),
                   np.stack([np.asarray(inp["ln1_g"][0], f), np.asarray(inp["ln1_b"][0], f)]),
                   np.stack([np.asarray(inp["ln2_g"][0], f), np.asarray(inp["ln2_b"][0], f)])])
    sh["gb"] = np.ascontiguousarray(gb.reshape(3, 1, 2 * D))
    ii = np.arange(128)
    sh["ident_f"] = np.eye(128, dtype=f)
    tri = (ii[:, None] <= ii[None, :]).astype(f)
    sh["masks_b"] = np.ascontiguousarray(
        np.stack([np.eye(128, dtype=f), tri, 1.0 - tri, np.ones((128, 128), f)], axis=1)).astype(NPBF)
    negc = np.stack([np.tile(-30000.0 * (1.0 - tri), (1, 4)), np.tile(-30000.0 * tri, (1, 4))], axis=1)
    sh["negc"] = np.ascontiguousarray(negc).astype(NPBF)
    eec = np.zeros((128, 32, 128), f)
    for r in range(2):
        for j in range(32):
            for hf in range(2):
                eec[64 * r + 2 * j + hf, j, hf * 64:(hf + 1) * 64] = 1.0
    sh["eec"] = eec.astype(NPBF)
    n = np.arange(cfg.NCT * 128)
    blk = np.arange(cfg.NBLK)
    cover = ((16 * n[:, None] <= 64 * blk[None, :] + 63) & (16 * n[:, None] + 31 >= 64 * blk[None, :])).astype(f)
    cover[n >= cfg.NCMP - 1] = 0.0
    c1 = np.concatenate([cover, np.ones((cfg.NCT * 128, 1), f)], axis=1)
    sh["cover1"] = np.ascontiguousarray(c1.reshape(cfg.NCT, 128, cfg.NBLK + 1).transpose(1, 0, 2)).astype(NPBF)
    return sh


def prep_core(cfg, inp, sh, c):
    f = np.float32
    b, j = c // 4, c % 4
    off = (3 - j) * cfg.CH
    m = dict(sh)
    xv = np.zeros((cfg.SEQ, D), f)
    xv[off:] = np.asarray(inp["x"][b, 0:(j + 1) * cfg.CH], f)
    m["xv"] = xv
    m["memx"] = np.ascontiguousarray(np.asarray(inp["mem"][b], f))
    lay, ns = small_layout(cfg)
    sp = sh["_small"].copy()
    o, n = lay["gflag"]
    sp[:, o:o + n] = (np.arange(cfg.NG) * GT >= off).astype(f)[None, :]
    o, n = lay["fbias"]
    fb = np.zeros(cfg.NBLK, f)
    fb[off // 64] = 1e9
    sp[:, o:o + n] = fb[None, :]
    m["smallp"] = sp
    del m["_small"]
    cm = np.zeros((cfg.NOT, 128, cfg.NCT, 128), f)
    for ti in range(cfg.NOT):
        vq = cfg.T_HALO + ti
        t = 128 * vq + np.arange(128)
        for nt in range(cfg.NCT):
            ng = nt * 128 + np.arange(128)
            ok = (16 * ng[:, None] + 31 <= t[None, :]) & (16 * ng[:, None] >= off) & (ng[:, None] < cfg.NCMP - 1)
            cm[ti, :, nt, :] = ok
    m["cmask"] = cm.astype(NPBF)
    return m


P_WIN, P_WOUT, P_WUP, P_WDN, P_W1K, P_W1V, NPIECE = 0, 30, 46, 134, 178, 180, 182
NB_POOL = 5
LOOKAHEAD = 4


class DryTracker(Tracker):
    def __init__(self):
        self.n_wait = 0
        self.n_ins = 0

    def op(self, en, fn, reads=(), writes=()):
        return None

    def dma(self, qn, out, in_, reads=(), writes=(), slots="main", **kw):
        return None

    def wait_events(self, en, events):
        pass

    def barrier(self, names=()):
        pass


class Prog:
    def __init__(self, cfg):
        self.cfg = cfg
        self.nc = bass.Bass("TRN2", target_bir_lowering=False)
        self.alloc()

    def sb(self, name, shape, dt):
        return self.nc.alloc_sbuf_tensor(name, list(shape), dt)

    def din(self, name, shape, dt):
        return self.nc.dram_tensor(name, list(shape), dt, kind="ExternalInput").ap()

    def alloc(self):
        cfg, nc = self.cfg, self.nc
        lay, ns = small_layout(cfg)
        self.lay = lay
        self.xv = self.din("xv", [cfg.SEQ, D], F32)
        self.memx = self.din("memx", [256, D], F32)
        self.w_in_t = self.din("w_in_t", [30, 128, KC, 128], F32)
        self.w_gate = self.din("w_gate", [128, KC, 24], F32)
        self.w_up_t = self.din("w_up_t", [88, 128, KC, 128], F32)
        self.w_out_t = self.din("w_out_t", [4, 4, 128, 4, 512], F32)
        self.w_dn_t = self.din("w_dn_t", [4, 11, 128, 4, 512], F32)
        self.w_mk_t = self.din("w_mk_t", [4, 128, KC, 128], F32)
        self.w_mv_t = self.din("w_mv_t", [4, 128, 4, 512], F32)
        self.w_ax = self.din("w_ax", [128, 2, 8, 128], F32)
        self.w1k = self.din("w1k", [2, 128, 16, 128], F32)
        self.w1v = self.din("w1v", [2, 128, 16, 128], F32)
        self.smallp = self.din("smallp", [128, ns], F32)
        self.gb = self.din("gb", [3, 1, 2 * D], F32)
        self.ident_f_d = self.din("ident_f", [128, 128], F32)
        self.masks_b_d = self.din("masks_b", [128, 4, 128], BF16)
        self.eec_d = self.din("eec", [128, 32, 128], BF16)
        self.negc_d = self.din("negc", [128, 2, 512], BF16)
        self.cover1_d = self.din("cover1", [128, cfg.NCT, cfg.NBLK + 1], BF16)
        self.cmask_d = self.din("cmask", [cfg.NOT, 128, cfg.NCT, 128], BF16)
        self.y = nc.dram_tensor("y", [cfg.CH, D], F32, kind="ExternalOutput").ap()
        self.wsc = nc.dram_tensor("wsc", [NPIECE, 128, 2048], BF16, kind="Internal").ap()
        if cfg.debug:
            self.d_mixed = nc.dram_tensor("d_mixed", [cfg.OWN_G + 1, 128, KC, 512], BF16, kind="ExternalOutput").ap()
            self.d_h1 = nc.dram_tensor("d_h1", [cfg.OWN_G + 1, 128, 4, D], F32, kind="ExternalOutput").ap()
            self.d_deriv = nc.dram_tensor("d_deriv", [128, 64], F32, kind="ExternalOutput").ap()
            self.d_hlru = nc.dram_tensor("d_hlru", [128, 8, 512], BF16, kind="ExternalOutput").ap()
            self.d_hT = nc.dram_tensor("d_hT", [128, KC, 512], BF16, kind="ExternalOutput").ap()
        self.SM = self.sb("SM", [128, ns], F32)
        self.ksT = self.sb("ksT", [128, cfg.SEQ], BF16)
        self.kwT = self.sb("kwT", [128, cfg.NWT * 128], BF16)
        self.V1s = self.sb("V1s", [128, cfg.NT, 2, 65], BF16)
        self.V1w = self.sb("V1w", [128, cfg.NWT, 2, 65], BF16)
        self.eec = self.sb("eec_s", [128, 32, 128], BF16)
        self.kcmpT = self.sb("kcmpT", [128, cfg.NCT * 128], BF16)
        self.vcmp1 = self.sb("vcmp1", [128, cfg.NCT, 2, 65], BF16)
        self.cover1 = self.sb("cover1_s", [128, cfg.NCT, cfg.NBLK + 1], BF16)
        self.kmemT = self.sb("kmemT", [128, 4, 256], BF16)
        self.vmem = self.sb("vmem", [128, 2, 4, 128], BF16)
        self.ones_bf = self.sb("ones_bf", [128, 128], BF16)
        self.masks = self.sb("masks_s", [128, 4, 128], BF16)
        self.ident_f = self.sb("ident_fs", [128, 128], F32)
        self.wg_bf = self.sb("wg_bf", [128, KC, 24], BF16)
        self.wax_sb = self.sb("wax_sb", [128, 2, 8, 128], BF16)
        self.smb = self.sb("smb", [128, 320], BF16)
        self.negc = self.sb("negc_s", [128, 2, 512], BF16)
        self.negm2 = self.sb("negm2", [128, 2, 512], BF16)
        self.vcmpT = self.sb("vcmpT", [128, cfg.NCT * 128], BF16)
        self.deriv = self.sb("deriv", [128, 64], F32)
        self.state = self.sb("lru_state", [128, 8], F32)
        self.xtail = self.sb("xtail", [128, 8, 3], F32)
        self.ftail = self.sb("ftail", [128, 88, 2], F32)
        self.kc_loc = self.sb("kc_loc", [128, 528], BF16)
        self.vc_loc = self.sb("vc_loc", [128, 528], BF16)
        self.hflag = self.sb("hflag", [128, cfg.NG], F32)
        self.xt = self.sb("xt", [128, 4, D], F32)
        self.bufM = self.sb("bufM", [128, KC, 512], BF16)
        self.hT = self.sb("hT", [128, KC, 512], BF16)
        self.bufQ = self.sb("bufQ", [128, 8, 512], BF16)
        self.bufA = self.sb("bufA", [128, 16, 512], BF16)
        self.wpool = self.sb("wpool", [128, NB_POOL, 2048], BF16)
        self.g_sb = self.sb("g_sb", [128, 4, 24], F32)
        self.stats = self.sb("stats", [128, 4, 4, 6], F32)
        self.mv = self.sb("mv", [128, 4, 2], F32)
        self.lnt = self.sb("lnt", [128, 3, 4], F32)
        self.bufS = self.sb("bufS", [128, 5248], F32)
        self.ps = [nc.alloc_psum_tensor("ps%d" % i, [128, 512], F32) for i in range(8)]
        print("sbuf bytes remaining/partition:", nc.sbuf_bytes_remaining)

    def mk_regions(self):
        cfg = self.cfg
        R = lambda n, excl=False: Region(n, excl)
        self.R_sm = R("sm")
        self.R_const = R("const")
        self.R_ks = [R("ks%d" % i) for i in range(cfg.NT)]
        self.R_kw = [R("kw%d" % i) for i in range(cfg.NWT)]
        self.R_vs = [R("vs%d" % i) for i in range(cfg.NT)]
        self.R_vw = [R("vw%d" % i) for i in range(cfg.NWT)]
        self.R_kcmp = R("kcmp")
        self.R_vcmp = R("vcmp")
        self.R_kmem = R("kmem")
        self.R_vmem = R("vmem")
        self.R_wg = R("wg")
        self.R_negm2 = R("negm2")
        self.R_wax = R("wax")
        self.R_smb = R("smb")
        self.R_deriv = R("deriv")
        self.R_state = [R("state%d" % i) for i in range(8)]
        self.R_xtail = [R("xtail%d" % i) for i in range(8)]
        self.R_ftail = [R("ftail%d" % i) for i in range(88)]
        self.R_kcl = R("kc_loc")
        self.R_vcl = R("vc_loc")
        self.R_xt = [R("xt%d" % i) for i in range(4)]
        self.R_M = [R("M%d" % i) for i in range(KC)]
        self.R_hT = [R("hT%d" % i) for i in range(KC)]
        self.R_Q = [R("Q%d" % i) for i in range(8)]
        self.R_A = [R("A%d" % i) for i in range(16)]
        self.R_wp = [R("wp%d" % i) for i in range(NB_POOL)]
        self.R_wsc = [R("wsc%d" % i) for i in range(NPIECE)]
        self.R_g = R("g_sb")
        self.R_stats = [R("stats%d" % i) for i in range(4)]
        self.R_mv = R("mv")
        self.R_lnt = R("lnt")
        self.R_S = {}
        self.R_ps = [R("ps%d" % i, True) for i in range(8)]
        self.R_y = R("y")
        self.R_dbg = R("dbg")

    def RS(self, name):
        if name not in self.R_S:
            self.R_S[name] = Region("S_" + name)
        return self.R_S[name]

    def sm(self, name, i=None, n=1):
        o, cnt = self.lay[name]
        if i is None:
            return self.SM[:, o:o + cnt]
        return self.SM[:, o + i:o + i + n]

    def piece_src(self, pid):
        if pid < P_WOUT:
            return self.w_in_t[pid].rearrange("p a b -> p (a b)")
        if pid < P_WUP:
            k = pid - P_WOUT
            return self.w_out_t[k // 4, k % 4].rearrange("p a b -> p (a b)")
        if pid < P_WDN:
            return self.w_up_t[pid - P_WUP].rearrange("p a b -> p (a b)")
        if pid < P_W1K:
            k = pid - P_WDN
            return self.w_dn_t[k // 11, k % 11].rearrange("p a b -> p (a b)")
        if pid < P_W1V:
            return self.w1k[pid - P_W1K].rearrange("p a b -> p (a b)")
        return self.w1v[pid - P_W1V].rearrange("p a b -> p (a b)")

    def stage_cast(self, src_ap, dst_ap, dst_regions, nelem=2048, eng="pool"):
        self.tr.dma("pool", dst_ap, src_ap, writes=dst_regions, slots="cast")

    def issue_casts(self, n):
        if self.dry:
            return
        while n > 0 and self.cast_ptr < len(self.cast_order):
            pid = self.cast_order[self.cast_ptr]
            self.cast_ptr += 1
            self.tr.dma("pool", self.wsc[pid], self.piece_src(pid), writes=[self.R_wsc[pid]], slots="cast")
            self.cast_done.add(pid)
            n -= 1

    def _issue_fetch(self, pid):
        tr = self.tr
        b = self.pool_rr
        self.pool_rr = (self.pool_rr + 1) % NB_POOL
        buf = self.wpool[:, b, :]
        while pid not in self.cast_done:
            self.issue_casts(1)
        tr.dma("sp", buf, self.wsc[pid], reads=[self.R_wsc[pid]], writes=[self.R_wp[b]])
        return b

    def fetch(self, pid):
        if self.dry:
            self.order.append(pid)
            return self.wpool[:, 0, :], self.R_wp[0]
        assert self.order[self.optr] == pid, (self.optr, self.order[self.optr], pid)
        while self.issued < min(len(self.order), self.optr + LOOKAHEAD + 1):
            self.inflight[self.issued] = self._issue_fetch(self.order[self.issued])
            self.issued += 1
        b = self.inflight.pop(self.optr)
        self.optr += 1
        return self.wpool[:, b, :], self.R_wp[b]

    def bank_mm(self):
        i = self.mm_rr % self.mm_nb
        self.mm_rr = (i + 1) % self.mm_nb
        return self.ps[i], self.R_ps[i]

    def bfv(self, bank):
        return bank[:, :].bitcast(BF16)

    def emit(self, dry):
        self.dry = dry
        self.mk_regions()
        if dry:
            self.tr = DryTracker()
            self.order = []
        else:
            self.tr = Tracker(self.nc)
            self.optr = 0
            self.issued = 0
            self.inflight = {}
        self.cast_done = set()
        self.cast_ptr = 0
        if not dry:
            seen = set()
            self.cast_order = [p for p in self.order if not (p in seen or seen.add(p))]
        self.pool_rr = 0
        self.stage_rr = 0
        self.mm_rr = 0
        self.mm_nb = 4
        self.x_loaded = set()
        self.a1_done = set()
        cfg = self.cfg
        self.setup()
        for G in range(cfg.NG):
            self.group(G)
        if not dry:
            tr = self.tr
            tr.wait_events("sp", [s[3] for s in tr.dma_sems])
            tr.wait_events("pool", [s[3] for s in tr.dma_sems])
            print("instructions:", tr.n_ins, "waits:", tr.n_wait)

    def setup(self):
        cfg, tr = self.cfg, self.tr
        SM = self.SM
        tr.dma("sp", SM[:, :], self.smallp[:, :], writes=[self.R_sm])
        tr.dma("sp", self.ident_f[:, :], self.ident_f_d[:, :], writes=[self.R_const])
        tr.dma("sp", self.masks[:, :, :], self.masks_b_d[:, :, :], writes=[self.R_const])
        tr.dma("sp", self.eec[:, :, :], self.eec_d[:, :, :], writes=[self.R_const])
        tr.dma("sp", self.negc[:, :, :], self.negc_d[:, :, :], writes=[self.R_const])
        tr.dma("sp", self.cover1[:, :, :], self.cover1_d[:, :, :], writes=[self.R_const])
        dv = self.deriv
        RD = self.R_deriv
        x_ = dv[:, 32:40]
        xs = dv[:, 40:48]
        t_ = dv[:, 48:56]
        yl = dv[:, 56:64]
        tr.op("act", lambda e: e.activation(out=x_, in_=self.sm("lam"), func=AF.Exp, scale=-1.0), reads=[self.R_sm], writes=[RD])
        tr.op("act", lambda e: e.activation(out=yl, in_=x_, func=AF.Ln, bias=1.0), reads=[RD], writes=[RD])
        tr.op("dve", lambda e: e.tensor_scalar(out=xs, in0=x_, scalar1=0.05, scalar2=None, op0=ALU.min), reads=[RD], writes=[RD])
        tr.op("dve", lambda e: e.tensor_scalar(out=t_, in0=xs, scalar1=-0.25, scalar2=1.0 / 3.0, op0=ALU.mult, op1=ALU.add), reads=[RD], writes=[RD])
        tr.op("dve", lambda e: e.tensor_tensor(out=t_, in0=t_, in1=xs, op=ALU.mult), reads=[RD], writes=[RD])
        tr.op("dve", lambda e: e.tensor_scalar(out=t_, in0=t_, scalar1=-0.5, scalar2=None, op0=ALU.add), reads=[RD], writes=[RD])
        tr.op("dve", lambda e: e.tensor_tensor(out=t_, in0=t_, in1=xs, op=ALU.mult), reads=[RD], writes=[RD])
        tr.op("dve", lambda e: e.tensor_scalar(out=t_, in0=t_, scalar1=1.0, scalar2=None, op0=ALU.add), reads=[RD], writes=[RD])
        tr.op("dve", lambda e: e.tensor_tensor(out=t_, in0=t_, in1=xs, op=ALU.mult), reads=[RD], writes=[RD])
        tr.op("dve", lambda e: e.tensor_tensor(out=t_, in0=t_, in1=yl, op=ALU.subtract), reads=[RD], writes=[RD])
        tr.op("dve", lambda e: e.tensor_scalar(out=xs, in0=x_, scalar1=0.05, scalar2=None, op0=ALU.is_lt), reads=[RD], writes=[RD])
        tr.op("dve", lambda e: e.tensor_tensor(out=t_, in0=t_, in1=xs, op=ALU.mult), reads=[RD], writes=[RD])
        tr.op("dve", lambda e: e.tensor_tensor(out=t_, in0=t_, in1=yl, op=ALU.add), reads=[RD], writes=[RD])
        tr.op("dve", lambda e: e.tensor_scalar(out=dv[:, 0:8], in0=t_, scalar1=-4.0, scalar2=None, op0=ALU.mult), reads=[RD], writes=[RD])
        tr.op("dve", lambda e: e.tensor_scalar(out=dv[:, 8:16], in0=self.sm("ba"), scalar1=0.5, scalar2=None, op0=ALU.mult), reads=[self.R_sm, RD], writes=[RD])
        tr.op("dve", lambda e: e.tensor_scalar(out=dv[:, 16:24], in0=self.sm("bx"), scalar1=0.5, scalar2=None, op0=ALU.mult), reads=[self.R_sm, RD], writes=[RD])
        tr.op("dve", lambda e: e.tensor_scalar(out=self.hflag[:, :], in0=self.sm("gflag"), scalar1=0.5, scalar2=None, op0=ALU.mult), reads=[self.R_sm, RD], writes=[RD])
        tr.op("dve", lambda e: e.memset(dv[:, 26:27], 1.0), reads=[RD], writes=[RD])
        tr.op("dve", lambda e: e.memset(dv[:, 27:28], LN_EPS), reads=[RD], writes=[RD])
        self.one_ap = dv[:, 26:27]
        self.eps_ap = dv[:, 27:28]
        self.phaseA1(0)
        o = self.lay["w2k"][0]
        tr.op("pool", lambda e: e.tensor_copy(out=self.smb[:, :], in_=SM[:, o:o + 320]), reads=[self.R_sm], writes=[self.R_smb])
        tr.op("pool", lambda e: e.memset(self.ones_bf[:, :], 1.0), writes=[self.R_const])
        for c in range(8):
            tr.op("pool", lambda e: e.memset(self.state[:, c:c + 1], 0.0), writes=[self.R_state[c]])
            tr.op("pool", lambda e: e.memset(self.xtail[:, c, :], 0.0), writes=[self.R_xtail[c]])
        for c in range(88):
            tr.op("pool", lambda e: e.memset(self.ftail[:, c, :], 0.0), writes=[self.R_ftail[c]])
        tr.op("pool", lambda e: e.memset(self.kc_loc[:, :], 0.0), writes=[self.R_kcl])
        tr.op("pool", lambda e: e.memset(self.vc_loc[:, :], 0.0), writes=[self.R_vcl])
        tr.op("pool", lambda e: e.memset(self.vcmp1[:, :, :, :], 0.0), writes=[self.R_vcmp])
        tr.op("pool", lambda e: e.memset(self.vcmp1[:, :, :, 64:65], 1.0), writes=[self.R_vcmp])
        tr.op("pool", lambda e: e.memset(self.negm2[:, :, :], 0.0), writes=[self.R_negm2])
        tr.op("pool", lambda e: e.memset(self.kcmpT[:, :], 0.0), writes=[self.R_kcmp])
        tr.op("pool", lambda e: e.memset(self.vcmpT[:, :], 0.0), writes=[self.R_vcmp])
        self.issue_casts(20)
        self.stage_cast(self.w_gate.rearrange("p a b -> p (a b)"), self.wg_bf[:, :, :].rearrange("p a b -> p (a b)"),
                        [self.R_wg], nelem=KC * 24)
        self.stage_cast(self.w_ax.rearrange("p a b c -> p (a b c)"), self.wax_sb[:, :, :, :].rearrange("p a b c -> p (a b c)"),
                        [self.R_wax])
        self.mem_kv()
        for kind, pid, pcol, dcol in (("k", P_W1K, 256, 24), ("v", P_W1V, 288, 25)):
            bank, rbk = self.bank_mm()
            pe2 = self.bufS[:, 64 * (dcol - 24):64 * (dcol - 24) + 64].bitcast(BF16)
            R_pe2 = self.RS("pe2%d" % dcol)
            tr.op("pool", lambda e: e.tensor_copy(out=pe2[:, 0:64].rearrange("p (l t) -> p l t", t=2),
                                                  in_=self.smb[:, pcol:pcol + 32].unsqueeze(2).broadcast_to([128, 32, 2])),
                  reads=[self.R_smb], writes=[R_pe2])
            for half in range(2):
                buf, rb = self.fetch(pid + half)
                w1 = buf.rearrange("p (l e) -> p l e", e=128)
                for li in range(16):
                    l = half * 16 + li
                    tr.op("pe", lambda e: e.matmul(bank[:, 0:2], lhsT=w1[:, li, :], rhs=pe2[:, 2 * l:2 * l + 2],
                                                   start=(l == 0), stop=(l == 31)),
                          reads=[rb, R_pe2], writes=[rbk])
            tr.op("dve", lambda e: e.tensor_copy(out=dv[:, dcol:dcol + 1], in_=bank[:, 0:1]), reads=[rbk, RD], writes=[RD])
        tr.barrier()

    def mem_kv(self):
        cfg, tr = self.cfg, self.tr
        memT = self.hT
        mbv = self.bufQ[:, :, :].rearrange("p a b -> p (a b)").rearrange("p (t d) -> p t d", d=D)
        for mt in range(2):
            mf = self.bufS[:, 512 + mt * D:512 + (mt + 1) * D]
            R_mf = self.RS("memf%d" % mt)
            tr.dma("sp", mf, self.memx[mt * 128:(mt + 1) * 128, :], writes=[R_mf])
            tr.op("pool", lambda e: e.tensor_copy(out=mbv[:, mt, :], in_=mf), reads=[R_mf], writes=self.R_Q[4 * mt:4 * mt + 4])
        for kc in range(KC):
            bank, rbk = self.bank_mm()
            bb = self.bfv(bank)
            for mt in range(2):
                tr.op("pe", lambda e: e.transpose(out=bb[:, mt * 128:(mt + 1) * 128], in_=mbv[:, mt, kc * 128:(kc + 1) * 128],
                                                  identity=self.masks[:, 0, :]),
                      reads=self.R_Q[4 * mt:4 * mt + 4] + [self.R_const], writes=[rbk])
            tr.op("dve", lambda e: e.tensor_copy(out=memT[:, kc, 0:256], in_=bb[:, 0:256]), reads=[rbk], writes=[self.R_hT[kc]])
        for h in range(4):
            wb = self.bufA[:, 0:4, :].rearrange("p a b -> p (a b)")
            self.stage_cast(self.w_mk_t[h].rearrange("p a b -> p (a b)"), wb, self.R_A[0:4])
            wv = wb.rearrange("p (a b) -> p a b", b=128)
            bank, rbk = self.bank_mm()
            for kc in range(KC):
                tr.op("pe", lambda e: e.matmul(bank[:, 0:256], lhsT=wv[:, kc, :], rhs=memT[:, kc, 0:256], start=(kc == 0), stop=(kc == KC - 1)),
                      reads=self.R_A[0:4] + [self.R_hT[kc]], writes=[rbk])
            tr.op("act", lambda e: e.copy(out=self.kmemT[:, h, :], in_=bank[:, 0:256]), reads=[rbk], writes=[self.R_kmem])
        banks = [self.bank_mm() for _ in range(2)]
        for kq in range(4):
            wb = self.bufA[:, 4:8, :].rearrange("p a b -> p (a b)")
            self.stage_cast(self.w_mv_t[kq].rearrange("p a b -> p (a b)"), wb, self.R_A[4:8])
            wv = wb.rearrange("p (a b) -> p a b", b=512)
            for mt in range(2):
                bank, rbk = banks[mt]
                for kci in range(4):
                    kc = kq * 4 + kci
                    tr.op("pe", lambda e: e.matmul(bank[:, 0:512], lhsT=memT[:, kc, mt * 128:(mt + 1) * 128], rhs=wv[:, kci, :],
                                                   start=(kc == 0), stop=(kc == KC - 1)),
                          reads=self.R_A[4:8] + [self.R_hT[kc]], writes=[rbk])
        for mt in range(2):
            bank, rbk = banks[mt]
            tr.op("act", lambda e: e.copy(out=self.vmem[:, mt, :, :].rearrange("p h d -> p (h d)"), in_=bank[:, 0:512]),
                  reads=[rbk], writes=[self.R_vmem])

    def ln_stats(self, tiles):
        tr = self.tr
        for t in tiles:
            for k in range(4):
                tr.op("dve", lambda e: e.bn_stats(out=self.stats[:, t, k, :], in_=self.xt[:, t, k * 512:(k + 1) * 512]),
                      reads=[self.R_xt[t]], writes=[self.R_stats[t]])
            tr.op("dve", lambda e: e.bn_aggr(out=self.mv[:, t, :], in_=self.stats[:, t, :, :].rearrange("p a b -> p (a b)")),
                  reads=[self.R_stats[t]], writes=[self.R_mv])
        t0, t1 = tiles[0], tiles[-1] + 1
        tr.op("act", lambda e: e.activation(out=self.lnt[:, 0, t0:t1], in_=self.mv[:, t0:t1, 1], func=AF.Sqrt, bias=self.eps_ap, scale=1.0),
              reads=[self.R_mv, self.R_lnt, self.R_deriv], writes=[self.R_lnt])
        tr.op("dve", lambda e: e.reciprocal(out=self.lnt[:, 1, t0:t1], in_=self.lnt[:, 0, t0:t1]), reads=[self.R_lnt], writes=[self.R_lnt])
        tr.op("dve", lambda e: e.scalar_tensor_tensor(out=self.lnt[:, 2, t0:t1], in0=self.mv[:, t0:t1, 0], scalar=-1.0,
                                                       in1=self.lnt[:, 1, t0:t1], op0=ALU.mult, op1=ALU.mult),
              reads=[self.R_mv, self.R_lnt], writes=[self.R_lnt])

    def load_gb(self, which):
        gbuf = self.bufA[:, :, :].rearrange("p a b -> p (a b)").bitcast(F32)
        self.tr.dma("sp", gbuf, self.gb[which, 0:1, :].partition_broadcast(128), writes=self.R_A[0:16])
        return gbuf

    def ln_apply(self, t, gbuf, g_name, b_name, want_tok, want_T=True):
        tr = self.tr
        xnv = self.bufM[:, :, :].rearrange("p a b -> p (a b)").rearrange("p (t d) -> p t d", d=D)
        if want_T:
            tr.op("act", lambda e: e.activation(out=xnv[:, t, :], in_=self.xt[:, t, :], func=AF.Identity,
                                                scale=self.lnt[:, 1, t:t + 1], bias=self.lnt[:, 2, t:t + 1]),
                  reads=[self.R_xt[t], self.R_lnt], writes=self.R_M[4 * t:4 * t + 4])
        if want_tok:
            tr.op("dve", lambda e: e.scalar_tensor_tensor(out=self.xt[:, t, :], in0=self.xt[:, t, :], scalar=self.mv[:, t, 0:1], in1=gbuf[:, 0:D],
                                                           op0=ALU.subtract, op1=ALU.mult),
                  reads=[self.R_xt[t], self.R_mv] + self.R_A[0:8], writes=[self.R_xt[t]])
            tr.op("dve", lambda e: e.scalar_tensor_tensor(out=self.xt[:, t, :], in0=self.xt[:, t, :], scalar=self.lnt[:, 1, t:t + 1], in1=gbuf[:, D:2 * D],
                                                           op0=ALU.mult, op1=ALU.add),
                  reads=[self.R_xt[t], self.R_lnt] + self.R_A[8:16], writes=[self.R_xt[t]])

    def ln_transpose(self, tiles, g_name, b_name):
        tr = self.tr
        xnv = self.bufM[:, :, :].rearrange("p a b -> p (a b)").rearrange("p (t d) -> p t d", d=D)
        c0, c1 = tiles[0] * 128, (tiles[-1] + 1) * 128
        for kc in range(KC):
            bank, rbk = self.bank_mm()
            bb = self.bfv(bank)
            for t in tiles:
                tr.op("pe", lambda e: e.transpose(out=bb[:, t * 128:(t + 1) * 128], in_=xnv[:, t, kc * 128:(kc + 1) * 128],
                                                  identity=self.masks[:, 0, :]),
                      reads=self.R_M[4 * t:4 * t + 4] + [self.R_const], writes=[rbk])
            if kc % 2 == 0:
                tr.op("act", lambda e: e.activation(out=self.hT[:, kc, c0:c1], in_=bb[:, c0:c1], func=AF.Identity,
                                                    scale=self.sm(g_name, kc), bias=self.sm(b_name, kc)),
                      reads=[rbk, self.R_sm], writes=[self.R_hT[kc]])
            else:
                tr.op("dve", lambda e: e.tensor_scalar(out=self.hT[:, kc, c0:c1], in0=bb[:, c0:c1], scalar1=self.sm(g_name, kc),
                                                        scalar2=self.sm(b_name, kc), op0=ALU.mult, op1=ALU.add),
                      reads=[rbk, self.R_sm], writes=[self.R_hT[kc]])

    def proj(self, pid, c0, c1):
        tr = self.tr
        buf, rb = self.fetch(pid)
        wv = buf.rearrange("p (a b) -> p a b", b=128)
        bank, rbk = self.bank_mm()
        n = c1 - c0
        for kc in range(KC):
            tr.op("pe", lambda e: e.matmul(bank[:, 0:n], lhsT=wv[:, kc, :], rhs=self.hT[:, kc, c0:c1], start=(kc == 0), stop=(kc == KC - 1)),
                  reads=[rb, self.R_hT[kc]], writes=[rbk])
        return bank, rbk

    def phaseA1(self, G):
        tr = self.tr
        for t in range(4):
            if (G, t) not in self.x_loaded:
                tok0 = G * GT + t * 128
                tr.dma("sp", self.xt[:, t, :], self.xv[tok0:tok0 + 128, :], writes=[self.R_xt[t]])
                self.x_loaded.add((G, t))
        self.ln_stats([0, 1, 2, 3])
        for t in range(4):
            self.ln_apply(t, None, "g_in", "b_in", want_tok=False)
        self.a1_done.add(G)

    def group(self, G):
        cfg, tr = self.cfg, self.tr
        own = G >= cfg.G0
        halo = G == cfg.G0 - 1
        TR = (0, 4) if own else ((3, 4) if halo else None)
        if not self.dry:
            self.issue_casts((len(self.cast_order) + cfg.G0 - 2) // max(1, cfg.G0 - 1))
        self.mm_nb = 8
        if G not in self.a1_done:
            self.phaseA1(G)
        self.ln_transpose([0, 1, 2, 3], "g_in", "b_in")
        if G + 1 < cfg.NG:
            free_t = [0, 1, 2, 3] if TR is None else ([0, 1, 2] if TR == (3, 4) else [])
            for t in free_t:
                tok0 = (G + 1) * GT + t * 128
                tr.dma("sp", self.xt[:, t, :], self.xv[tok0:tok0 + 128, :], writes=[self.R_xt[t]])
                self.x_loaded.add((G + 1, t))
        if cfg.debug and G == cfg.G0:
            tr.dma("pool", self.d_hT, self.hT[:, :, :], reads=self.R_hT, writes=[self.R_dbg])
        self.lru(G, TR)
        if cfg.debug and G == cfg.G0:
            tr.dma("pool", self.d_hlru, self.bufQ[:, :, :], reads=self.R_Q, writes=[self.R_dbg])
            tr.dma("pool", self.d_deriv, self.deriv[:, :], reads=[self.R_deriv], writes=[self.R_dbg])
        self.kv(G)
        self.compress(G)
        if TR is None:
            return
        self.mm_nb = 4
        c0, c1 = TR[0] * 128, TR[1] * 128
        qT = self.bufQ
        for c in range(4):
            bank, rbk = self.proj(P_WIN + 16 + c, c0, c1)
            tr.op("act", lambda e: e.activation(out=qT[:, c, c0:c1], in_=bank[:, 0:c1 - c0], func=AF.Copy, scale=0.125),
                  reads=[rbk], writes=[self.R_Q[c]])
        for h in range(4):
            bank, rbk = self.proj(P_WIN + 26 + h, c0, c1)
            tr.op("act", lambda e: e.activation(out=qT[:, 4 + h, c0:c1], in_=bank[:, 0:c1 - c0], func=AF.Copy, scale=128.0 ** -0.5),
                  reads=[rbk], writes=[self.R_Q[4 + h]])
        bank, rbk = self.bank_mm()
        for t in range(TR[0], TR[1]):
            for kc in range(KC):
                tr.op("pe", lambda e: e.matmul(bank[:, t * 24:(t + 1) * 24], lhsT=self.hT[:, kc, t * 128:(t + 1) * 128], rhs=self.wg_bf[:, kc, :],
                                               start=(kc == 0), stop=(kc == KC - 1)),
                      reads=[self.R_hT[kc], self.R_wg], writes=[rbk])
        gs = self.g_sb[:, :, :].rearrange("p a b -> p (a b)")
        tr.op("act", lambda e: e.activation(out=gs[:, TR[0] * 24:TR[1] * 24], in_=bank[:, TR[0] * 24:TR[1] * 24], func=AF.Tanh, scale=0.5),
              reads=[rbk], writes=[self.R_g])
        tr.op("dve", lambda e: e.tensor_scalar(out=gs[:, TR[0] * 24:TR[1] * 24], in0=gs[:, TR[0] * 24:TR[1] * 24], scalar1=0.5, scalar2=0.5,
                                                op0=ALU.mult, op1=ALU.add),
              reads=[self.R_g], writes=[self.R_g])
        tr.barrier()
        self.mm_nb = 3
        tr.op("dve", lambda e: e.memset(self.bufA[:, 6:8, :], 0.0), writes=self.R_A[6:8])
        units = [(t, g) for t in range(TR[0], TR[1]) for g in range(2)]
        U = [self.nsa_unit(G, t, g, k) for k, (t, g) in enumerate(units)]
        U[0]["h0"]()
        U[0]["h1"]()
        U[0]["h2"]()
        for k in range(len(units)):
            n = U[k]["n"]
            inj = {}
            if k >= 1:
                inj.setdefault(min(6, n - 1), []).append(U[k - 1]["EP"])
            if k + 1 < len(units):
                inj.setdefault(min(10, n - 1), []).append(U[k + 1]["h0"])
                inj.setdefault(min(16, n - 1), []).append(U[k + 1]["h1"])
                inj.setdefault(min(30, n + 1), []).append(U[k + 1]["h2"])
            U[k]["run"](inj)
        U[-1]["EP"]()
        self.mm_nb = 4
        self.mem_attn(c0, c1)
        if cfg.debug:
            gi = G - cfg.G0 + 1
            tr.dma("pool", self.d_mixed[gi], self.bufM[:, :, :], reads=self.R_M, writes=[self.R_dbg])
        tr.barrier()
        self.out_proj_ln1(G, TR)
        if cfg.debug:
            gi = G - cfg.G0 + 1
            tr.dma("pool", self.d_h1[gi], self.xt[:, :, :], reads=self.R_xt, writes=[self.R_dbg])
        self.ffn(G, TR)
        tr.barrier()

    def lru(self, G, TR):
        cfg, tr = self.cfg, self.tr
        need_h = TR is not None
        S = self.bufS
        dv = self.deriv
        xr = [S[:, b * 516:(b + 1) * 516] for b in range(2)]
        xc = [S[:, 1032 + b * 512:1032 + (b + 1) * 512] for b in range(2)]
        th = [S[:, 2056 + b * 512:2056 + (b + 1) * 512] for b in range(2)]
        hf = [S[:, 3080 + b * 512:3080 + (b + 1) * 512] for b in range(2)]
        xcb = [S[:, 4104 + b * 256:4104 + (b + 1) * 256].bitcast(BF16) for b in range(2)]
        gy = S[:, 4616:4872].bitcast(BF16)
        R_xr = [self.RS("xr%d" % b) for b in range(2)]
        R_xc = [self.RS("xc%d" % b) for b in range(2)]
        R_th = [self.RS("th%d" % b) for b in range(2)]
        R_hf = [self.RS("hf%d" % b) for b in range(2)]
        R_xcb = [self.RS("xcb%d" % b) for b in range(2)]
        R_gy = self.RS("gy")
        Af = self.bufA[:, :, :].rearrange("p a b -> p (a b)").bitcast(F32)
        A_a = [Af[:, s * 512:(s + 1) * 512] for s in range(2)]
        A_om = [Af[:, 1024 + s * 512:1024 + (s + 1) * 512] for s in range(2)]
        A_ix = [Af[:, 2048 + s * 512:2048 + (s + 1) * 512] for s in range(2)]
        RA_a = [self.R_A[2 * s:2 * s + 2] for s in range(2)]
        RA_om = [self.R_A[4 + 2 * s:4 + 2 * s + 2] for s in range(2)]
        RA_ix = [self.R_A[8 + 2 * s:8 + 2 * s + 2] for s in range(2)]
        wax, r_ax = self.wax_sb, self.R_wax
        gfl = self.sm("gflag", G)
        hfl = self.hflag[:, G:G + 1]
        lcw = lambda c, k: self.sm("lcw", c * 4 + k)
        w3g = S[:, 5160:5168]
        R_w3g = self.RS("w3g")
        o_l = self.lay["lcw"][0]
        tr.op("dve", lambda e: e.tensor_scalar(out=w3g, in0=self.SM[:, o_l + 3:o_l + 32:4], scalar1=gfl, scalar2=None, op0=ALU.mult),
              reads=[self.R_sm], writes=[R_w3g])
        def S1(bt):
            info = []
            for s in range(2):
                c = bt * 2 + s
                b = s
                bank, rbk = self.proj(P_WIN + c, 0, 512)
                info.append((c, b))
                tr.op("pool", lambda e: e.tensor_copy(out=xr[b][:, 0:3], in_=self.xtail[:, c, :]), reads=[self.R_xtail[c]], writes=[R_xr[b]])
                tr.op("act", lambda e: e.activation(out=xr[b][:, 3:515], in_=bank[:, 0:512], func=AF.Identity, scale=gfl),
                      reads=[rbk, self.R_sm], writes=[R_xr[b]])
                tr.op("pool", lambda e: e.tensor_copy(out=self.xtail[:, c, :], in_=xr[b][:, 512:515]), reads=[R_xr[b]], writes=[self.R_xtail[c]])
                tr.op("act", lambda e: e.activation(out=xc[b], in_=bank[:, 0:512], func=AF.Identity, scale=w3g[:, c:c + 1], bias=self.sm("lcb", c)),
                      reads=[rbk, self.R_sm, R_w3g], writes=[R_xc[b]])
            for (c, b) in info:
                for k in (2, 1, 0):
                    tr.op("dve", lambda e: e.scalar_tensor_tensor(out=xc[b], in0=xr[b][:, k:k + 512], scalar=lcw(c, k), in1=xc[b],
                                                                   op0=ALU.mult, op1=ALU.add),
                          reads=[R_xr[b], R_xc[b], self.R_sm], writes=[R_xc[b]])
                tr.op("act", lambda e: e.copy(out=xcb[b], in_=xc[b]), reads=[R_xc[b]], writes=[R_xcb[b]])

        def S2(bt):
            banks = []
            for s in range(2):
                c = bt * 2 + s
                b = s
                bank_r, rbr = self.bank_mm()
                tr.op("pe", lambda e: e.matmul(bank_r[:, 0:512], lhsT=wax[:, 0, c, :], rhs=xcb[b], start=True, stop=True),
                      reads=[r_ax, R_xcb[b]], writes=[rbr])
                bank_i, rbi = self.bank_mm()
                tr.op("pe", lambda e: e.matmul(bank_i[:, 0:512], lhsT=wax[:, 1, c, :], rhs=xcb[b], start=True, stop=True),
                      reads=[r_ax, R_xcb[b]], writes=[rbi])
                banks.append((bank_r, rbr, bank_i, rbi))
            for s in range(2):
                c = bt * 2 + s
                b = s
                bank_r, rbr, bank_i, rbi = banks[s]
                tr.op("act", lambda e: e.activation(out=th[b], in_=bank_r[:, 0:512], func=AF.Tanh, scale=0.5, bias=dv[:, 8 + c:9 + c]),
                      reads=[rbr, self.R_deriv], writes=[R_th[b]])
                tr.op("act", lambda e: e.activation(out=A_a[s], in_=th[b], func=AF.Exp, scale=dv[:, c:c + 1], bias=dv[:, c:c + 1]),
                      reads=[R_th[b], self.R_deriv], writes=RA_a[s])
                tr.op("act", lambda e: e.activation(out=th[b], in_=bank_i[:, 0:512], func=AF.Tanh, scale=0.5, bias=dv[:, 16 + c:17 + c]),
                      reads=[rbi, self.R_deriv], writes=[R_th[b]])
                tr.op("dve", lambda e: e.scalar_tensor_tensor(out=A_ix[s], in0=th[b], scalar=1.0, in1=xc[b], op0=ALU.add, op1=ALU.mult),
                      reads=[R_th[b], R_xc[b]], writes=RA_ix[s])
                tr.op("act", lambda e: e.activation(out=A_om[s], in_=A_a[s], func=AF.Square), reads=RA_a[s], writes=RA_om[s])

        def S3(bt):
            tr.op("act", lambda e: e.activation(out=Af[:, 1024:2048], in_=Af[:, 1024:2048], func=AF.Sqrt, scale=-1.0, bias=self.one_ap),
                  reads=self.R_A[4:8] + [self.R_deriv], writes=self.R_A[4:8])
            for s in range(2):
                c = bt * 2 + s
                b = s
                tr.op("dve", lambda e: e.scalar_tensor_tensor(out=A_ix[s], in0=A_ix[s], scalar=hfl, in1=A_om[s], op0=ALU.mult, op1=ALU.mult),
                      reads=RA_ix[s] + RA_om[s] + [self.R_deriv], writes=RA_ix[s])
                tr.op("dve", lambda e: e.tensor_tensor_scan(out=hf[b], data0=A_a[s], data1=A_ix[s], initial=self.state[:, c:c + 1],
                                                             op0=ALU.mult, op1=ALU.add),
                      reads=RA_a[s] + RA_ix[s] + [self.R_state[c]], writes=[R_hf[b]])
                tr.op("pool", lambda e: e.tensor_copy(out=self.state[:, c:c + 1], in_=hf[b][:, 511:512]), reads=[R_hf[b]], writes=[self.R_state[c]])
                if need_h:
                    tr.op("pool", lambda e: e.tensor_copy(out=self.bufQ[:, c, :], in_=hf[b]), reads=[R_hf[b]], writes=[self.R_Q[c]])

        S1(0)
        for bt in range(4):
            S2(bt)
            if bt < 3:
                S1(bt + 1)
            S3(bt)
        if need_h:
            c0, c1 = TR[0] * 128, TR[1] * 128
            n = c1 - c0
            for c in range(8):
                bank, rbk = self.proj(P_WIN + 8 + c, c0, c1)
                tr.op("act", lambda e: e.activation(out=gy[:, 0:n], in_=bank[:, 0:n], func=AF.Gelu_apprx_tanh), reads=[rbk], writes=[R_gy])
                tr.op("dve", lambda e: e.tensor_tensor(out=self.bufM[:, c, c0:c1], in0=gy[:, 0:n], in1=self.bufQ[:, c, c0:c1], op=ALU.mult),
                      reads=[R_gy, self.R_Q[c]], writes=[self.R_M[c]])

    def kv(self, G):
        cfg, tr = self.cfg, self.tr
        S = self.bufS
        vt = S[:, 4872:5128].bitcast(BF16)
        R_vt = self.RS("vt")
        gfl = self.sm("gflag", G)
        for pid, loc, rl in ((P_WIN + 20, self.kc_loc, self.R_kcl), (P_WIN + 21, self.vc_loc, self.R_vcl)):
            bank, rbk = self.proj(pid, 0, 512)
            tr.op("pool", lambda e: e.tensor_copy(out=loc[:, 0:16], in_=loc[:, 512:528]), reads=[rl], writes=[rl])
            tr.op("act", lambda e: e.copy(out=loc[:, 16:528], in_=bank[:, 0:512]), reads=[rbk], writes=[rl])
        bank, rbk = self.proj(P_WIN + 22, 0, 512)
        tr.op("act", lambda e: e.copy(out=self.ksT[:, G * GT:(G + 1) * GT], in_=bank[:, 0:512]), reads=[rbk], writes=self.R_ks[4 * G:4 * G + 4])

        def store_v(pid, V1, RV, tbase, ta):
            bank, rbk = self.proj(pid, 0, 512)
            tr.op("act", lambda e: e.activation(out=vt, in_=bank[:, 0:512], func=AF.Identity, scale=gfl), reads=[rbk, self.R_sm], writes=[R_vt])
            bank2, rb2 = self.bank_mm()
            bb = self.bfv(bank2)
            for t in range(ta, 4):
                tr.op("pe", lambda e: e.transpose(out=bb[:, t * 128:(t + 1) * 128], in_=vt[:, t * 128:(t + 1) * 128], identity=self.masks[:, 0, :]),
                      reads=[R_vt, self.R_const], writes=[rb2])
            nt_ = 4 - ta
            i0 = 4 * G + ta - tbase
            tr.op("dve", lambda e: e.tensor_copy(out=V1[:, i0:i0 + nt_, :, 0:64],
                                                 in_=bb[:, ta * 128:512].rearrange("p (t g d) -> p t g d", t=nt_, g=2)),
                  reads=[rb2], writes=RV[i0:i0 + nt_])
            tr.op("pool", lambda e: e.tensor_scalar(out=V1[:, i0:i0 + nt_, :, 64],
                                                     in0=self.ones_bf[:, 0:2 * nt_].rearrange("p (a b) -> p a b", a=nt_),
                                                     scalar1=gfl, scalar2=None, op0=ALU.mult),
                  reads=[self.R_const, self.R_sm], writes=RV[i0:i0 + nt_])

        store_v(P_WIN + 23, self.V1s, self.R_vs, 0, 0)
        if 4 * G + 3 >= cfg.WT0:
            ta = max(0, cfg.WT0 - 4 * G)
            bank, rbk = self.proj(P_WIN + 24, 0, 512)
            w0 = 4 * G + ta - cfg.WT0
            tr.op("act", lambda e: e.copy(out=self.kwT[:, w0 * 128:(w0 + 4 - ta) * 128], in_=bank[:, ta * 128:512]),
                  reads=[rbk], writes=self.R_kw[w0:w0 + 4 - ta])
            store_v(P_WIN + 25, self.V1w, self.R_vw, cfg.WT0, ta)

    def compress(self, G):
        cfg, tr = self.cfg, self.tr
        S = self.bufS
        dv = self.deriv
        m0 = 1 if G == 0 else 0
        col0 = 32 * G - 1 + m0
        ncol = 32 - m0
        for kind in range(2):
            pid = P_W1K if kind == 0 else P_W1V
            loc, rl = (self.kc_loc, self.R_kcl) if kind == 0 else (self.vc_loc, self.R_vcl)
            hid = S[:, 5128 + kind * 16:5128 + (kind + 1) * 16].bitcast(BF16)
            R_hid = self.RS("hid%d" % kind)
            bank, rbk = self.bank_mm()
            for half in range(2):
                buf, rb = self.fetch(pid + half)
                w1 = buf.rearrange("p (l e) -> p l e", e=128)
                for li in range(16):
                    l = half * 16 + li
                    tr.op("pe", lambda e: e.matmul(bank[:, 0:32], lhsT=w1[:, li, :], rhs=loc[:, l:l + 497:16], start=(l == 0), stop=(l == 31)),
                          reads=[rb, rl], writes=[rbk])
            tr.op("act", lambda e: e.activation(out=hid, in_=bank[:, 0:32], func=AF.Gelu_apprx_tanh, bias=dv[:, 24 + kind:25 + kind]),
                  reads=[rbk, self.R_deriv], writes=[R_hid])
            bank2, rb2 = self.bank_mm()
            tr.op("pe", lambda e: e.matmul(bank2[:, 0:32], lhsT=self.smb[:, 128 * kind:128 * kind + 128], rhs=hid, start=True, stop=True),
                  reads=[self.R_smb, R_hid], writes=[rb2])
            if kind == 0:
                tr.op("act", lambda e: e.copy(out=self.kcmpT[:, col0:col0 + ncol], in_=bank2[:, m0:32]), reads=[rb2], writes=[self.R_kcmp])
            else:
                tr.op("act", lambda e: e.copy(out=self.vcmpT[:, col0:col0 + ncol], in_=bank2[:, m0:32]), reads=[rb2], writes=[self.R_vcmp])
                for nt in sorted({col0 // 128, (col0 + ncol - 1) // 128}):
                    bankT, rbT = self.bank_mm()
                    bb = self.bfv(bankT)
                    tr.op("pe", lambda e: e.transpose(out=bb[:, 0:128], in_=self.vcmpT[:, nt * 128:(nt + 1) * 128], identity=self.masks[:, 0, :]),
                          reads=[self.R_vcmp, self.R_const], writes=[rbT])
                    tr.op("dve", lambda e: e.tensor_copy(out=self.vcmp1[:, nt, :, 0:64], in_=bb[:, 0:128].rearrange("p (g d) -> p g d", g=2)),
                          reads=[rbT], writes=[self.R_vcmp])

    def nsa_unit(self, G, t, g, uidx):
        cfg, tr = self.cfg, self.tr
        vq = 4 * G + t
        ti = vq - cfg.T_HALO
        S = self.bufS
        A = self.bufA
        NCT, NB = cfg.NCT, cfg.NBLK
        p = uidx % 2
        bf = lambda o, n: S[:, o:o + n].bitcast(BF16)
        RS = self.RS
        cm = bf(0, NCT * 64).rearrange("p (n q) -> p n q", q=128)
        R_cm = RS("cm")
        if p == 0:
            Pc = [bf(256 + n * 256, 256) for n in range(NCT)]
            R_Pc = [RS("Pc%d" % n) for n in range(NCT)]
            qz, R_qz = bf(2304, 256), RS("qz")
            negm, R_negm = self.negm2, [self.R_negm2]
            OT0, R_OT0 = S[:, 2904:2904 + 512], [RS("OT0")]
        else:
            Pc = [A[:, 2 + n, :] for n in range(NCT)]
            R_Pc = [self.R_A[2 + n] for n in range(NCT)]
            qz, R_qz = A[:, 1, :], self.R_A[1]
            negm, R_negm = A[:, 6:8, :], self.R_A[6:8]
            OT0, R_OT0 = A[:, 8:10, :].rearrange("p a b -> p (a b)").bitcast(F32), self.R_A[8:10]
        E4 = [bf(1280 + i * 256, 256) for i in range(3)] + [A[:, 0, :]]
        R_E4 = [RS("E%d" % i) for i in range(3)] + [self.R_A[0]]
        imp = S[:, 2560:2560 + NB]
        impw = S[:, 2688:2688 + NB]
        m8 = S[:, 2816:2832]
        zc = S[:, 2832:2836]
        rz = S[:, 2836:2840]
        sel = bf(2840, 64)[:, 0:NB]
        OT = [OT0, S[:, 2904 + 512:2904 + 1024], S[:, 2904 + 1024:2904 + 1536]]
        R_OT = [R_OT0, [RS("OT1")], [RS("OT2")]]
        etmp = S[:, 4440:4824]
        otok = bf(4824, 256)
        coef = S[:, 5080:5086]
        zz = S[:, 5086:5092]
        R_imp, R_m8, R_sel = RS("imp"), RS("m8"), RS("sel")
        R_et, R_otok, R_coef = RS("etmp"), RS("otok"), RS("coef")
        qT = self.bufQ
        tq = slice(t * 128, (t + 1) * 128)
        ident_b = self.masks[:, 0, :]
        v4 = lambda ap: ap.rearrange("p (h q) -> p h q", h=4)
        bc4 = lambda ap: ap.unsqueeze(1).broadcast_to([128, 4, 128])
        rq = self.R_Q[0:4]
        bOc, rOc = self.ps[4], self.R_ps[4]
        bOs, rOs = self.ps[5], self.R_ps[5]
        bI = [(self.ps[6], self.R_ps[6]), (self.ps[3], self.R_ps[3])]
        bOw, rOw = self.ps[7], self.R_ps[7]
        SKEW = 2

        def h0():
            if g == 0:
                tr.dma("sp", cm, self.cmask_d[ti], writes=[R_cm])
            tr.op("act", lambda e: e.activation(out=v4(qz), in_=qT[:, 0:4, tq], func=AF.Identity, scale=self.sm("hm", g)),
                  reads=rq + [self.R_sm], writes=[R_qz])
            for nt in range(NCT):
                bank, rbk = self.bank_mm()
                tr.op("pe", lambda e: e.matmul(bank[:, 0:512], lhsT=self.kcmpT[:, nt * 128:(nt + 1) * 128], rhs=qz, start=True, stop=True),
                      reads=[self.R_kcmp, R_qz], writes=[rbk])
                tr.op("act", lambda e: e.activation(out=Pc[nt], in_=bank[:, 0:512], func=AF.Exp), reads=[rbk], writes=[R_Pc[nt]])
                tr.op("dve", lambda e: e.tensor_tensor(out=v4(Pc[nt]), in0=v4(Pc[nt]), in1=bc4(cm[:, nt, :]), op=ALU.mult),
                      reads=[R_Pc[nt], R_cm], writes=[R_Pc[nt]])

        def h1():
            for nt in range(NCT):
                tr.op("pe", lambda e: e.matmul(bOc[0:65, 0:512], lhsT=self.vcmp1[:, nt, g, :], rhs=Pc[nt], start=(nt == 0), stop=(nt == NCT - 1)),
                      reads=[self.R_vcmp, R_Pc[nt]], writes=[rOc])
            for hh in range(2):
                bk, rk = bI[hh]
                for h2_ in range(2):
                    h = 2 * hh + h2_
                    for nt in range(NCT):
                        tr.op("pe", lambda e: e.matmul(bk[:, h2_ * (NB + 1):(h2_ + 1) * (NB + 1)], lhsT=Pc[nt][:, h * 128:(h + 1) * 128],
                                                       rhs=self.cover1[:, nt, :], start=(nt == 0), stop=(nt == NCT - 1)),
                              reads=[R_Pc[nt], self.R_const], writes=[rk])
            for hh in range(2):
                bk, rk = bI[hh]
                tr.op("dve", lambda e: e.tensor_scalar(out=zc[:, 2 * hh:2 * hh + 2], in0=bk[:, NB:2 * (NB + 1):NB + 1], scalar1=TINY, scalar2=None,
                                                        op0=ALU.max),
                      reads=[rk], writes=[R_m8])
            tr.op("dve", lambda e: e.reciprocal(out=rz, in_=zc), reads=[R_m8], writes=[R_m8])
            for h in range(4):
                bk, rk = bI[h // 2]
                src = bk[:, (h % 2) * (NB + 1):(h % 2) * (NB + 1) + NB]
                if h == 0:
                    tr.op("dve", lambda e: e.scalar_tensor_tensor(out=imp, in0=src, scalar=rz[:, 0:1], in1=self.sm("fbias"), op0=ALU.mult, op1=ALU.add),
                          reads=[rk, R_m8, self.R_sm], writes=[R_imp])
                else:
                    tr.op("dve", lambda e: e.scalar_tensor_tensor(out=imp, in0=src, scalar=rz[:, h:h + 1], in1=imp, op0=ALU.mult, op1=ALU.add),
                          reads=[rk, R_m8, R_imp], writes=[R_imp])
            lo0 = max(0, 2 * vq - 1)
            tr.op("dve", lambda e: e.memset(imp[0:64, lo0:2 * vq + 1], 1e9), reads=[R_imp], writes=[R_imp])
            tr.op("dve", lambda e: e.memset(imp[64:128, 2 * vq:2 * vq + 2], 1e9), reads=[R_imp], writes=[R_imp])
            tr.op("dve", lambda e: e.max(out=m8[:, 0:8], in_=imp), reads=[R_imp], writes=[R_m8])
            tr.op("dve", lambda e: e.match_replace(out=impw, in_to_replace=m8[:, 0:8], in_values=imp, imm_value=-1e30),
                  reads=[R_imp, R_m8], writes=[R_sel])
            tr.op("dve", lambda e: e.max(out=m8[:, 8:16], in_=impw), reads=[R_sel], writes=[R_m8])
            tr.op("dve", lambda e: e.tensor_scalar(out=sel, in0=imp, scalar1=m8[:, 15:16], scalar2=None, op0=ALU.is_ge),
                  reads=[R_imp, R_m8, R_sel], writes=[R_sel])

        def h2():
            bankT, rbT = self.bank_mm()
            bbT = self.bfv(bankT)
            tr.op("pe", lambda e: e.transpose(out=bbT[0:NB, 0:128], in_=sel, identity=ident_b), reads=[R_sel, self.R_const], writes=[rbT])
            for hb in range(2):
                if 64 * hb >= NB:
                    continue
                rows = slice(64 * hb, min(NB, 64 * hb + 64))
                nr = rows.stop - rows.start
                tr.op("dve", lambda e: e.tensor_scalar(out=v4(negm[:, hb, :])[rows],
                                                        in0=bbT[rows, 0:128].unsqueeze(1).broadcast_to([nr, 4, 128]),
                                                        scalar1=30000.0, scalar2=-30000.0, op0=ALU.mult, op1=ALU.add),
                      reads=[rbT], writes=R_negm)
            tr.op("act", lambda e: e.copy(out=OT[0][0:65, :], in_=bOc[0:65, 0:512]), reads=[rOc], writes=R_OT[0])

        steps = []
        kts = [kt for kt in range(vq - 4, vq + 1) if kt >= 0]
        for i, kt in enumerate(kts):
            def wf(j, i=i, kt=kt):
                w = kt - cfg.WT0
                masked = (kt == vq or kt == vq - 4)
                bank, rbk = self.bank_mm()
                tr.op("pe", lambda e: e.matmul(bank[:, 0:512], lhsT=self.kwT[:, w * 128:(w + 1) * 128], rhs=qz, start=True, stop=(not masked)),
                      reads=[self.R_kw[w], R_qz], writes=[rbk])
                if masked:
                    mi = 0 if kt == vq else 1
                    tr.op("pe", lambda e: e.matmul(bank[:, 0:512], lhsT=ident_b, rhs=self.negc[:, mi, :], start=False, stop=True),
                          reads=[self.R_const], writes=[rbk])
                tr.op("act", lambda e: e.activation(out=E4[j % 4], in_=bank[:, 0:512], func=AF.Exp), reads=[rbk], writes=[R_E4[j % 4]])

            def wb(j, i=i, kt=kt):
                w = kt - cfg.WT0
                tr.op("pe", lambda e: e.matmul(bOw[0:65, 0:512], lhsT=self.V1w[:, w, g, :], rhs=E4[j % 4], start=(i == 0), stop=(i == len(kts) - 1)),
                      reads=[self.R_vw[w], R_E4[j % 4]], writes=[rOw])
            steps.append((wf, wb))
        nstep = vq + 1
        for kt in range(nstep):
            def sf(j, kt=kt):
                bank, rbk = self.bank_mm()
                tr.op("pe", lambda e: e.matmul(bank[:, 0:512], lhsT=self.ksT[:, kt * 128:(kt + 1) * 128], rhs=qz, start=True, stop=False),
                      reads=[self.R_ks[kt], R_qz], writes=[rbk])
                r = (2 * kt) // 64
                jj = kt % 32
                tr.op("pe", lambda e: e.matmul(bank[:, 0:512], lhsT=self.eec[:, jj, :], rhs=negm[:, r, :],
                                               start=False, stop=(kt != vq)),
                      reads=[self.R_const] + R_negm, writes=[rbk])
                if kt == vq:
                    tr.op("pe", lambda e: e.matmul(bank[:, 0:512], lhsT=ident_b, rhs=self.negc[:, 0, :], start=False, stop=True),
                          reads=[self.R_const], writes=[rbk])
                tr.op("act", lambda e: e.activation(out=E4[j % 4], in_=bank[:, 0:512], func=AF.Exp), reads=[rbk], writes=[R_E4[j % 4]])

            def sb(j, kt=kt):
                tr.op("pe", lambda e: e.matmul(bOs[0:65, 0:512], lhsT=self.V1s[:, kt, g, :], rhs=E4[j % 4], start=(kt == 0), stop=(kt == nstep - 1)),
                      reads=[self.R_vs[kt], R_E4[j % 4]], writes=[rOs])
            steps.append((sf, sb))

        def run(inject):
            n = len(steps)
            for i in range(n + SKEW):
                if i < n:
                    steps[i][0](i)
                if i >= SKEW:
                    steps[i - SKEW][1](i - SKEW)
                for fn in inject.get(i, ()):
                    fn()
            for i in sorted(k for k in inject if k >= n + SKEW):
                for fn in inject[i]:
                    fn()
            tr.op("dve", lambda e: e.tensor_copy(out=OT[1][0:65, :], in_=bOs[0:65, 0:512]), reads=[rOs], writes=R_OT[1])
            tr.op("act", lambda e: e.copy(out=OT[2][0:65, :], in_=bOw[0:65, 0:512]), reads=[rOw], writes=R_OT[2])

        def EP():
            for hh in range(2):
                bankE, rbE = self.bank_mm()
                for c2 in range(2):
                    c = 2 * hh + c2
                    for b in range(3):
                        k = c2 * 3 + b
                        tr.op("pe", lambda e: e.transpose(out=bankE[:, k * 65:(k + 1) * 65], in_=OT[b][0:65, c * 128:(c + 1) * 128],
                                                          identity=self.ident_f[0:65, 0:65]),
                              reads=R_OT[b] + [self.R_const], writes=[rbE])
                tr.op("dve", lambda e: e.tensor_scalar(out=zz, in0=bankE[:, 64:390:65], scalar1=TINY, scalar2=None, op0=ALU.max),
                      reads=[rbE], writes=[R_coef])
                tr.op("dve", lambda e: e.reciprocal(out=zz, in_=zz), reads=[R_coef], writes=[R_coef])
                tr.op("dve", lambda e: e.tensor_tensor(out=coef, in0=zz, in1=self.g_sb[:, t, 12 * g + 6 * hh:12 * g + 6 * hh + 6], op=ALU.mult),
                      reads=[R_coef, self.R_g], writes=[R_coef])
                tr.op("dve", lambda e: e.tensor_tensor(out=etmp.rearrange("p (k d) -> p k d", d=64),
                                                       in0=bankE[:, 0:390].rearrange("p (k d) -> p k d", d=65)[:, :, 0:64],
                                                       in1=coef.unsqueeze(2).broadcast_to([128, 6, 64]), op=ALU.mult),
                      reads=[rbE, R_coef], writes=[R_et])
                o0 = 256 * g + 128 * hh
                with self.nc.allow_low_precision(reason="fp32 reduce, bf16 store"):
                    tr.op("dve", lambda e: e.tensor_reduce(out=otok[:, o0:o0 + 128].rearrange("p (c d) -> p c d", c=2),
                                                           in_=etmp.rearrange("p (c b d) -> p c d b", c=2, b=3), axis=AX.X, op=ALU.add),
                          reads=[R_et], writes=[R_otok])
            if g == 1:
                bankF, rbF = self.bank_mm()
                bbF = self.bfv(bankF)
                for cc in range(4):
                    tr.op("pe", lambda e: e.transpose(out=bbF[:, cc * 128:(cc + 1) * 128], in_=otok[:, cc * 128:(cc + 1) * 128], identity=ident_b),
                          reads=[R_otok, self.R_const], writes=[rbF])
                tr.op("act", lambda e: e.copy(out=self.bufM[:, 8:12, tq], in_=bbF[:, 0:512].rearrange("p (c q) -> p c q", c=4)),
                      reads=[rbF], writes=self.R_M[8:12])

        return dict(h0=h0, h1=h1, h2=h2, run=run, EP=EP, n=len(steps))

    def mem_attn(self, c0, c1):
        cfg, tr = self.cfg, self.tr
        S = self.bufS
        n = c1 - c0
        E = [S[:, 1280 + i * 256:1280 + (i + 1) * 256].bitcast(BF16) for i in range(3)]
        R_E = [self.RS("E%d" % i) for i in range(3)]
        rzm = S[:, 2904:2904 + 512]
        R_rz = self.RS("OT0")
        for h in range(4):
            for mt in range(2):
                bank, rbk = self.bank_mm()
                tr.op("pe", lambda e: e.matmul(bank[:, 0:n], lhsT=self.kmemT[:, h, mt * 128:(mt + 1) * 128], rhs=self.bufQ[:, 4 + h, c0:c1],
                                               start=True, stop=True),
                      reads=[self.R_kmem, self.R_Q[4 + h]], writes=[rbk])
                tr.op("act", lambda e: e.activation(out=E[mt][:, 0:n], in_=bank[:, 0:n], func=AF.Exp), reads=[rbk], writes=[R_E[mt]])
            bO, rO = self.ps[4], self.R_ps[4]
            bZ, rZ = self.ps[5], self.R_ps[5]
            for mt in range(2):
                tr.op("pe", lambda e: e.matmul(bO[:, 0:n], lhsT=self.vmem[:, mt, h, :], rhs=E[mt][:, 0:n], start=(mt == 0), stop=(mt == 1)),
                      reads=[self.R_vmem, R_E[mt]], writes=[rO])
            for mt in range(2):
                tr.op("pe", lambda e: e.matmul(bZ[:, 0:n], lhsT=self.ones_bf[:, :], rhs=E[mt][:, 0:n], start=(mt == 0), stop=(mt == 1)),
                      reads=[self.R_const, R_E[mt]], writes=[rZ])
            tr.op("dve", lambda e: e.reciprocal(out=rzm[:, 0:n], in_=bZ[:, 0:n]), reads=[rZ], writes=[R_rz])
            tr.op("dve", lambda e: e.tensor_tensor(out=self.bufM[:, 12 + h, c0:c1], in0=bO[:, 0:n], in1=rzm[:, 0:n], op=ALU.mult),
                  reads=[rO, R_rz], writes=[self.R_M[12 + h]])

    def out_proj_ln1(self, G, TR):
        cfg, tr = self.cfg, self.tr
        tiles = list(range(TR[0], TR[1]))
        acc = {t: (self.ps[4 + i], self.R_ps[4 + i]) for i, t in enumerate(tiles)}
        gbuf0 = self.load_gb(0)
        for t in tiles:
            tr.op("dve", lambda e: e.scalar_tensor_tensor(out=self.xt[:, t, :], in0=self.xt[:, t, :], scalar=self.mv[:, t, 0:1], in1=gbuf0[:, 0:D],
                                                           op0=ALU.subtract, op1=ALU.mult),
                  reads=[self.R_xt[t], self.R_mv] + self.R_A[0:8], writes=[self.R_xt[t]])
            tr.op("dve", lambda e: e.scalar_tensor_tensor(out=self.xt[:, t, :], in0=self.xt[:, t, :], scalar=self.lnt[:, 1, t:t + 1], in1=gbuf0[:, D:2 * D],
                                                           op0=ALU.mult, op1=ALU.add),
                  reads=[self.R_xt[t], self.R_lnt] + self.R_A[8:16], writes=[self.R_xt[t]])
        for cb in range(4):
            for kq in range(4):
                buf, rb = self.fetch(P_WOUT + cb * 4 + kq)
                wv = buf.rearrange("p (a b) -> p a b", b=512)
                for t in tiles:
                    bk, rk = acc[t]
                    for kci in range(4):
                        kc = kq * 4 + kci
                        tr.op("pe", lambda e: e.matmul(bk[:, 0:512], lhsT=self.bufM[:, kc, t * 128:(t + 1) * 128], rhs=wv[:, kci, :],
                                                       start=(kc == 0), stop=(kc == KC - 1)),
                              reads=[rb, self.R_M[kc]], writes=[rk])
            for t in tiles:
                bk, rk = acc[t]
                xs = self.xt[:, t, cb * 512:(cb + 1) * 512]
                tr.op("dve", lambda e: e.scalar_tensor_tensor(out=xs, in0=xs, scalar=ALPHA, in1=bk[:, 0:512], op0=ALU.mult, op1=ALU.add),
                      reads=[rk, self.R_xt[t]], writes=[self.R_xt[t]])
        gbuf = self.load_gb(1)
        self.ln_stats(tiles)
        for t in tiles:
            self.ln_apply(t, gbuf, "g1", "b1", want_tok=True)
        self.ln_transpose(tiles, "g1", "b1")

    def ffn(self, G, TR):
        cfg, tr = self.cfg, self.tr
        S = self.bufS
        c0, c1 = TR[0] * 128, TR[1] * 128
        n = c1 - c0
        if TR != (0, 4):
            for c in range(88):
                bank, rbk = self.proj(P_WUP + c, c1 - 2, c1)
                tr.op("dve", lambda e: e.tensor_scalar(out=self.ftail[:, c, :], in0=bank[:, 0:2], scalar1=self.sm("gflag", G), scalar2=None,
                                                        op0=ALU.mult),
                      reads=[rbk, self.R_sm], writes=[self.R_ftail[c]])
            return
        ext = [[S[:, (2 * w + b) * 516:(2 * w + b + 1) * 516] for b in range(2)] for w in range(2)]
        cv = [[S[:, 2064 + (2 * w + b) * 512:2064 + (2 * w + b + 1) * 512] for b in range(2)] for w in range(2)]
        ga = [S[:, 4112 + b * 512:4112 + (b + 1) * 512] for b in range(2)]
        R_ext = [[self.RS("ext%d%d" % (w, b)) for b in range(2)] for w in range(2)]
        R_cv = [[self.RS("cv%d%d" % (w, b)) for b in range(2)] for w in range(2)]
        R_ga = [self.RS("ga%d" % b) for b in range(2)]
        actT = self.bufA
        fcw = lambda cc, k: self.sm("fcw", cc * 3 + k)
        acc = [(self.ps[4 + t], self.R_ps[4 + t]) for t in range(4)]
        for third, (lo, hi) in enumerate(((0, 16), (16, 32), (32, 44))):
            for c in range(lo, hi):
                b = c % 2
                for w, cc in ((0, c), (1, 44 + c)):
                    bank, rbk = self.proj(P_WUP + cc, 0, 512)
                    ex, rex = ext[w][b], R_ext[w][b]
                    cvv, rcv = cv[w][b], R_cv[w][b]
                    tr.op("pool", lambda e: e.tensor_copy(out=ex[:, 0:2], in_=self.ftail[:, cc, :]), reads=[self.R_ftail[cc]], writes=[rex])
                    tr.op("act", lambda e: e.copy(out=ex[:, 2:514], in_=bank[:, 0:512]), reads=[rbk], writes=[rex])
                    tr.op("pool", lambda e: e.tensor_copy(out=self.ftail[:, cc, :], in_=ex[:, 512:514]), reads=[rex], writes=[self.R_ftail[cc]])
                    tr.op("pool", lambda e: e.tensor_scalar(out=cvv, in0=ex[:, 2:514], scalar1=fcw(cc, 2), scalar2=self.sm("fcb", cc),
                                                             op0=ALU.mult, op1=ALU.add),
                          reads=[rex, self.R_sm], writes=[rcv])
                    for k in (1, 0):
                        tr.op("dve", lambda e: e.scalar_tensor_tensor(out=cvv, in0=ex[:, k:k + 512], scalar=fcw(cc, k), in1=cvv,
                                                                       op0=ALU.mult, op1=ALU.add),
                              reads=[rex, rcv, self.R_sm], writes=[rcv])
                tr.op("act", lambda e: e.activation(out=ga[b], in_=cv[0][b], func=AF.Gelu_apprx_tanh), reads=[R_cv[0][b]], writes=[R_ga[b]])
                tr.op("dve", lambda e: e.tensor_tensor(out=actT[:, c - lo, :], in0=ga[b], in1=cv[1][b], op=ALU.mult),
                      reads=[R_ga[b], R_cv[1][b]], writes=[self.R_A[c - lo]])
            for cb in range(4):
                for fq in range(lo // 4, hi // 4):
                    buf, rb = self.fetch(P_WDN + cb * 11 + fq)
                    wv = buf.rearrange("p (a b) -> p a b", b=512)
                    for t in range(4):
                        bk, rk = acc[t]
                        for fi in range(4):
                            ffc = fq * 4 + fi
                            tr.op("pe", lambda e: e.matmul(bk[:, 0:512], lhsT=actT[:, ffc - lo, t * 128:(t + 1) * 128], rhs=wv[:, fi, :],
                                                           start=(ffc == lo), stop=(ffc == hi - 1)),
                                  reads=[rb, self.R_A[ffc - lo]], writes=[rk])
                for t in range(4):
                    bk, rk = acc[t]
                    xs = self.xt[:, t, cb * 512:(cb + 1) * 512]
                    if third == 0:
                        tr.op("dve", lambda e: e.scalar_tensor_tensor(out=xs, in0=xs, scalar=ALPHA, in1=bk[:, 0:512], op0=ALU.mult, op1=ALU.add),
                              reads=[rk, self.R_xt[t]], writes=[self.R_xt[t]])
                    else:
                        tr.op("dve", lambda e: e.tensor_tensor(out=xs, in0=xs, in1=bk[:, 0:512], op=ALU.add),
                              reads=[rk, self.R_xt[t]], writes=[self.R_xt[t]])
        gbuf = self.load_gb(2)
        self.ln_stats([0, 1, 2, 3])
        for t in range(4):
            self.ln_apply(t, gbuf, "g2", "b2", want_tok=True, want_T=False)
            r0 = (G - cfg.G0) * GT + t * 128
            tr.dma("pool", self.y[r0:r0 + 128, :], self.xt[:, t, :], reads=[self.R_xt[t]], writes=[self.R_y])


def kernel(**inputs):
    cfg = Cfg(SEQ=np.asarray(inputs["x"]).shape[1])
    prog = Prog(cfg)
    prog.emit(dry=True)
    prog.emit(dry=False)
    sh = prep_shared(cfg, inputs)
    in_maps = [prep_core(cfg, inputs, sh, c) for c in range(8)]
    res = run_bass_kernel_spmd(prog.nc, in_maps, core_ids=list(range(8)))
    B = 2
    out = np.zeros((B, cfg.SEQ, D), np.float32)
    for c in range(8):
        b, j = c // 4, c % 4
        out[b, j * cfg.CH:(j + 1) * cfg.CH] = np.asarray(res.results[c]["y"], np.float32)
    return out
```

```python
import numpy as np
import ml_dtypes
import concourse.bass as bass
import concourse.mybir as mybir
from concourse.bass_utils import run_bass_kernel_spmd

F32 = mybir.dt.float32
BF16 = mybir.dt.bfloat16
AF = mybir.ActivationFunctionType
ALU = mybir.AluOpType
AX = mybir.AxisListType
NPBF = ml_dtypes.bfloat16

D = 2048
KC = 16
DFF = 5632
NFF = 44
LN_EPS = 1e-5
ALPHA = 2.0 ** 0.25
GT = 512
TINY = 1e-30


class Region:
    __slots__ = ("name", "w", "r", "excl")

    def __init__(self, name, excl=False):
        self.name = name
        self.w = None
        self.r = []
        self.excl = excl


class Eng:
    def __init__(self, name, h, sem):
        self.name = name
        self.h = h
        self.sem = sem
        self.key = name
        self.cnt = 0
        self.known = {}


class Tracker:
    def __init__(self, nc, n_dma_sems=40):
        self.nc = nc
        self.sems = {}
        self.eng = {}
        for name, h in (("pe", nc.tensor), ("act", nc.scalar), ("dve", nc.vector),
                        ("pool", nc.gpsimd), ("sp", nc.sync)):
            sem = nc.alloc_semaphore("cnt_" + name)
            self.sems[name] = sem
            self.eng[name] = Eng(name, h, sem)
        self.dma_sems = []
        self.dma_sets = {}
        for sname, cnt in (("main", n_dma_sems), ("cast", 24)):
            lst = []
            for i in range(cnt):
                key = "dma_%s%d" % (sname, i)
                sem = nc.alloc_semaphore(key)
                self.sems[key] = sem
                slot = [key, sem, 0, None]
                lst.append(slot)
                self.dma_sems.append(slot)
            self.dma_sets[sname] = [lst, 0]
        self.n_wait = 0
        self.n_ins = 0

    def _need(self, e, deps):
        best = {}
        for ev in deps:
            if ev is None:
                continue
            k, v, hist = ev
            if k == e.key and e.name == "pe":
                continue
            if e.known.get(k, 0) >= v:
                continue
            if k not in best or best[k][1] < v:
                best[k] = ev
        if not best:
            return
        newk = dict(e.known)
        for k, (kk, v, hist) in best.items():
            e.h.wait_ge(self.sems[k], v)
            self.n_wait += 1
            if hist is not None:
                for k2, v2 in hist.items():
                    if newk.get(k2, 0) < v2:
                        newk[k2] = v2
            if newk.get(k, 0) < v:
                newk[k] = v
        e.known = newk

    def _collect(self, e, reads, writes):
        deps = []
        for r in reads:
            deps.append(r.w)
            if r.excl:
                deps.extend(x for x in r.r if x[0] != e.key)
        for w in writes:
            if w.w is not None:
                deps.append(w.w)
            deps.extend(w.r)
        return deps

    def _update(self, ev, reads, writes):
        for r in reads:
            r.r = [x for x in r.r if x[0] != ev[0]]
            r.r.append(ev)
        for w in writes:
            w.w = ev
            w.r = []

    def op(self, en, fn, reads=(), writes=()):
        e = self.eng[en]
        self._need(e, self._collect(e, reads, writes))
        ins = fn(e.h)
        e.cnt += 1
        ins.then_inc(e.sem, 1)
        self.n_ins += 1
        ev = (e.key, e.cnt, e.known)
        self._update(ev, reads, writes)
        return ev

    def dma(self, qn, out, in_, reads=(), writes=(), slots="main", **kw):
        e = self.eng[qn]
        st = self.dma_sets[slots]
        slot = st[0][st[1]]
        st[1] = (st[1] + 1) % len(st[0])
        deps = self._collect(e, reads, writes)
        if slot[3] is not None:
            deps.append(slot[3])
        self._need(e, deps)
        slot[2] += 16
        ins = e.h.dma_start(out=out, in_=in_, **kw)
        ins.then_inc(slot[1], 16)
        self.n_ins += 1
        ev = (slot[0], slot[2], e.known)
        slot[3] = ev
        self._update(ev, reads, writes)
        return ev

    def wait_events(self, en, events):
        self._need(self.eng[en], events)

    def barrier(self, names=("pe", "act", "dve", "pool")):
        evs = [(self.eng[n].key, self.eng[n].cnt, self.eng[n].known) for n in names
               if self.eng[n].cnt > 0]
        for n in names:
            self._need(self.eng[n], evs)


class Cfg:
    def __init__(self, SEQ=8192, debug=False):
        self.SEQ = SEQ
        self.CH = SEQ // 4
        self.NG = SEQ // GT
        self.OWN_G = self.CH // GT
        self.G0 = self.NG - self.OWN_G
        self.NT = SEQ // 128
        self.NBLK = SEQ // 64
        self.NCMP = SEQ // 16
        self.NCT = max(1, self.NCMP // 128)
        self.NCP = min(128, self.NCMP)
        self.T_HALO = self.G0 * 4 - 1
        self.WT0 = self.T_HALO - 4
        self.NWT = self.NT - self.WT0
        self.NOT = self.NT - self.T_HALO
        self.debug = debug


def small_layout(cfg):
    lay = {}
    off = 0
    for name, n in (("g_in", 16), ("b_in", 16), ("g1", 16), ("b1", 16), ("lcw", 32), ("lcb", 8),
                    ("ba", 8), ("bx", 8), ("lam", 8), ("fcw", 264), ("fcb", 88),
                    ("w2k", 128), ("w2v", 128), ("pek", 32), ("pev", 32), ("hm", 2),
                    ("gflag", cfg.NG), ("fbias", cfg.NBLK)):
        lay[name] = (off, n)
        off += n
    return lay, off


W_IN_SPLIT = dict(lx=0, ly=1024, q=2048, kc=2560, vc=2688, ks=2816, vs=2944, kw=3072, vw=3200,
                  gate=3328, mq=3352)


def win_chunk_cols():
    cols = []
    for c in range(8):
        cols.append(np.arange(c * 128, (c + 1) * 128))
    for c in range(8):
        cols.append(1024 + np.arange(c * 128, (c + 1) * 128))
    for c in range(4):
        cols.append(np.concatenate([2048 + c * 64 + np.arange(64), 2048 + (4 + c) * 64 + np.arange(64)]))
    for k in range(6):
        cols.append(2560 + k * 128 + np.arange(128))
    for h in range(4):
        cols.append(3352 + h * 128 + np.arange(128))
    return cols


def prep_shared(cfg, inp):
    f = np.float32
    sh = {}
    w_in = np.asarray(inp["w_in"][0], f)
    cols = win_chunk_cols()
    wt = np.stack([w_in[:, c] for c in cols])
    sh["w_in_t"] = np.ascontiguousarray(wt.reshape(30, KC, 128, 128).transpose(0, 2, 1, 3))
    sh["w_gate"] = np.ascontiguousarray(w_in[:, 3328:3352].reshape(KC, 128, 24).transpose(1, 0, 2))
    w_up = np.asarray(inp["ffn_w_up"][0], f)
    sh["w_up_t"] = np.ascontiguousarray(w_up.reshape(KC, 128, 88, 128).transpose(2, 1, 0, 3))
    w_out = np.asarray(inp["w_out"][0], f)
    sh["w_out_t"] = np.ascontiguousarray(w_out.reshape(4, 4, 128, 4, 512).transpose(3, 0, 2, 1, 4))
    w_dn = np.asarray(inp["ffn_w_down"][0], f)
    sh["w_dn_t"] = np.ascontiguousarray(w_dn.reshape(11, 4, 128, 4, 512).transpose(3, 0, 2, 1, 4))
    wm = np.asarray(inp["w_mem_kv"][0], f)
    sh["w_mk_t"] = np.ascontiguousarray(wm[:, 0:512].reshape(KC, 128, 4, 128).transpose(2, 1, 0, 3))
    sh["w_mv_t"] = np.ascontiguousarray(wm[:, 512:1024].reshape(4, 4, 128, 512).transpose(0, 2, 1, 3))
    wa = np.asarray(inp["lru_wa"][0], f)
    wx = np.asarray(inp["lru_wx"][0], f)
    sh["w_ax"] = np.ascontiguousarray(np.stack([wa.transpose(1, 0, 2), wx.transpose(1, 0, 2)], axis=1))
    for nm, key in (("w1k", "cmp_w1_k"), ("w1v", "cmp_w1_v")):
        w1 = np.asarray(inp[key][0], f)
        t = w1.transpose(1, 0, 2)
        bd = np.zeros((128, 32, 128), f)
        bd[0:64, :, 0:64] = t
        bd[64:128, :, 64:128] = t
        sh[nm] = np.ascontiguousarray(bd.reshape(128, 2, 16, 128).transpose(1, 0, 2, 3))
    lay, ns = small_layout(cfg)
    sp = np.zeros((128, ns), f)

    def put(name, arr):
        o, n = lay[name]
        assert arr.shape == (128, n), (name, arr.shape, n)
        sp[:, o:o + n] = arr

    col = lambda v: np.asarray(v, f).reshape(-1, 128).T
    put("g_in", col(inp["ln_in_g"]))
    put("b_in", col(inp["ln_in_b"]))
    put("g1", col(inp["ln1_g"][0]))
    put("b1", col(inp["ln1_b"][0]))
    cw = np.asarray(inp["lru_conv_w"][0], f)
    put("lcw", cw.reshape(4, 8, 128).transpose(2, 1, 0).reshape(128, 32))
    put("lcb", col(inp["lru_conv_b"][0]))
    put("ba", np.asarray(inp["lru_ba"][0], f).T)
    put("bx", np.asarray(inp["lru_bx"][0], f).T)
    put("lam", col(inp["lru_lam"][0]))
    fw = np.asarray(inp["ffn_conv_w"][0], f)
    put("fcw", fw.reshape(3, 88, 128).transpose(2, 1, 0).reshape(128, 264))
    put("fcb", col(inp["ffn_conv_b"][0]))
    w2k = np.asarray(inp["cmp_w2_k"][0], f)
    w2v = np.asarray(inp["cmp_w2_v"][0], f)
    def bdiag(w):
        o_ = np.zeros((128, 128), f)
        o_[0:64, 0:64] = w
        o_[64:128, 64:128] = w
        return o_
    put("w2k", bdiag(w2k))
    put("w2v", bdiag(w2v))
    hm = np.zeros((128, 2), f)
    hm[0:64, 0] = 1.0
    hm[64:128, 1] = 1.0
    put("hm", hm)
    pk = np.asarray(inp["cmp_pe_k"][0], f).T
    pv = np.asarray(inp["cmp_pe_v"][0], f).T
    put("pek", np.concatenate([pk, pk], axis=0))
    put("pev", np.concatenate([pv, pv], axis=0))
    sh["_small"] = sp
    gb = np.stack([np.stack([np.asarray(inp["ln_in_g"], f), np.asarray(inp["ln_in_b"], f)]),
                   np.stack([np.asarray(inp["ln1_g"][0], f), np.asarray(inp["ln1_b"][0], f)]),
                   np.stack([np.asarray(inp["ln2_g"][0], f), np.asarray(inp["ln2_b"][0], f)])])
    sh["gb"] = np.ascontiguousarray(gb.reshape(3, 1, 2 * D))
    ii = np.arange(128)
    sh["ident_f"] = np.eye(128, dtype=f)
    tri = (ii[:, None] <= ii[None, :]).astype(f)
    sh["masks_b"] = np.ascontiguousarray(
        np.stack([np.eye(128, dtype=f), tri, 1.0 - tri, np.ones((128, 128), f)], axis=1)).astype(NPBF)
    negc = np.stack([np.tile(-30000.0 * (1.0 - tri), (1, 4)), np.tile(-30000.0 * tri, (1, 4))], axis=1)
    sh["negc"] = np.ascontiguousarray(negc).astype(NPBF)
    eec = np.zeros((128, 32, 128), f)
    for r in range(2):
        for j in range(32):
            for hf in range(2):
                eec[64 * r + 2 * j + hf, j, hf * 64:(hf + 1) * 64] = 1.0
    sh["eec"] = eec.astype(NPBF)
    n = np.arange(cfg.NCT * 128)
    blk = np.arange(cfg.NBLK)
    cover = ((16 * n[:, None] <= 64 * blk[None, :] + 63) & (16 * n[:, None] + 31 >= 64 * blk[None, :])).astype(f)
    cover[n >= cfg.NCMP - 1] = 0.0
    c1 = np.concatenate([cover, np.ones((cfg.NCT * 128, 1), f)], axis=1)
    sh["cover1"] = np.ascontiguousarray(c1.reshape(cfg.NCT, 128, cfg.NBLK + 1).transpose(1, 0, 2)).astype(NPBF)
    return sh


def prep_core(cfg, inp, sh, c):
    f = np.float32
    b, j = c // 4, c % 4
    off = (3 - j) * cfg.CH
    m = dict(sh)
    xv = np.zeros((cfg.SEQ, D), f)
    xv[off:] = np.asarray(inp["x"][b, 0:(j + 1) * cfg.CH], f)
    m["xv"] = xv
    m["memx"] = np.ascontiguousarray(np.asarray(inp["mem"][b], f))
    lay, ns = small_layout(cfg)
    sp = sh["_small"].copy()
    o, n = lay["gflag"]
    sp[:, o:o + n] = (np.arange(cfg.NG) * GT >= off).astype(f)[None, :]
    o, n = lay["fbias"]
    fb = np.zeros(cfg.NBLK, f)
    fb[off // 64] = 1e9
    sp[:, o:o + n] = fb[None, :]
    m["smallp"] = sp
    del m["_small"]
    cm = np.zeros((cfg.NOT, 128, cfg.NCT, 128), f)
    for ti in range(cfg.NOT):
        vq = cfg.T_HALO + ti
        t = 128 * vq + np.arange(128)
        for nt in range(cfg.NCT):
            ng = nt * 128 + np.arange(128)
            ok = (16 * ng[:, None] + 31 <= t[None, :]) & (16 * ng[:, None] >= off) & (ng[:, None] < cfg.NCMP - 1)
            cm[ti, :, nt, :] = ok
    m["cmask"] = cm.astype(NPBF)
    return m


P_WIN, P_WOUT, P_WUP, P_WDN, P_W1K, P_W1V, NPIECE = 0, 30, 46, 134, 178, 180, 182
NB_POOL = 5
LOOKAHEAD = 4


class DryTracker(Tracker):
    def __init__(self):
        self.n_wait = 0
        self.n_ins = 0

    def op(self, en, fn, reads=(), writes=()):
        return None

    def dma(self, qn, out, in_, reads=(), writes=(), slots="main", **kw):
        return None

    def wait_events(self, en, events):
        pass

    def barrier(self, names=()):
        pass


class Prog:
    def __init__(self, cfg):
        self.cfg = cfg
        self.nc = bass.Bass("TRN2", target_bir_lowering=False)
        self.alloc()

    def sb(self, name, shape, dt):
        return self.nc.alloc_sbuf_tensor(name, list(shape), dt)

    def din(self, name, shape, dt):
        return self.nc.dram_tensor(name, list(shape), dt, kind="ExternalInput").ap()

    def alloc(self):
        cfg, nc = self.cfg, self.nc
        lay, ns = small_layout(cfg)
        self.lay = lay
        self.xv = self.din("xv", [cfg.SEQ, D], F32)
        self.memx = self.din("memx", [256, D], F32)
        self.w_in_t = self.din("w_in_t", [30, 128, KC, 128], F32)
        self.w_gate = self.din("w_gate", [128, KC, 24], F32)
        self.w_up_t = self.din("w_up_t", [88, 128, KC, 128], F32)
        self.w_out_t = self.din("w_out_t", [4, 4, 128, 4, 512], F32)
        self.w_dn_t = self.din("w_dn_t", [4, 11, 128, 4, 512], F32)
        self.w_mk_t = self.din("w_mk_t", [4, 128, KC, 128], F32)
        self.w_mv_t = self.din("w_mv_t", [4, 128, 4, 512], F32)
        self.w_ax = self.din("w_ax", [128, 2, 8, 128], F32)
        self.w1k = self.din("w1k", [2, 128, 16, 128], F32)
        self.w1v = self.din("w1v", [2, 128, 16, 128], F32)
        self.smallp = self.din("smallp", [128, ns], F32)
        self.gb = self.din("gb", [3, 1, 2 * D], F32)
        self.ident_f_d = self.din("ident_f", [128, 128], F32)
        self.masks_b_d = self.din("masks_b", [128, 4, 128], BF16)
        self.eec_d = self.din("eec", [128, 32, 128], BF16)
        self.negc_d = self.din("negc", [128, 2, 512], BF16)
        self.cover1_d = self.din("cover1", [128, cfg.NCT, cfg.NBLK + 1], BF16)
        self.cmask_d = self.din("cmask", [cfg.NOT, 128, cfg.NCT, 128], BF16)
        self.y = nc.dram_tensor("y", [cfg.CH, D], F32, kind="ExternalOutput").ap()
        self.wsc = nc.dram_tensor("wsc", [NPIECE, 128, 2048], BF16, kind="Internal").ap()
        if cfg.debug:
            self.d_mixed = nc.dram_tensor("d_mixed", [cfg.OWN_G + 1, 128, KC, 512], BF16, kind="ExternalOutput").ap()
            self.d_h1 = nc.dram_tensor("d_h1", [cfg.OWN_G + 1, 128, 4, D], F32, kind="ExternalOutput").ap()
            self.d_deriv = nc.dram_tensor("d_deriv", [128, 64], F32, kind="ExternalOutput").ap()
            self.d_hlru = nc.dram_tensor("d_hlru", [128, 8, 512], BF16, kind="ExternalOutput").ap()
            self.d_hT = nc.dram_tensor("d_hT", [128, KC, 512], BF16, kind="ExternalOutput").ap()
        self.SM = self.sb("SM", [128, ns], F32)
        self.ksT = self.sb("ksT", [128, cfg.SEQ], BF16)
        self.kwT = self.sb("kwT", [128, cfg.NWT * 128], BF16)
        self.V1s = self.sb("V1s", [128, cfg.NT, 2, 65], BF16)
        self.V1w = self.sb("V1w", [128, cfg.NWT, 2, 65], BF16)
        self.eec = self.sb("eec_s", [128, 32, 128], BF16)
        self.kcmpT = self.sb("kcmpT", [128, cfg.NCT * 128], BF16)
        self.vcmp1 = self.sb("vcmp1", [128, cfg.NCT, 2, 65], BF16)
        self.cover1 = self.sb("cover1_s", [128, cfg.NCT, cfg.NBLK + 1], BF16)
        self.kmemT = self.sb("kmemT", [128, 4, 256], BF16)
        self.vmem = self.sb("vmem", [128, 2, 4, 128], BF16)
        self.ones_bf = self.sb("ones_bf", [128, 128], BF16)
        self.masks = self.sb("masks_s", [128, 4, 128], BF16)
        self.ident_f = self.sb("ident_fs", [128, 128], F32)
        self.wg_bf = self.sb("wg_bf", [128, KC, 24], BF16)
        self.wax_sb = self.sb("wax_sb", [128, 2, 8, 128], BF16)
        self.smb = self.sb("smb", [128, 320], BF16)
        self.negc = self.sb("negc_s", [128, 2, 512], BF16)
        self.negm2 = self.sb("negm2", [128, 2, 512], BF16)
        self.vcmpT = self.sb("vcmpT", [128, cfg.NCT * 128], BF16)
        self.deriv = self.sb("deriv", [128, 64], F32)
        self.state = self.sb("lru_state", [128, 8], F32)
        self.xtail = self.sb("xtail", [128, 8, 3], F32)
        self.ftail = self.sb("ftail", [128, 88, 2], F32)
        self.kc_loc = self.sb("kc_loc", [128, 528], BF16)
        self.vc_loc = self.sb("vc_loc", [128, 528], BF16)
        self.hflag = self.sb("hflag", [128, cfg.NG], F32)
        self.xt = self.sb("xt", [128, 4, D], F32)
        self.bufM = self.sb("bufM", [128, KC, 512], BF16)
        self.hT = self.sb("hT", [128, KC, 512], BF16)
        self.bufQ = self.sb("bufQ", [128, 8, 512], BF16)
        self.bufA = self.sb("bufA", [128, 16, 512], BF16)
        self.wpool = self.sb("wpool", [128, NB_POOL, 2048], BF16)
        self.g_sb = self.sb("g_sb", [128, 4, 24], F32)
        self.stats = self.sb("stats", [128, 4, 4, 6], F32)
        self.mv = self.sb("mv", [128, 4, 2], F32)
        self.lnt = self.sb("lnt", [128, 3, 4], F32)
        self.bufS = self.sb("bufS", [128, 5248], F32)
        self.ps = [nc.alloc_psum_tensor("ps%d" % i, [128, 512], F32) for i in range(8)]
        print("sbuf bytes remaining/partition:", nc.sbuf_bytes_remaining)

    def mk_regions(self):
        cfg = self.cfg
        R = lambda n, excl=False: Region(n, excl)
        self.R_sm = R("sm")
        self.R_const = R("const")
        self.R_ks = [R("ks%d" % i) for i in range(cfg.NT)]
        self.R_kw = [R("kw%d" % i) for i in range(cfg.NWT)]
        self.R_vs = [R("vs%d" % i) for i in range(cfg.NT)]
        self.R_vw = [R("vw%d" % i) for i in range(cfg.NWT)]
        self.R_kcmp = R("kcmp")
        self.R_vcmp = R("vcmp")
        self.R_kmem = R("kmem")
        self.R_vmem = R("vmem")
        self.R_wg = R("wg")
        self.R_negm2 = R("negm2")
        self.R_wax = R("wax")
        self.R_smb = R("smb")
        self.R_deriv = R("deriv")
        self.R_state = [R("state%d" % i) for i in range(8)]
        self.R_xtail = [R("xtail%d" % i) for i in range(8)]
        self.R_ftail = [R("ftail%d" % i) for i in range(88)]
        self.R_kcl = R("kc_loc")
        self.R_vcl = R("vc_loc")
        self.R_xt = [R("xt%d" % i) for i in range(4)]
        self.R_M = [R("M%d" % i) for i in range(KC)]
        self.R_hT = [R("hT%d" % i) for i in range(KC)]
        self.R_Q = [R("Q%d" % i) for i in range(8)]
        self.R_A = [R("A%d" % i) for i in range(16)]
        self.R_wp = [R("wp%d" % i) for i in range(NB_POOL)]
        self.R_wsc = [R("wsc%d" % i) for i in range(NPIECE)]
        self.R_g = R("g_sb")
        self.R_stats = [R("stats%d" % i) for i in range(4)]
        self.R_mv = R("mv")
        self.R_lnt = R("lnt")
        self.R_S = {}
        self.R_ps = [R("ps%d" % i, True) for i in range(8)]
        self.R_y = R("y")
        self.R_dbg = R("dbg")

    def RS(self, name):
        if name not in self.R_S:
            self.R_S[name] = Region("S_" + name)
        return self.R_S[name]

    def sm(self, name, i=None, n=1):
        o, cnt = self.lay[name]
        if i is None:
            return self.SM[:, o:o + cnt]
        return self.SM[:, o + i:o + i + n]

    def piece_src(self, pid):
        if pid < P_WOUT:
            return self.w_in_t[pid].rearrange("p a b -> p (a b)")
        if pid < P_WUP:
            k = pid - P_WOUT
            return self.w_out_t[k // 4, k % 4].rearrange("p a b -> p (a b)")
        if pid < P_WDN:
            return self.w_up_t[pid - P_WUP].rearrange("p a b -> p (a b)")
        if pid < P_W1K:
            k = pid - P_WDN
            return self.w_dn_t[k // 11, k % 11].rearrange("p a b -> p (a b)")
        if pid < P_W1V:
            return self.w1k[pid - P_W1K].rearrange("p a b -> p (a b)")
        return self.w1v[pid - P_W1V].rearrange("p a b -> p (a b)")

    def stage_cast(self, src_ap, dst_ap, dst_regions, nelem=2048, eng="pool"):
        self.tr.dma("pool", dst_ap, src_ap, writes=dst_regions, slots="cast")

    def issue_casts(self, n):
        if self.dry:
            return
        while n > 0 and self.cast_ptr < len(self.cast_order):
            pid = self.cast_order[self.cast_ptr]
            self.cast_ptr += 1
            self.tr.dma("pool", self.wsc[pid], self.piece_src(pid), writes=[self.R_wsc[pid]], slots="cast")
            self.cast_done.add(pid)
            n -= 1

    def _issue_fetch(self, pid):
        tr = self.tr
        b = self.pool_rr
        self.pool_rr = (self.pool_rr + 1) % NB_POOL
        buf = self.wpool[:, b, :]
        while pid not in self.cast_done:
            self.issue_casts(1)
        tr.dma("sp", buf, self.wsc[pid], reads=[self.R_wsc[pid]], writes=[self.R_wp[b]])
        return b

    def fetch(self, pid):
        if self.dry:
            self.order.append(pid)
            return self.wpool[:, 0, :], self.R_wp[0]
        assert self.order[self.optr] == pid, (self.optr, self.order[self.optr], pid)
        while self.issued < min(len(self.order), self.optr + LOOKAHEAD + 1):
            self.inflight[self.issued] = self._issue_fetch(self.order[self.issued])
            self.issued += 1
        b = self.inflight.pop(self.optr)
        self.optr += 1
        return self.wpool[:, b, :], self.R_wp[b]

    def bank_mm(self):
        i = self.mm_rr % self.mm_nb
        self.mm_rr = (i + 1) % self.mm_nb
        return self.ps[i], self.R_ps[i]

    def bfv(self, bank):
        return bank[:, :].bitcast(BF16)

    def emit(self, dry):
        self.dry = dry
        self.mk_regions()
        if dry:
            self.tr = DryTracker()
            self.order = []
        else:
            self.tr = Tracker(self.nc)
            self.optr = 0
            self.issued = 0
            self.inflight = {}
        self.cast_done = set()
        self.cast_ptr = 0
        if not dry:
            seen = set()
            self.cast_order = [p for p in self.order if not (p in seen or seen.add(p))]
        self.pool_rr = 0
        self.stage_rr = 0
        self.mm_rr = 0
        self.mm_nb = 4
        self.x_loaded = set()
        self.a1_done = set()
        cfg = self.cfg
        self.setup()
        for G in range(cfg.NG):
            self.group(G)
        if not dry:
            tr = self.tr
            tr.wait_events("sp", [s[3] for s in tr.dma_sems])
            tr.wait_events("pool", [s[3] for s in tr.dma_sems])
            print("instructions:", tr.n_ins, "waits:", tr.n_wait)

    def setup(self):
        cfg, tr = self.cfg, self.tr
        SM = self.SM
        tr.dma("sp", SM[:, :], self.smallp[:, :], writes=[self.R_sm])
        tr.dma("sp", self.ident_f[:, :], self.ident_f_d[:, :], writes=[self.R_const])
        tr.dma("sp", self.masks[:, :, :], self.masks_b_d[:, :, :], writes=[self.R_const])
        tr.dma("sp", self.eec[:, :, :], self.eec_d[:, :, :], writes=[self.R_const])
        tr.dma("sp", self.negc[:, :, :], self.negc_d[:, :, :], writes=[self.R_const])
        tr.dma("sp", self.cover1[:, :, :], self.cover1_d[:, :, :], writes=[self.R_const])
        dv = self.deriv
        RD = self.R_deriv
        x_ = dv[:, 32:40]
        xs = dv[:, 40:48]
        t_ = dv[:, 48:56]
        yl = dv[:, 56:64]
        tr.op("act", lambda e: e.activation(out=x_, in_=self.sm("lam"), func=AF.Exp, scale=-1.0), reads=[self.R_sm], writes=[RD])
        tr.op("act", lambda e: e.activation(out=yl, in_=x_, func=AF.Ln, bias=1.0), reads=[RD], writes=[RD])
        tr.op("dve", lambda e: e.tensor_scalar(out=xs, in0=x_, scalar1=0.05, scalar2=None, op0=ALU.min), reads=[RD], writes=[RD])
        tr.op("dve", lambda e: e.tensor_scalar(out=t_, in0=xs, scalar1=-0.25, scalar2=1.0 / 3.0, op0=ALU.mult, op1=ALU.add), reads=[RD], writes=[RD])
        tr.op("dve", lambda e: e.tensor_tensor(out=t_, in0=t_, in1=xs, op=ALU.mult), reads=[RD], writes=[RD])
        tr.op("dve", lambda e: e.tensor_scalar(out=t_, in0=t_, scalar1=-0.5, scalar2=None, op0=ALU.add), reads=[RD], writes=[RD])
        tr.op("dve", lambda e: e.tensor_tensor(out=t_, in0=t_, in1=xs, op=ALU.mult), reads=[RD], writes=[RD])
        tr.op("dve", lambda e: e.tensor_scalar(out=t_, in0=t_, scalar1=1.0, scalar2=None, op0=ALU.add), reads=[RD], writes=[RD])
        tr.op("dve", lambda e: e.tensor_tensor(out=t_, in0=t_, in1=xs, op=ALU.mult), reads=[RD], writes=[RD])
        tr.op("dve", lambda e: e.tensor_tensor(out=t_, in0=t_, in1=yl, op=ALU.subtract), reads=[RD], writes=[RD])
        tr.op("dve", lambda e: e.tensor_scalar(out=xs, in0=x_, scalar1=0.05, scalar2=None, op0=ALU.is_lt), reads=[RD], writes=[RD])
        tr.op("dve", lambda e: e.tensor_tensor(out=t_, in0=t_, in1=xs, op=ALU.mult), reads=[RD], writes=[RD])
        tr.op("dve", lambda e: e.tensor_tensor(out=t_, in0=t_, in1=yl, op=ALU.add), reads=[RD], writes=[RD])
        tr.op("dve", lambda e: e.tensor_scalar(out=dv[:, 0:8], in0=t_, scalar1=-4.0, scalar2=None, op0=ALU.mult), reads=[RD], writes=[RD])
        tr.op("dve", lambda e: e.tensor_scalar(out=dv[:, 8:16], in0=self.sm("ba"), scalar1=0.5, scalar2=None, op0=ALU.mult), reads=[self.R_sm, RD], writes=[RD])
        tr.op("dve", lambda e: e.tensor_scalar(out=dv[:, 16:24], in0=self.sm("bx"), scalar1=0.5, scalar2=None, op0=ALU.mult), reads=[self.R_sm, RD], writes=[RD])
        tr.op("dve", lambda e: e.tensor_scalar(out=self.hflag[:, :], in0=self.sm("gflag"), scalar1=0.5, scalar2=None, op0=ALU.mult), reads=[self.R_sm, RD], writes=[RD])
        tr.op("dve", lambda e: e.memset(dv[:, 26:27], 1.0), reads=[RD], writes=[RD])
        tr.op("dve", lambda e: e.memset(dv[:, 27:28], LN_EPS), reads=[RD], writes=[RD])
        self.one_ap = dv[:, 26:27]
        self.eps_ap = dv[:, 27:28]
        self.phaseA1(0)
        o = self.lay["w2k"][0]
        tr.op("pool", lambda e: e.tensor_copy(out=self.smb[:, :], in_=SM[:, o:o + 320]), reads=[self.R_sm], writes=[self.R_smb])
        tr.op("pool", lambda e: e.memset(self.ones_bf[:, :], 1.0), writes=[self.R_const])
        for c in range(8):
            tr.op("pool", lambda e: e.memset(self.state[:, c:c + 1], 0.0), writes=[self.R_state[c]])
            tr.op("pool", lambda e: e.memset(self.xtail[:, c, :], 0.0), writes=[self.R_xtail[c]])
        for c in range(88):
            tr.op("pool", lambda e: e.memset(self.ftail[:, c, :], 0.0), writes=[self.R_ftail[c]])
        tr.op("pool", lambda e: e.memset(self.kc_loc[:, :], 0.0), writes=[self.R_kcl])
        tr.op("pool", lambda e: e.memset(self.vc_loc[:, :], 0.0), writes=[self.R_vcl])
        tr.op("pool", lambda e: e.memset(self.vcmp1[:, :, :, :], 0.0), writes=[self.R_vcmp])
        tr.op("pool", lambda e: e.memset(self.vcmp1[:, :, :, 64:65], 1.0), writes=[self.R_vcmp])
        tr.op("pool", lambda e: e.memset(self.negm2[:, :, :], 0.0), writes=[self.R_negm2])
        tr.op("pool", lambda e: e.memset(self.kcmpT[:, :], 0.0), writes=[self.R_kcmp])
        tr.op("pool", lambda e: e.memset(self.vcmpT[:, :], 0.0), writes=[self.R_vcmp])
        self.issue_casts(20)
        self.stage_cast(self.w_gate.rearrange("p a b -> p (a b)"), self.wg_bf[:, :, :].rearrange("p a b -> p (a b)"),
                        [self.R_wg], nelem=KC * 24)
        self.stage_cast(self.w_ax.rearrange("p a b c -> p (a b c)"), self.wax_sb[:, :, :, :].rearrange("p a b c -> p (a b c)"),
                        [self.R_wax])
        self.mem_kv()
        for kind, pid, pcol, dcol in (("k", P_W1K, 256, 24), ("v", P_W1V, 288, 25)):
            bank, rbk = self.bank_mm()
            pe2 = self.bufS[:, 64 * (dcol - 24):64 * (dcol - 24) + 64].bitcast(BF16)
            R_pe2 = self.RS("pe2%d" % dcol)
            tr.op("pool", lambda e: e.tensor_copy(out=pe2[:, 0:64].rearrange("p (l t) -> p l t", t=2),
                                                  in_=self.smb[:, pcol:pcol + 32].unsqueeze(2).broadcast_to([128, 32, 2])),
                  reads=[self.R_smb], writes=[R_pe2])
            for half in range(2):
                buf, rb = self.fetch(pid + half)
                w1 = buf.rearrange("p (l e) -> p l e", e=128)
                for li in range(16):
                    l = half * 16 + li
                    tr.op("pe", lambda e: e.matmul(bank[:, 0:2], lhsT=w1[:, li, :], rhs=pe2[:, 2 * l:2 * l + 2],
                                                   start=(l == 0), stop=(l == 31)),
                          reads=[rb, R_pe2], writes=[rbk])
            tr.op("dve", lambda e: e.tensor_copy(out=dv[:, dcol:dcol + 1], in_=bank[:, 0:1]), reads=[rbk, RD], writes=[RD])
        tr.barrier()

    def mem_kv(self):
        cfg, tr = self.cfg, self.tr
        memT = self.hT
        mbv = self.bufQ[:, :, :].rearrange("p a b -> p (a b)").rearrange("p (t d) -> p t d", d=D)
        for mt in range(2):
            mf = self.bufS[:, 512 + mt * D:512 + (mt + 1) * D]
            R_mf = self.RS("memf%d" % mt)
            tr.dma("sp", mf, self.memx[mt * 128:(mt + 1) * 128, :], writes=[R_mf])
            tr.op("pool", lambda e: e.tensor_copy(out=mbv[:, mt, :], in_=mf), reads=[R_mf], writes=self.R_Q[4 * mt:4 * mt + 4])
        for kc in range(KC):
            bank, rbk = self.bank_mm()
            bb = self.bfv(bank)
            for mt in range(2):
                tr.op("pe", lambda e: e.transpose(out=bb[:, mt * 128:(mt + 1) * 128], in_=mbv[:, mt, kc * 128:(kc + 1) * 128],
                                                  identity=self.masks[:, 0, :]),
                      reads=self.R_Q[4 * mt:4 * mt + 4] + [self.R_const], writes=[rbk])
            tr.op("dve", lambda e: e.tensor_copy(out=memT[:, kc, 0:256], in_=bb[:, 0:256]), reads=[rbk], writes=[self.R_hT[kc]])
        for h in range(4):
            wb = self.bufA[:, 0:4, :].rearrange("p a b -> p (a b)")
            self.stage_cast(self.w_mk_t[h].rearrange("p a b -> p (a b)"), wb, self.R_A[0:4])
            wv = wb.rearrange("p (a b) -> p a b", b=128)
            bank, rbk = self.bank_mm()
            for kc in range(KC):
                tr.op("pe", lambda e: e.matmul(bank[:, 0:256], lhsT=wv[:, kc, :], rhs=memT[:, kc, 0:256], start=(kc == 0), stop=(kc == KC - 1)),
                      reads=self.R_A[0:4] + [self.R_hT[kc]], writes=[rbk])
            tr.op("act", lambda e: e.copy(out=self.kmemT[:, h, :], in_=bank[:, 0:256]), reads=[rbk], writes=[self.R_kmem])
        banks = [self.bank_mm() for _ in range(2)]
        for kq in range(4):
            wb = self.bufA[:, 4:8, :].rearrange("p a b -> p (a b)")
            self.stage_cast(self.w_mv_t[kq].rearrange("p a b -> p (a b)"), wb, self.R_A[4:8])
            wv = wb.rearrange("p (a b) -> p a b", b=512)
            for mt in range(2):
                bank, rbk = banks[mt]
                for kci in range(4):
                    kc = kq * 4 + kci
                    tr.op("pe", lambda e: e.matmul(bank[:, 0:512], lhsT=memT[:, kc, mt * 128:(mt + 1) * 128], rhs=wv[:, kci, :],
                                                   start=(kc == 0), stop=(kc == KC - 1)),
                          reads=self.R_A[4:8] + [self.R_hT[kc]], writes=[rbk])
        for mt in range(2):
            bank, rbk = banks[mt]
            tr.op("act", lambda e: e.copy(out=self.vmem[:, mt, :, :].rearrange("p h d -> p (h d)"), in_=bank[:, 0:512]),
                  reads=[rbk], writes=[self.R_vmem])

    def ln_stats(self, tiles):
        tr = self.tr
        for t in tiles:
            for k in range(4):
                tr.op("dve", lambda e: e.bn_stats(out=self.stats[:, t, k, :], in_=self.xt[:, t, k * 512:(k + 1) * 512]),
                      reads=[self.R_xt[t]], writes=[self.R_stats[t]])
            tr.op("dve", lambda e: e.bn_aggr(out=self.mv[:, t, :], in_=self.stats[:, t, :, :].rearrange("p a b -> p (a b)")),
                  reads=[self.R_stats[t]], writes=[self.R_mv])
        t0, t1 = tiles[0], tiles[-1] + 1
        tr.op("act", lambda e: e.activation(out=self.lnt[:, 0, t0:t1], in_=self.mv[:, t0:t1, 1], func=AF.Sqrt, bias=self.eps_ap, scale=1.0),
              reads=[self.R_mv, self.R_lnt, self.R_deriv], writes=[self.R_lnt])
        tr.op("dve", lambda e: e.reciprocal(out=self.lnt[:, 1, t0:t1], in_=self.lnt[:, 0, t0:t1]), reads=[self.R_lnt], writes=[self.R_lnt])
        tr.op("dve", lambda e: e.scalar_tensor_tensor(out=self.lnt[:, 2, t0:t1], in0=self.mv[:, t0:t1, 0], scalar=-1.0,
                                                       in1=self.lnt[:, 1, t0:t1], op0=ALU.mult, op1=ALU.mult),
              reads=[self.R_mv, self.R_lnt], writes=[self.R_lnt])

    def load_gb(self, which):
        gbuf = self.bufA[:, :, :].rearrange("p a b -> p (a b)").bitcast(F32)
        self.tr.dma("sp", gbuf, self.gb[which, 0:1, :].partition_broadcast(128), writes=self.R_A[0:16])
        return gbuf

    def ln_apply(self, t, gbuf, g_name, b_name, want_tok, want_T=True):
        tr = self.tr
        xnv = self.bufM[:, :, :].rearrange("p a b -> p (a b)").rearrange("p (t d) -> p t d", d=D)
        if want_T:
            tr.op("act", lambda e: e.activation(out=xnv[:, t, :], in_=self.xt[:, t, :], func=AF.Identity,
                                                scale=self.lnt[:, 1, t:t + 1], bias=self.lnt[:, 2, t:t + 1]),
                  reads=[self.R_xt[t], self.R_lnt], writes=self.R_M[4 * t:4 * t + 4])
        if want_tok:
            tr.op("dve", lambda e: e.scalar_tensor_tensor(out=self.xt[:, t, :], in0=self.xt[:, t, :], scalar=self.mv[:, t, 0:1], in1=gbuf[:, 0:D],
                                                           op0=ALU.subtract, op1=ALU.mult),
                  reads=[self.R_xt[t], self.R_mv] + self.R_A[0:8], writes=[self.R_xt[t]])
            tr.op("dve", lambda e: e.scalar_tensor_tensor(out=self.xt[:, t, :], in0=self.xt[:, t, :], scalar=self.lnt[:, 1, t:t + 1], in1=gbuf[:, D:2 * D],
                                                           op0=ALU.mult, op1=ALU.add),
                  reads=[self.R_xt[t], self.R_lnt] + self.R_A[8:16], writes=[self.R_xt[t]])

    def ln_transpose(self, tiles, g_name, b_name):
        tr = self.tr
        xnv = self.bufM[:, :, :].rearrange("p a b -> p (a b)").rearrange("p (t d) -> p t d", d=D)
        c0, c1 = tiles[0] * 128, (tiles[-1] + 1) * 128
        for kc in range(KC):
            bank, rbk = self.bank_mm()
            bb = self.bfv(bank)
            for t in tiles:
                tr.op("pe", lambda e: e.transpose(out=bb[:, t * 128:(t + 1) * 128], in_=xnv[:, t, kc * 128:(kc + 1) * 128],
                                                  identity=self.masks[:, 0, :]),
                      reads=self.R_M[4 * t:4 * t + 4] + [self.R_const], writes=[rbk])
            if kc % 2 == 0:
                tr.op("act", lambda e: e.activation(out=self.hT[:, kc, c0:c1], in_=bb[:, c0:c1], func=AF.Identity,
                                                    scale=self.sm(g_name, kc), bias=self.sm(b_name, kc)),
                      reads=[rbk, self.R_sm], writes=[self.R_hT[kc]])
            else:
                tr.op("dve", lambda e: e.tensor_scalar(out=self.hT[:, kc, c0:c1], in0=bb[:, c0:c1], scalar1=self.sm(g_name, kc),
                                                        scalar2=self.sm(b_name, kc), op0=ALU.mult, op1=ALU.add),
                      reads=[rbk, self.R_sm], writes=[self.R_hT[kc]])

    def proj(self, pid, c0, c1):
        tr = self.tr
        buf, rb = self.fetch(pid)
        wv = buf.rearrange("p (a b) -> p a b", b=128)
        bank, rbk = self.bank_mm()
        n = c1 - c0
        for kc in range(KC):
            tr.op("pe", lambda e: e.matmul(bank[:, 0:n], lhsT=wv[:, kc, :], rhs=self.hT[:, kc, c0:c1], start=(kc == 0), stop=(kc == KC - 1)),
                  reads=[rb, self.R_hT[kc]], writes=[rbk])
        return bank, rbk

    def phaseA1(self, G):
        tr = self.tr
        for t in range(4):
            if (G, t) not in self.x_loaded:
                tok0 = G * GT + t * 128
                tr.dma("sp", self.xt[:, t, :], self.xv[tok0:tok0 + 128, :], writes=[self.R_xt[t]])
                self.x_loaded.add((G, t))
        self.ln_stats([0, 1, 2, 3])
        for t in range(4):
            self.ln_apply(t, None, "g_in", "b_in", want_tok=False)
        self.a1_done.add(G)

    def group(self, G):
        cfg, tr = self.cfg, self.tr
        own = G >= cfg.G0
        halo = G == cfg.G0 - 1
        TR = (0, 4) if own else ((3, 4) if halo else None)
        if not self.dry:
            self.issue_casts((len(self.cast_order) + cfg.G0 - 2) // max(1, cfg.G0 - 1))
        self.mm_nb = 8
        if G not in self.a1_done:
            self.phaseA1(G)
        self.ln_transpose([0, 1, 2, 3], "g_in", "b_in")
        if G + 1 < cfg.NG:
            free_t = [0, 1, 2, 3] if TR is None else ([0, 1, 2] if TR == (3, 4) else [])
            for t in free_t:
                tok0 = (G + 1) * GT + t * 128
                tr.dma("sp", self.xt[:, t, :], self.xv[tok0:tok0 + 128, :], writes=[self.R_xt[t]])
                self.x_loaded.add((G + 1, t))
        if cfg.debug and G == cfg.G0:
            tr.dma("pool", self.d_hT, self.hT[:, :, :], reads=self.R_hT, writes=[self.R_dbg])
        self.lru(G, TR)
        if cfg.debug and G == cfg.G0:
            tr.dma("pool", self.d_hlru, self.bufQ[:, :, :], reads=self.R_Q, writes=[self.R_dbg])
            tr.dma("pool", self.d_deriv, self.deriv[:, :], reads=[self.R_deriv], writes=[self.R_dbg])
        self.kv(G)
        self.compress(G)
        if TR is None:
            return
        self.mm_nb = 4
        c0, c1 = TR[0] * 128, TR[1] * 128
        qT = self.bufQ
        for c in range(4):
            bank, rbk = self.proj(P_WIN + 16 + c, c0, c1)
            tr.op("act", lambda e: e.activation(out=qT[:, c, c0:c1], in_=bank[:, 0:c1 - c0], func=AF.Copy, scale=0.125),
                  reads=[rbk], writes=[self.R_Q[c]])
        for h in range(4):
            bank, rbk = self.proj(P_WIN + 26 + h, c0, c1)
            tr.op("act", lambda e: e.activation(out=qT[:, 4 + h, c0:c1], in_=bank[:, 0:c1 - c0], func=AF.Copy, scale=128.0 ** -0.5),
                  reads=[rbk], writes=[self.R_Q[4 + h]])
        bank, rbk = self.bank_mm()
        for t in range(TR[0], TR[1]):
            for kc in range(KC):
                tr.op("pe", lambda e: e.matmul(bank[:, t * 24:(t + 1) * 24], lhsT=self.hT[:, kc, t * 128:(t + 1) * 128], rhs=self.wg_bf[:, kc, :],
                                               start=(kc == 0), stop=(kc == KC - 1)),
                      reads=[self.R_hT[kc], self.R_wg], writes=[rbk])
        gs = self.g_sb[:, :, :].rearrange("p a b -> p (a b)")
        tr.op("act", lambda e: e.activation(out=gs[:, TR[0] * 24:TR[1] * 24], in_=bank[:, TR[0] * 24:TR[1] * 24], func=AF.Tanh, scale=0.5),
              reads=[rbk], writes=[self.R_g])
        tr.op("dve", lambda e: e.tensor_scalar(out=gs[:, TR[0] * 24:TR[1] * 24], in0=gs[:, TR[0] * 24:TR[1] * 24], scalar1=0.5, scalar2=0.5,
                                                op0=ALU.mult, op1=ALU.add),
              reads=[self.R_g], writes=[self.R_g])
        tr.barrier()
        self.mm_nb = 3
        tr.op("dve", lambda e: e.memset(self.bufA[:, 6:8, :], 0.0), writes=self.R_A[6:8])
        units = [(t, g) for t in range(TR[0], TR[1]) for g in range(2)]
        U = [self.nsa_unit(G, t, g, k) for k, (t, g) in enumerate(units)]
        U[0]["h0"]()
        U[0]["h1"]()
        U[0]["h2"]()
        for k in range(len(units)):
            n = U[k]["n"]
            inj = {}
            if k >= 1:
                inj.setdefault(min(6, n - 1), []).append(U[k - 1]["EP"])
            if k + 1 < len(units):
                inj.setdefault(min(10, n - 1), []).append(U[k + 1]["h0"])
                inj.setdefault(min(16, n - 1), []).append(U[k + 1]["h1"])
                inj.setdefault(min(30, n + 1), []).append(U[k + 1]["h2"])
            U[k]["run"](inj)
        U[-1]["EP"]()
        self.mm_nb = 4
        self.mem_attn(c0, c1)
        if cfg.debug:
            gi = G - cfg.G0 + 1
            tr.dma("pool", self.d_mixed[gi], self.bufM[:, :, :], reads=self.R_M, writes=[self.R_dbg])
        tr.barrier()
        self.out_proj_ln1(G, TR)
        if cfg.debug:
            gi = G - cfg.G0 + 1
            tr.dma("pool", self.d_h1[gi], self.xt[:, :, :], reads=self.R_xt, writes=[self.R_dbg])
        self.ffn(G, TR)
        tr.barrier()

    def lru(self, G, TR):
        cfg, tr = self.cfg, self.tr
        need_h = TR is not None
        S = self.bufS
        dv = self.deriv
        xr = [S[:, b * 516:(b + 1) * 516] for b in range(2)]
        xc = [S[:, 1032 + b * 512:1032 + (b + 1) * 512] for b in range(2)]
        th = [S[:, 2056 + b * 512:2056 + (b + 1) * 512] for b in range(2)]
        hf = [S[:, 3080 + b * 512:3080 + (b + 1) * 512] for b in range(2)]
        xcb = [S[:, 4104 + b * 256:4104 + (b + 1) * 256].bitcast(BF16) for b in range(2)]
        gy = S[:, 4616:4872].bitcast(BF16)
        R_xr = [self.RS("xr%d" % b) for b in range(2)]
        R_xc = [self.RS("xc%d" % b) for b in range(2)]
        R_th = [self.RS("th%d" % b) for b in range(2)]
        R_hf = [self.RS("hf%d" % b) for b in range(2)]
        R_xcb = [self.RS("xcb%d" % b) for b in range(2)]
        R_gy = self.RS("gy")
        Af = self.bufA[:, :, :].rearrange("p a b -> p (a b)").bitcast(F32)
        A_a = [Af[:, s * 512:(s + 1) * 512] for s in range(2)]
        A_om = [Af[:, 1024 + s * 512:1024 + (s + 1) * 512] for s in range(2)]
        A_ix = [Af[:, 2048 + s * 512:2048 + (s + 1) * 512] for s in range(2)]
        RA_a = [self.R_A[2 * s:2 * s + 2] for s in range(2)]
        RA_om = [self.R_A[4 + 2 * s:4 + 2 * s + 2] for s in range(2)]
        RA_ix = [self.R_A[8 + 2 * s:8 + 2 * s + 2] for s in range(2)]
        wax, r_ax = self.wax_sb, self.R_wax
        gfl = self.sm("gflag", G)
        hfl = self.hflag[:, G:G + 1]
        lcw = lambda c, k: self.sm("lcw", c * 4 + k)
        w3g = S[:, 5160:5168]
        R_w3g = self.RS("w3g")
        o_l = self.lay["lcw"][0]
        tr.op("dve", lambda e: e.tensor_scalar(out=w3g, in0=self.SM[:, o_l + 3:o_l + 32:4], scalar1=gfl, scalar2=None, op0=ALU.mult),
              reads=[self.R_sm], writes=[R_w3g])
        def S1(bt):
            info = []
            for s in range(2):
                c = bt * 2 + s
                b = s
                bank, rbk = self.proj(P_WIN + c, 0, 512)
                info.append((c, b))
                tr.op("pool", lambda e: e.tensor_copy(out=xr[b][:, 0:3], in_=self.xtail[:, c, :]), reads=[self.R_xtail[c]], writes=[R_xr[b]])
                tr.op("act", lambda e: e.activation(out=xr[b][:, 3:515], in_=bank[:, 0:512], func=AF.Identity, scale=gfl),
                      reads=[rbk, self.R_sm], writes=[R_xr[b]])
                tr.op("pool", lambda e: e.tensor_copy(out=self.xtail[:, c, :], in_=xr[b][:, 512:515]), reads=[R_xr[b]], writes=[self.R_xtail[c]])
                tr.op("act", lambda e: e.activation(out=xc[b], in_=bank[:, 0:512], func=AF.Identity, scale=w3g[:, c:c + 1], bias=self.sm("lcb", c)),
                      reads=[rbk, self.R_sm, R_w3g], writes=[R_xc[b]])
            for (c, b) in info:
                for k in (2, 1, 0):
                    tr.op("dve", lambda e: e.scalar_tensor_tensor(out=xc[b], in0=xr[b][:, k:k + 512], scalar=lcw(c, k), in1=xc[b],
                                                                   op0=ALU.mult, op1=ALU.add),
                          reads=[R_xr[b], R_xc[b], self.R_sm], writes=[R_xc[b]])
                tr.op("act", lambda e: e.copy(out=xcb[b], in_=xc[b]), reads=[R_xc[b]], writes=[R_xcb[b]])

        def S2(bt):
            banks = []
            for s in range(2):
                c = bt * 2 + s
                b = s
                bank_r, rbr = self.bank_mm()
                tr.op("pe", lambda e: e.matmul(bank_r[:, 0:512], lhsT=wax[:, 0, c, :], rhs=xcb[b], start=True, stop=True),
                      reads=[r_ax, R_xcb[b]], writes=[rbr])
                bank_i, rbi = self.bank_mm()
                tr.op("pe", lambda e: e.matmul(bank_i[:, 0:512], lhsT=wax[:, 1, c, :], rhs=xcb[b], start=True, stop=True),
                      reads=[r_ax, R_xcb[b]], writes=[rbi])
                banks.append((bank_r, rbr, bank_i, rbi))
            for s in range(2):
                c = bt * 2 + s
                b = s
                bank_r, rbr, bank_i, rbi = banks[s]
                tr.op("act", lambda e: e.activation(out=th[b], in_=bank_r[:, 0:512], func=AF.Tanh, scale=0.5, bias=dv[:, 8 + c:9 + c]),
                      reads=[rbr, self.R_deriv], writes=[R_th[b]])
                tr.op("act", lambda e: e.activation(out=A_a[s], in_=th[b], func=AF.Exp, scale=dv[:, c:c + 1], bias=dv[:, c:c + 1]),
                      reads=[R_th[b], self.R_deriv], writes=RA_a[s])
                tr.op("act", lambda e: e.activation(out=th[b], in_=bank_i[:, 0:512], func=AF.Tanh, scale=0.5, bias=dv[:, 16 + c:17 + c]),
                      reads=[rbi, self.R_deriv], writes=[R_th[b]])
                tr.op("dve", lambda e: e.scalar_tensor_tensor(out=A_ix[s], in0=th[b], scalar=1.0, in1=xc[b], op0=ALU.add, op1=ALU.mult),
                      reads=[R_th[b], R_xc[b]], writes=RA_ix[s])
                tr.op("act", lambda e: e.activation(out=A_om[s], in_=A_a[s], func=AF.Square), reads=RA_a[s], writes=RA_om[s])

        def S3(bt):
            tr.op("act", lambda e: e.activation(out=Af[:, 1024:2048], in_=Af[:, 1024:2048], func=AF.Sqrt, scale=-1.0, bias=self.one_ap),
                  reads=self.R_A[4:8] + [self.R_deriv], writes=self.R_A[4:8])
            for s in range(2):
                c = bt * 2 + s
                b = s
                tr.op("dve", lambda e: e.scalar_tensor_tensor(out=A_ix[s], in0=A_ix[s], scalar=hfl, in1=A_om[s], op0=ALU.mult, op1=ALU.mult),
                      reads=RA_ix[s] + RA_om[s] + [self.R_deriv], writes=RA_ix[s])
                tr.op("dve", lambda e: e.tensor_tensor_scan(out=hf[b], data0=A_a[s], data1=A_ix[s], initial=self.state[:, c:c + 1],
                                                             op0=ALU.mult, op1=ALU.add),
                      reads=RA_a[s] + RA_ix[s] + [self.R_state[c]], writes=[R_hf[b]])
                tr.op("pool", lambda e: e.tensor_copy(out=self.state[:, c:c + 1], in_=hf[b][:, 511:512]), reads=[R_hf[b]], writes=[self.R_state[c]])
                if need_h:
                    tr.op("pool", lambda e: e.tensor_copy(out=self.bufQ[:, c, :], in_=hf[b]), reads=[R_hf[b]], writes=[self.R_Q[c]])

        S1(0)
        for bt in range(4):
            S2(bt)
            if bt < 3:
                S1(bt + 1)
            S3(bt)
        if need_h:
            c0, c1 = TR[0] * 128, TR[1] * 128
            n = c1 - c0
            for c in range(8):
                bank, rbk = self.proj(P_WIN + 8 + c, c0, c1)
                tr.op("act", lambda e: e.activation(out=gy[:, 0:n], in_=bank[:, 0:n], func=AF.Gelu_apprx_tanh), reads=[rbk], writes=[R_gy])
                tr.op("dve", lambda e: e.tensor_tensor(out=self.bufM[:, c, c0:c1], in0=gy[:, 0:n], in1=self.bufQ[:, c, c0:c1], op=ALU.mult),
                      reads=[R_gy, self.R_Q[c]], writes=[self.R_M[c]])

    def kv(self, G):
        cfg, tr = self.cfg, self.tr
        S = self.bufS
        vt = S[:, 4872:5128].bitcast(BF16)
        R_vt = self.RS("vt")
        gfl = self.sm("gflag", G)
        for pid, loc, rl in ((P_WIN + 20, self.kc_loc, self.R_kcl), (P_WIN + 21, self.vc_loc, self.R_vcl)):
            bank, rbk = self.proj(pid, 0, 512)
            tr.op("pool", lambda e: e.tensor_copy(out=loc[:, 0:16], in_=loc[:, 512:528]), reads=[rl], writes=[rl])
            tr.op("act", lambda e: e.copy(out=loc[:, 16:528], in_=bank[:, 0:512]), reads=[rbk], writes=[rl])
        bank, rbk = self.proj(P_WIN + 22, 0, 512)
        tr.op("act", lambda e: e.copy(out=self.ksT[:, G * GT:(G + 1) * GT], in_=bank[:, 0:512]), reads=[rbk], writes=self.R_ks[4 * G:4 * G + 4])

        def store_v(pid, V1, RV, tbase, ta):
            bank, rbk = self.proj(pid, 0, 512)
            tr.op("act", lambda e: e.activation(out=vt, in_=bank[:, 0:512], func=AF.Identity, scale=gfl), reads=[rbk, self.R_sm], writes=[R_vt])
            bank2, rb2 = self.bank_mm()
            bb = self.bfv(bank2)
            for t in range(ta, 4):
                tr.op("pe", lambda e: e.transpose(out=bb[:, t * 128:(t + 1) * 128], in_=vt[:, t * 128:(t + 1) * 128], identity=self.masks[:, 0, :]),
                      reads=[R_vt, self.R_const], writes=[rb2])
            nt_ = 4 - ta
            i0 = 4 * G + ta - tbase
            tr.op("dve", lambda e: e.tensor_copy(out=V1[:, i0:i0 + nt_, :, 0:64],
                                                 in_=bb[:, ta * 128:512].rearrange("p (t g d) -> p t g d", t=nt_, g=2)),
                  reads=[rb2], writes=RV[i0:i0 + nt_])
            tr.op("pool", lambda e: e.tensor_scalar(out=V1[:, i0:i0 + nt_, :, 64],
                                                     in0=self.ones_bf[:, 0:2 * nt_].rearrange("p (a b) -> p a b", a=nt_),
                                                     scalar1=gfl, scalar2=None, op0=ALU.mult),
                  reads=[self.R_const, self.R_sm], writes=RV[i0:i0 + nt_])

        store_v(P_WIN + 23, self.V1s, self.R_vs, 0, 0)
        if 4 * G + 3 >= cfg.WT0:
            ta = max(0, cfg.WT0 - 4 * G)
            bank, rbk = self.proj(P_WIN + 24, 0, 512)
            w0 = 4 * G + ta - cfg.WT0
            tr.op("act", lambda e: e.copy(out=self.kwT[:, w0 * 128:(w0 + 4 - ta) * 128], in_=bank[:, ta * 128:512]),
                  reads=[rbk], writes=self.R_kw[w0:w0 + 4 - ta])
            store_v(P_WIN + 25, self.V1w, self.R_vw, cfg.WT0, ta)

    def compress(self, G):
        cfg, tr = self.cfg, self.tr
        S = self.bufS
        dv = self.deriv
        m0 = 1 if G == 0 else 0
        col0 = 32 * G - 1 + m0
        ncol = 32 - m0
        for kind in range(2):
            pid = P_W1K if kind == 0 else P_W1V
            loc, rl = (self.kc_loc, self.R_kcl) if kind == 0 else (self.vc_loc, self.R_vcl)
            hid = S[:, 5128 + kind * 16:5128 + (kind + 1) * 16].bitcast(BF16)
            R_hid = self.RS("hid%d" % kind)
            bank, rbk = self.bank_mm()
            for half in range(2):
                buf, rb = self.fetch(pid + half)
                w1 = buf.rearrange("p (l e) -> p l e", e=128)
                for li in range(16):
                    l = half * 16 + li
                    tr.op("pe", lambda e: e.matmul(bank[:, 0:32], lhsT=w1[:, li, :], rhs=loc[:, l:l + 497:16], start=(l == 0), stop=(l == 31)),
                          reads=[rb, rl], writes=[rbk])
            tr.op("act", lambda e: e.activation(out=hid, in_=bank[:, 0:32], func=AF.Gelu_apprx_tanh, bias=dv[:, 24 + kind:25 + kind]),
                  reads=[rbk, self.R_deriv], writes=[R_hid])
            bank2, rb2 = self.bank_mm()
            tr.op("pe", lambda e: e.matmul(bank2[:, 0:32], lhsT=self.smb[:, 128 * kind:128 * kind + 128], rhs=hid, start=True, stop=True),
                  reads=[self.R_smb, R_hid], writes=[rb2])
            if kind == 0:
                tr.op("act", lambda e: e.copy(out=self.kcmpT[:, col0:col0 + ncol], in_=bank2[:, m0:32]), reads=[rb2], writes=[self.R_kcmp])
            else:
                tr.op("act", lambda e: e.copy(out=self.vcmpT[:, col0:col0 + ncol], in_=bank2[:, m0:32]), reads=[rb2], writes=[self.R_vcmp])
                for nt in sorted({col0 // 128, (col0 + ncol - 1) // 128}):
                    bankT, rbT = self.bank_mm()
                    bb = self.bfv(bankT)
                    tr.op("pe", lambda e: e.transpose(out=bb[:, 0:128], in_=self.vcmpT[:, nt * 128:(nt + 1) * 128], identity=self.masks[:, 0, :]),
                          reads=[self.R_vcmp, self.R_const], writes=[rbT])
                    tr.op("dve", lambda e: e.tensor_copy(out=self.vcmp1[:, nt, :, 0:64], in_=bb[:, 0:128].rearrange("p (g d) -> p g d", g=2)),
                          reads=[rbT], writes=[self.R_vcmp])

    def nsa_unit(self, G, t, g, uidx):
        cfg, tr = self.cfg, self.tr
        vq = 4 * G + t
        ti = vq - cfg.T_HALO
        S = self.bufS
        A = self.bufA
        NCT, NB = cfg.NCT, cfg.NBLK
        p = uidx % 2
        bf = lambda o, n: S[:, o:o + n].bitcast(BF16)
        RS = self.RS
        cm = bf(0, NCT * 64).rearrange("p (n q) -> p n q", q=128)
        R_cm = RS("cm")
        if p == 0:
            Pc = [bf(256 + n * 256, 256) for n in range(NCT)]
            R_Pc = [RS("Pc%d" % n) for n in range(NCT)]
            qz, R_qz = bf(2304, 256), RS("qz")
            negm, R_negm = self.negm2, [self.R_negm2]
            OT0, R_OT0 = S[:, 2904:2904 + 512], [RS("OT0")]
        else:
            Pc = [A[:, 2 + n, :] for n in range(NCT)]
            R_Pc = [self.R_A[2 + n] for n in range(NCT)]
            qz, R_qz = A[:, 1, :], self.R_A[1]
            negm, R_negm = A[:, 6:8, :], self.R_A[6:8]
            OT0, R_OT0 = A[:, 8:10, :].rearrange("p a b -> p (a b)").bitcast(F32), self.R_A[8:10]
        E4 = [bf(1280 + i * 256, 256) for i in range(3)] + [A[:, 0, :]]
        R_E4 = [RS("E%d" % i) for i in range(3)] + [self.R_A[0]]
        imp = S[:, 2560:2560 + NB]
        impw = S[:, 2688:2688 + NB]
        m8 = S[:, 2816:2832]
        zc = S[:, 2832:2836]
        rz = S[:, 2836:2840]
        sel = bf(2840, 64)[:, 0:NB]
        OT = [OT0, S[:, 2904 + 512:2904 + 1024], S[:, 2904 + 1024:2904 + 1536]]
        R_OT = [R_OT0, [RS("OT1")], [RS("OT2")]]
        etmp = S[:, 4440:4824]
        otok = bf(4824, 256)
        coef = S[:, 5080:5086]
        zz = S[:, 5086:5092]
        R_imp, R_m8, R_sel = RS("imp"), RS("m8"), RS("sel")
        R_et, R_otok, R_coef = RS("etmp"), RS("otok"), RS("coef")
        qT = self.bufQ
        tq = slice(t * 128, (t + 1) * 128)
        ident_b = self.masks[:, 0, :]
        v4 = lambda ap: ap.rearrange("p (h q) -> p h q", h=4)
        bc4 = lambda ap: ap.unsqueeze(1).broadcast_to([128, 4, 128])
        rq = self.R_Q[0:4]
        bOc, rOc = self.ps[4], self.R_ps[4]
        bOs, rOs = self.ps[5], self.R_ps[5]
        bI = [(self.ps[6], self.R_ps[6]), (self.ps[3], self.R_ps[3])]
        bOw, rOw = self.ps[7], self.R_ps[7]
        SKEW = 2

        def h0():
            if g == 0:
                tr.dma("sp", cm, self.cmask_d[ti], writes=[R_cm])
            tr.op("act", lambda e: e.activation(out=v4(qz), in_=qT[:, 0:4, tq], func=AF.Identity, scale=self.sm("hm", g)),
                  reads=rq + [self.R_sm], writes=[R_qz])
            for nt in range(NCT):
                bank, rbk = self.bank_mm()
                tr.op("pe", lambda e: e.matmul(bank[:, 0:512], lhsT=self.kcmpT[:, nt * 128:(nt + 1) * 128], rhs=qz, start=True, stop=True),
                      reads=[self.R_kcmp, R_qz], writes=[rbk])
                tr.op("act", lambda e: e.activation(out=Pc[nt], in_=bank[:, 0:512], func=AF.Exp), reads=[rbk], writes=[R_Pc[nt]])
                tr.op("dve", lambda e: e.tensor_tensor(out=v4(Pc[nt]), in0=v4(Pc[nt]), in1=bc4(cm[:, nt, :]), op=ALU.mult),
                      reads=[R_Pc[nt], R_cm], writes=[R_Pc[nt]])

        def h1():
            for nt in range(NCT):
                tr.op("pe", lambda e: e.matmul(bOc[0:65, 0:512], lhsT=self.vcmp1[:, nt, g, :], rhs=Pc[nt], start=(nt == 0), stop=(nt == NCT - 1)),
                      reads=[self.R_vcmp, R_Pc[nt]], writes=[rOc])
            for hh in range(2):
                bk, rk = bI[hh]
                for h2_ in range(2):
                    h = 2 * hh + h2_
                    for nt in range(NCT):
                        tr.op("pe", lambda e: e.matmul(bk[:, h2_ * (NB + 1):(h2_ + 1) * (NB + 1)], lhsT=Pc[nt][:, h * 128:(h + 1) * 128],
                                                       rhs=self.cover1[:, nt, :], start=(nt == 0), stop=(nt == NCT - 1)),
                              reads=[R_Pc[nt], self.R_const], writes=[rk])
            for hh in range(2):
                bk, rk = bI[hh]
                tr.op("dve", lambda e: e.tensor_scalar(out=zc[:, 2 * hh:2 * hh + 2], in0=bk[:, NB:2 * (NB + 1):NB + 1], scalar1=TINY, scalar2=None,
                                                        op0=ALU.max),
                      reads=[rk], writes=[R_m8])
            tr.op("dve", lambda e: e.reciprocal(out=rz, in_=zc), reads=[R_m8], writes=[R_m8])
            for h in range(4):
                bk, rk = bI[h // 2]
                src = bk[:, (h % 2) * (NB + 1):(h % 2) * (NB + 1) + NB]
                if h == 0:
                    tr.op("dve", lambda e: e.scalar_tensor_tensor(out=imp, in0=src, scalar=rz[:, 0:1], in1=self.sm("fbias"), op0=ALU.mult, op1=ALU.add),
                          reads=[rk, R_m8, self.R_sm], writes=[R_imp])
                else:
                    tr.op("dve", lambda e: e.scalar_tensor_tensor(out=imp, in0=src, scalar=rz[:, h:h + 1], in1=imp, op0=ALU.mult, op1=ALU.add),
                          reads=[rk, R_m8, R_imp], writes=[R_imp])
            lo0 = max(0, 2 * vq - 1)
            tr.op("dve", lambda e: e.memset(imp[0:64, lo0:2 * vq + 1], 1e9), reads=[R_imp], writes=[R_imp])
            tr.op("dve", lambda e: e.memset(imp[64:128, 2 * vq:2 * vq + 2], 1e9), reads=[R_imp], writes=[R_imp])
            tr.op("dve", lambda e: e.max(out=m8[:, 0:8], in_=imp), reads=[R_imp], writes=[R_m8])
            tr.op("dve", lambda e: e.match_replace(out=impw, in_to_replace=m8[:, 0:8], in_values=imp, imm_value=-1e30),
                  reads=[R_imp, R_m8], writes=[R_sel])
            tr.op("dve", lambda e: e.max(out=m8[:, 8:16], in_=impw), reads=[R_sel], writes=[R_m8])
            tr.op("dve", lambda e: e.tensor_scalar(out=sel, in0=imp, scalar1=m8[:, 15:16], scalar2=None, op0=ALU.is_ge),
                  reads=[R_imp, R_m8, R_sel], writes=[R_sel])

        def h2():
            bankT, rbT = bI[0]
            bbT = self.bfv(bankT)
            tr.op("pe", lambda e: e.transpose(out=bbT[0:NB, 0:128], in_=sel, identity=ident_b), reads=[R_sel, self.R_const], writes=[rbT])
            for hb in range(2):
                if 64 * hb >= NB:
                    continue
                rows = slice(64 * hb, min(NB, 64 * hb + 64))
                nr = rows.stop - rows.start
                tr.op("dve", lambda e: e.tensor_scalar(out=v4(negm[:, hb, :])[rows],
                                                        in0=bbT[rows, 0:128].unsqueeze(1).broadcast_to([nr, 4, 128]),
                                                        scalar1=30000.0, scalar2=-30000.0, op0=ALU.mult, op1=ALU.add),
                      reads=[rbT], writes=R_negm)
            tr.op("act", lambda e: e.copy(out=OT[0][0:65, :], in_=bOc[0:65, 0:512]), reads=[rOc], writes=R_OT[0])

        steps = []
        kts = [kt for kt in range(vq - 4, vq + 1) if kt >= 0]
        for i, kt in enumerate(kts):
            def wf(j, i=i, kt=kt):
                w = kt - cfg.WT0
                masked = (kt == vq or kt == vq - 4)
                bank, rbk = self.bank_mm()
                tr.op("pe", lambda e: e.matmul(bank[:, 0:512], lhsT=self.kwT[:, w * 128:(w + 1) * 128], rhs=qz, start=True, stop=(not masked)),
                      reads=[self.R_kw[w], R_qz], writes=[rbk])
                if masked:
                    mi = 0 if kt == vq else 1
                    tr.op("pe", lambda e: e.matmul(bank[:, 0:512], lhsT=ident_b, rhs=self.negc[:, mi, :], start=False, stop=True),
                          reads=[self.R_const], writes=[rbk])
                tr.op("act", lambda e: e.activation(out=E4[j % 4], in_=bank[:, 0:512], func=AF.Exp), reads=[rbk], writes=[R_E4[j % 4]])

            def wb(j, i=i, kt=kt):
                w = kt - cfg.WT0
                tr.op("pe", lambda e: e.matmul(bOw[0:65, 0:512], lhsT=self.V1w[:, w, g, :], rhs=E4[j % 4], start=(i == 0), stop=(i == len(kts) - 1)),
                      reads=[self.R_vw[w], R_E4[j % 4]], writes=[rOw])
            steps.append((wf, wb))
        nstep = vq + 1
        for kt in range(nstep):
            def sf(j, kt=kt):
                bank, rbk = self.bank_mm()
                tr.op("pe", lambda e: e.matmul(bank[:, 0:512], lhsT=self.ksT[:, kt * 128:(kt + 1) * 128], rhs=qz, start=True, stop=False),
                      reads=[self.R_ks[kt], R_qz], writes=[rbk])
                r = (2 * kt) // 64
                jj = kt % 32
                tr.op("pe", lambda e: e.matmul(bank[:, 0:512], lhsT=self.eec[:, jj, :], rhs=negm[:, r, :],
                                               start=False, stop=(kt != vq)),
                      reads=[self.R_const] + R_negm, writes=[rbk])
                if kt == vq:
                    tr.op("pe", lambda e: e.matmul(bank[:, 0:512], lhsT=ident_b, rhs=self.negc[:, 0, :], start=False, stop=True),
                          reads=[self.R_const], writes=[rbk])
                tr.op("act", lambda e: e.activation(out=E4[j % 4], in_=bank[:, 0:512], func=AF.Exp), reads=[rbk], writes=[R_E4[j % 4]])

            def sb(j, kt=kt):
                tr.op("pe", lambda e: e.matmul(bOs[0:65, 0:512], lhsT=self.V1s[:, kt, g, :], rhs=E4[j % 4], start=(kt == 0), stop=(kt == nstep - 1)),
                      reads=[self.R_vs[kt], R_E4[j % 4]], writes=[rOs])
            steps.append((sf, sb))

        def run(inject):
            n = len(steps)
            for i in range(n + SKEW):
                if i < n:
                    steps[i][0](i)
                if i >= SKEW:
                    steps[i - SKEW][1](i - SKEW)
                for fn in inject.get(i, ()):
                    fn()
            for i in sorted(k for k in inject if k >= n + SKEW):
                for fn in inject[i]:
                    fn()
            tr.op("dve", lambda e: e.tensor_copy(out=OT[1][0:65, :], in_=bOs[0:65, 0:512]), reads=[rOs], writes=R_OT[1])
            tr.op("act", lambda e: e.copy(out=OT[2][0:65, :], in_=bOw[0:65, 0:512]), reads=[rOw], writes=R_OT[2])

        def EP():
            for hh in range(2):
                bankE, rbE = bI[hh]
                for c2 in range(2):
                    c = 2 * hh + c2
                    for b in range(3):
                        k = c2 * 3 + b
                        tr.op("pe", lambda e: e.transpose(out=bankE[:, k * 65:(k + 1) * 65], in_=OT[b][0:65, c * 128:(c + 1) * 128],
                                                          identity=self.ident_f[0:65, 0:65]),
                              reads=R_OT[b] + [self.R_const], writes=[rbE])
                tr.op("dve", lambda e: e.tensor_scalar(out=zz, in0=bankE[:, 64:390:65], scalar1=TINY, scalar2=None, op0=ALU.max),
                      reads=[rbE], writes=[R_coef])
                tr.op("dve", lambda e: e.reciprocal(out=zz, in_=zz), reads=[R_coef], writes=[R_coef])
                tr.op("dve", lambda e: e.tensor_tensor(out=coef, in0=zz, in1=self.g_sb[:, t, 12 * g + 6 * hh:12 * g + 6 * hh + 6], op=ALU.mult),
                      reads=[R_coef, self.R_g], writes=[R_coef])
                tr.op("dve", lambda e: e.tensor_tensor(out=etmp.rearrange("p (k d) -> p k d", d=64),
                                                       in0=bankE[:, 0:390].rearrange("p (k d) -> p k d", d=65)[:, :, 0:64],
                                                       in1=coef.unsqueeze(2).broadcast_to([128, 6, 64]), op=ALU.mult),
                      reads=[rbE, R_coef], writes=[R_et])
                o0 = 256 * g + 128 * hh
                with self.nc.allow_low_precision(reason="fp32 reduce, bf16 store"):
                    tr.op("dve", lambda e: e.tensor_reduce(out=otok[:, o0:o0 + 128].rearrange("p (c d) -> p c d", c=2),
                                                           in_=etmp.rearrange("p (c b d) -> p c d b", c=2, b=3), axis=AX.X, op=ALU.add),
                          reads=[R_et], writes=[R_otok])
            if g == 1:
                bankF, rbF = self.bank_mm()
                bbF = self.bfv(bankF)
                for cc in range(4):
                    tr.op("pe", lambda e: e.transpose(out=bbF[:, cc * 128:(cc + 1) * 128], in_=otok[:, cc * 128:(cc + 1) * 128], identity=ident_b),
                          reads=[R_otok, self.R_const], writes=[rbF])
                tr.op("act", lambda e: e.copy(out=self.bufM[:, 8:12, tq], in_=bbF[:, 0:512].rearrange("p (c q) -> p c q", c=4)),
                      reads=[rbF], writes=self.R_M[8:12])

        return dict(h0=h0, h1=h1, h2=h2, run=run, EP=EP, n=len(steps))

    def mem_attn(self, c0, c1):
        cfg, tr = self.cfg, self.tr
        S = self.bufS
        n = c1 - c0
        E = [S[:, 1280 + i * 256:1280 + (i + 1) * 256].bitcast(BF16) for i in range(3)]
        R_E = [self.RS("E%d" % i) for i in range(3)]
        rzm = S[:, 2904:2904 + 512]
        R_rz = self.RS("OT0")
        for h in range(4):
            for mt in range(2):
                bank, rbk = self.bank_mm()
                tr.op("pe", lambda e: e.matmul(bank[:, 0:n], lhsT=self.kmemT[:, h, mt * 128:(mt + 1) * 128], rhs=self.bufQ[:, 4 + h, c0:c1],
                                               start=True, stop=True),
                      reads=[self.R_kmem, self.R_Q[4 + h]], writes=[rbk])
                tr.op("act", lambda e: e.activation(out=E[mt][:, 0:n], in_=bank[:, 0:n], func=AF.Exp), reads=[rbk], writes=[R_E[mt]])
            bO, rO = self.ps[4], self.R_ps[4]
            bZ, rZ = self.ps[5], self.R_ps[5]
            for mt in range(2):
                tr.op("pe", lambda e: e.matmul(bO[:, 0:n], lhsT=self.vmem[:, mt, h, :], rhs=E[mt][:, 0:n], start=(mt == 0), stop=(mt == 1)),
                      reads=[self.R_vmem, R_E[mt]], writes=[rO])
            for mt in range(2):
                tr.op("pe", lambda e: e.matmul(bZ[:, 0:n], lhsT=self.ones_bf[:, :], rhs=E[mt][:, 0:n], start=(mt == 0), stop=(mt == 1)),
                      reads=[self.R_const, R_E[mt]], writes=[rZ])
            tr.op("dve", lambda e: e.reciprocal(out=rzm[:, 0:n], in_=bZ[:, 0:n]), reads=[rZ], writes=[R_rz])
            tr.op("dve", lambda e: e.tensor_tensor(out=self.bufM[:, 12 + h, c0:c1], in0=bO[:, 0:n], in1=rzm[:, 0:n], op=ALU.mult),
                  reads=[rO, R_rz], writes=[self.R_M[12 + h]])

    def out_proj_ln1(self, G, TR):
        cfg, tr = self.cfg, self.tr
        tiles = list(range(TR[0], TR[1]))
        acc = {t: (self.ps[4 + i], self.R_ps[4 + i]) for i, t in enumerate(tiles)}
        gbuf0 = self.load_gb(0)
        for t in tiles:
            tr.op("dve", lambda e: e.scalar_tensor_tensor(out=self.xt[:, t, :], in0=self.xt[:, t, :], scalar=self.mv[:, t, 0:1], in1=gbuf0[:, 0:D],
                                                           op0=ALU.subtract, op1=ALU.mult),
                  reads=[self.R_xt[t], self.R_mv] + self.R_A[0:8], writes=[self.R_xt[t]])
            tr.op("dve", lambda e: e.scalar_tensor_tensor(out=self.xt[:, t, :], in0=self.xt[:, t, :], scalar=self.lnt[:, 1, t:t + 1], in1=gbuf0[:, D:2 * D],
                                                           op0=ALU.mult, op1=ALU.add),
                  reads=[self.R_xt[t], self.R_lnt] + self.R_A[8:16], writes=[self.R_xt[t]])
        for cb in range(4):
            for kq in range(4):
                buf, rb = self.fetch(P_WOUT + cb * 4 + kq)
                wv = buf.rearrange("p (a b) -> p a b", b=512)
                for t in tiles:
                    bk, rk = acc[t]
                    for kci in range(4):
                        kc = kq * 4 + kci
                        tr.op("pe", lambda e: e.matmul(bk[:, 0:512], lhsT=self.bufM[:, kc, t * 128:(t + 1) * 128], rhs=wv[:, kci, :],
                                                       start=(kc == 0), stop=(kc == KC - 1)),
                              reads=[rb, self.R_M[kc]], writes=[rk])
            for t in tiles:
                bk, rk = acc[t]
                xs = self.xt[:, t, cb * 512:(cb + 1) * 512]
                tr.op("dve", lambda e: e.scalar_tensor_tensor(out=xs, in0=xs, scalar=ALPHA, in1=bk[:, 0:512], op0=ALU.mult, op1=ALU.add),
                      reads=[rk, self.R_xt[t]], writes=[self.R_xt[t]])
        gbuf = self.load_gb(1)
        self.ln_stats(tiles)
        for t in tiles:
            self.ln_apply(t, gbuf, "g1", "b1", want_tok=True)
        self.ln_transpose(tiles, "g1", "b1")

    def ffn(self, G, TR):
        cfg, tr = self.cfg, self.tr
        S = self.bufS
        c0, c1 = TR[0] * 128, TR[1] * 128
        n = c1 - c0
        if TR != (0, 4):
            for c in range(88):
                bank, rbk = self.proj(P_WUP + c, c1 - 2, c1)
                tr.op("dve", lambda e: e.tensor_scalar(out=self.ftail[:, c, :], in0=bank[:, 0:2], scalar1=self.sm("gflag", G), scalar2=None,
                                                        op0=ALU.mult),
                      reads=[rbk, self.R_sm], writes=[self.R_ftail[c]])
            return
        ext = [[S[:, (2 * w + b) * 516:(2 * w + b + 1) * 516] for b in range(2)] for w in range(2)]
        cv = [[S[:, 2064 + (2 * w + b) * 512:2064 + (2 * w + b + 1) * 512] for b in range(2)] for w in range(2)]
        ga = [S[:, 4112 + b * 512:4112 + (b + 1) * 512] for b in range(2)]
        R_ext = [[self.RS("ext%d%d" % (w, b)) for b in range(2)] for w in range(2)]
        R_cv = [[self.RS("cv%d%d" % (w, b)) for b in range(2)] for w in range(2)]
        R_ga = [self.RS("ga%d" % b) for b in range(2)]
        actT = self.bufA
        fcw = lambda cc, k: self.sm("fcw", cc * 3 + k)
        acc = [(self.ps[4 + t], self.R_ps[4 + t]) for t in range(4)]
        for third, (lo, hi) in enumerate(((0, 16), (16, 32), (32, 44))):
            for c in range(lo, hi):
                b = c % 2
                for w, cc in ((0, c), (1, 44 + c)):
                    bank, rbk = self.proj(P_WUP + cc, 0, 512)
                    ex, rex = ext[w][b], R_ext[w][b]
                    cvv, rcv = cv[w][b], R_cv[w][b]
                    tr.op("pool", lambda e: e.tensor_copy(out=ex[:, 0:2], in_=self.ftail[:, cc, :]), reads=[self.R_ftail[cc]], writes=[rex])
                    tr.op("act", lambda e: e.copy(out=ex[:, 2:514], in_=bank[:, 0:512]), reads=[rbk], writes=[rex])
                    tr.op("pool", lambda e: e.tensor_copy(out=self.ftail[:, cc, :], in_=ex[:, 512:514]), reads=[rex], writes=[self.R_ftail[cc]])
                    tr.op("pool", lambda e: e.tensor_scalar(out=cvv, in0=ex[:, 2:514], scalar1=fcw(cc, 2), scalar2=self.sm("fcb", cc),
                                                             op0=ALU.mult, op1=ALU.add),
                          reads=[rex, self.R_sm], writes=[rcv])
                    for k in (1, 0):
                        tr.op("dve", lambda e: e.scalar_tensor_tensor(out=cvv, in0=ex[:, k:k + 512], scalar=fcw(cc, k), in1=cvv,
                                                                       op0=ALU.mult, op1=ALU.add),
                              reads=[rex, rcv, self.R_sm], writes=[rcv])
                tr.op("act", lambda e: e.activation(out=ga[b], in_=cv[0][b], func=AF.Gelu_apprx_tanh), reads=[R_cv[0][b]], writes=[R_ga[b]])
                tr.op("dve", lambda e: e.tensor_tensor(out=actT[:, c - lo, :], in0=ga[b], in1=cv[1][b], op=ALU.mult),
                      reads=[R_ga[b], R_cv[1][b]], writes=[self.R_A[c - lo]])
            for cb in range(4):
                for fq in range(lo // 4, hi // 4):
                    buf, rb = self.fetch(P_WDN + cb * 11 + fq)
                    wv = buf.rearrange("p (a b) -> p a b", b=512)
                    for t in range(4):
                        bk, rk = acc[t]
                        for fi in range(4):
                            ffc = fq * 4 + fi
                            tr.op("pe", lambda e: e.matmul(bk[:, 0:512], lhsT=actT[:, ffc - lo, t * 128:(t + 1) * 128], rhs=wv[:, fi, :],
                                                           start=(ffc == lo), stop=(ffc == hi - 1)),
                                  reads=[rb, self.R_A[ffc - lo]], writes=[rk])
                for t in range(4):
                    bk, rk = acc[t]
                    xs = self.xt[:, t, cb * 512:(cb + 1) * 512]
                    if third == 0:
                        tr.op("dve", lambda e: e.scalar_tensor_tensor(out=xs, in0=xs, scalar=ALPHA, in1=bk[:, 0:512], op0=ALU.mult, op1=ALU.add),
                              reads=[rk, self.R_xt[t]], writes=[self.R_xt[t]])
                    else:
                        tr.op("dve", lambda e: e.tensor_tensor(out=xs, in0=xs, in1=bk[:, 0:512], op=ALU.add),
                              reads=[rk, self.R_xt[t]], writes=[self.R_xt[t]])
        gbuf = self.load_gb(2)
        self.ln_stats([0, 1, 2, 3])
        for t in range(4):
            self.ln_apply(t, gbuf, "g2", "b2", want_tok=True, want_T=False)
            r0 = (G - cfg.G0) * GT + t * 128
            tr.dma("pool", self.y[r0:r0 + 128, :], self.xt[:, t, :], reads=[self.R_xt[t]], writes=[self.R_y])


def kernel(**inputs):
    cfg = Cfg(SEQ=np.asarray(inputs["x"]).shape[1])
    prog = Prog(cfg)
    prog.emit(dry=True)
    prog.emit(dry=False)
    sh = prep_shared(cfg, inputs)
    in_maps = [prep_core(cfg, inputs, sh, c) for c in range(8)]
    res = run_bass_kernel_spmd(prog.nc, in_maps, core_ids=list(range(8)))
    B = 2
    out = np.zeros((B, cfg.SEQ, D), np.float32)
    for c in range(8):
        b, j = c // 4, c % 4
        out[b, j * cfg.CH:(j + 1) * cfg.CH] = np.asarray(res.results[c]["y"], np.float32)
    return out
```

```python
import numpy as np
import ml_dtypes
import concourse.bass as bass
import concourse.mybir as mybir
from concourse.bass_utils import run_bass_kernel_spmd

F32 = mybir.dt.float32
BF16 = mybir.dt.bfloat16
AF = mybir.ActivationFunctionType
ALU = mybir.AluOpType
AX = mybir.AxisListType
NPBF = ml_dtypes.bfloat16

D = 2048
KC = 16
DFF = 5632
NFF = 44
LN_EPS = 1e-5
ALPHA = 2.0 ** 0.25
GT = 512
TINY = 1e-30


class Region:
    __slots__ = ("name", "w", "r", "excl")

    def __init__(self, name, excl=False):
        self.name = name
        self.w = None
        self.r = []
        self.excl = excl


class Eng:
    def __init__(self, name, h, sem):
        self.name = name
        self.h = h
        self.sem = sem
        self.key = name
        self.cnt = 0
        self.known = {}


class Tracker:
    def __init__(self, nc, n_dma_sems=40):
        self.nc = nc
        self.sems = {}
        self.eng = {}
        for name, h in (("pe", nc.tensor), ("act", nc.scalar), ("dve", nc.vector),
                        ("pool", nc.gpsimd), ("sp", nc.sync)):
            sem = nc.alloc_semaphore("cnt_" + name)
            self.sems[name] = sem
            self.eng[name] = Eng(name, h, sem)
        self.dma_sems = []
        self.dma_sets = {}
        for sname, cnt in (("main", n_dma_sems), ("cast", 24)):
            lst = []
            for i in range(cnt):
                key = "dma_%s%d" % (sname, i)
                sem = nc.alloc_semaphore(key)
                self.sems[key] = sem
                slot = [key, sem, 0, None]
                lst.append(slot)
                self.dma_sems.append(slot)
            self.dma_sets[sname] = [lst, 0]
        self.n_wait = 0
        self.n_ins = 0

    def _need(self, e, deps):
        best = {}
        for ev in deps:
            if ev is None:
                continue
            k, v, hist = ev
            if k == e.key and e.name == "pe":
                continue
            if e.known.get(k, 0) >= v:
                continue
            if k not in best or best[k][1] < v:
                best[k] = ev
        if not best:
            return
        newk = dict(e.known)
        for k, (kk, v, hist) in best.items():
            e.h.wait_ge(self.sems[k], v)
            self.n_wait += 1
            if hist is not None:
                for k2, v2 in hist.items():
                    if newk.get(k2, 0) < v2:
                        newk[k2] = v2
            if newk.get(k, 0) < v:
                newk[k] = v
        e.known = newk

    def _collect(self, e, reads, writes):
        deps = []
        for r in reads:
            deps.append(r.w)
            if r.excl:
                deps.extend(x for x in r.r if x[0] != e.key)
        for w in writes:
            if w.w is not None:
                deps.append(w.w)
            deps.extend(w.r)
        return deps

    def _update(self, ev, reads, writes):
        for r in reads:
            r.r = [x for x in r.r if x[0] != ev[0]]
            r.r.append(ev)
        for w in writes:
            w.w = ev
            w.r = []

    def op(self, en, fn, reads=(), writes=()):
        e = self.eng[en]
        self._need(e, self._collect(e, reads, writes))
        ins = fn(e.h)
        e.cnt += 1
        ins.then_inc(e.sem, 1)
        self.n_ins += 1
        ev = (e.key, e.cnt, e.known)
        self._update(ev, reads, writes)
        return ev

    def dma(self, qn, out, in_, reads=(), writes=(), slots="main", **kw):
        e = self.eng[qn]
        if qn == "pool":
            slots = "cast"
        st = self.dma_sets[slots]
        slot = st[0][st[1]]
        st[1] = (st[1] + 1) % len(st[0])
        deps = self._collect(e, reads, writes)
        if slot[3] is not None:
            deps.append(slot[3])
        self._need(e, deps)
        slot[2] += 16
        ins = e.h.dma_start(out=out, in_=in_, **kw)
        ins.then_inc(slot[1], 16)
        self.n_ins += 1
        ev = (slot[0], slot[2], e.known)
        slot[3] = ev
        self._update(ev, reads, writes)
        return ev

    def wait_events(self, en, events):
        self._need(self.eng[en], events)

    def barrier(self, names=("pe", "act", "dve", "pool")):
        evs = [(self.eng[n].key, self.eng[n].cnt, self.eng[n].known) for n in names
               if self.eng[n].cnt > 0]
        for n in names:
            self._need(self.eng[n], evs)


class Cfg:
    def __init__(self, SEQ=8192, debug=False):
        self.SEQ = SEQ
        self.CH = SEQ // 4
        self.NG = SEQ // GT
        self.OWN_G = self.CH // GT
        self.G0 = self.NG - self.OWN_G
        self.NT = SEQ // 128
        self.NBLK = SEQ // 64
        self.NCMP = SEQ // 16
        self.NCT = max(1, self.NCMP // 128)
        self.NCP = min(128, self.NCMP)
        self.T_HALO = self.G0 * 4 - 1
        self.WT0 = self.T_HALO - 4
        self.NWT = self.NT - self.WT0
        self.NOT = self.NT - self.T_HALO
        self.debug = debug


def small_layout(cfg):
    lay = {}
    off = 0
    for name, n in (("g_in", 16), ("b_in", 16), ("g1", 16), ("b1", 16), ("lcw", 32), ("lcb", 8),
                    ("ba", 8), ("bx", 8), ("lam", 8), ("fcw", 264), ("fcb", 88),
                    ("w2k", 128), ("w2v", 128), ("pek", 32), ("pev", 32), ("hm", 2),
                    ("gflag", cfg.NG), ("fbias", cfg.NBLK)):
        lay[name] = (off, n)
        off += n
    return lay, off


W_IN_SPLIT = dict(lx=0, ly=1024, q=2048, kc=2560, vc=2688, ks=2816, vs=2944, kw=3072, vw=3200,
                  gate=3328, mq=3352)


def win_chunk_cols():
    cols = []
    for c in range(8):
        cols.append(np.arange(c * 128, (c + 1) * 128))
    for c in range(8):
        cols.append(1024 + np.arange(c * 128, (c + 1) * 128))
    for c in range(4):
        cols.append(np.concatenate([2048 + c * 64 + np.arange(64), 2048 + (4 + c) * 64 + np.arange(64)]))
    for k in range(6):
        cols.append(2560 + k * 128 + np.arange(128))
    for h in range(4):
        cols.append(3352 + h * 128 + np.arange(128))
    return cols


def prep_shared(cfg, inp):
    f = np.float32
    sh = {}
    w_in = np.asarray(inp["w_in"][0], f)
    cols = win_chunk_cols()
    wt = np.stack([w_in[:, c] for c in cols])
    sh["w_in_t"] = np.ascontiguousarray(wt.reshape(30, KC, 128, 128).transpose(0, 2, 1, 3))
    sh["w_gate"] = np.ascontiguousarray(w_in[:, 3328:3352].reshape(KC, 128, 24).transpose(1, 0, 2))
    w_up = np.asarray(inp["ffn_w_up"][0], f)
    sh["w_up_t"] = np.ascontiguousarray(w_up.reshape(KC, 128, 88, 128).transpose(2, 1, 0, 3))
    w_out = np.asarray(inp["w_out"][0], f)
    sh["w_out_t"] = np.ascontiguousarray(w_out.reshape(4, 4, 128, 4, 512).transpose(3, 0, 2, 1, 4))
    w_dn = np.asarray(inp["ffn_w_down"][0], f)
    sh["w_dn_t"] = np.ascontiguousarray(w_dn.reshape(11, 4, 128, 4, 512).transpose(3, 0, 2, 1, 4))
    wm = np.asarray(inp["w_mem_kv"][0], f)
    sh["w_mk_t"] = np.ascontiguousarray(wm[:, 0:512].reshape(KC, 128, 4, 128).transpose(2, 1, 0, 3))
    sh["w_mv_t"] = np.ascontiguousarray(wm[:, 512:1024].reshape(4, 4, 128, 512).transpose(0, 2, 1, 3))
    wa = np.asarray(inp["lru_wa"][0], f)
    wx = np.asarray(inp["lru_wx"][0], f)
    sh["w_ax"] = np.ascontiguousarray(np.stack([wa.transpose(1, 0, 2), wx.transpose(1, 0, 2)], axis=1))
    for nm, key in (("w1k", "cmp_w1_k"), ("w1v", "cmp_w1_v")):
        w1 = np.asarray(inp[key][0], f)
        t = w1.transpose(1, 0, 2)
        bd = np.zeros((128, 32, 128), f)
        bd[0:64, :, 0:64] = t
        bd[64:128, :, 64:128] = t
        sh[nm] = np.ascontiguousarray(bd.reshape(128, 2, 16, 128).transpose(1, 0, 2, 3))
    lay, ns = small_layout(cfg)
    sp = np.zeros((128, ns), f)

    def put(name, arr):
        o, n = lay[name]
        assert arr.shape == (128, n), (name, arr.shape, n)
        sp[:, o:o + n] = arr

    col = lambda v: np.asarray(v, f).reshape(-1, 128).T
    put("g_in", col(inp["ln_in_g"]))
    put("b_in", col(inp["ln_in_b"]))
    put("g1", col(inp["ln1_g"][0]))
    put("b1", col(inp["ln1_b"][0]))
    cw = np.asarray(inp["lru_conv_w"][0], f)
    put("lcw", cw.reshape(4, 8, 128).transpose(2, 1, 0).reshape(128, 32))
    put("lcb", col(inp["lru_conv_b"][0]))
    put("ba", np.asarray(inp["lru_ba"][0], f).T)
    put("bx", np.asarray(inp["lru_bx"][0], f).T)
    put("lam", col(inp["lru_lam"][0]))
    fw = np.asarray(inp["ffn_conv_w"][0], f)
    put("fcw", fw.reshape(3, 88, 128).transpose(2, 1, 0).reshape(128, 264))
    put("fcb", col(inp["ffn_conv_b"][0]))
    w2k = np.asarray(inp["cmp_w2_k"][0], f)
    w2v = np.asarray(inp["cmp_w2_v"][0], f)
    def bdiag(w):
        o_ = np.zeros((128, 128), f)
        o_[0:64, 0:64] = w
        o_[64:128, 64:128] = w
        return o_
    put("w2k", bdiag(w2k))
    put("w2v", bdiag(w2v))
    hm = np.zeros((128, 2), f)
    hm[0:64, 0] = 1.0
    hm[64:128, 1] = 1.0
    put("hm", hm)
    pk = np.asarray(inp["cmp_pe_k"][0], f).T
    pv = np.asarray(inp["cmp_pe_v"][0], f).T
    put("pek", np.concatenate([pk, pk], axis=0))
    put("pev", np.concatenate([pv, pv], axis=0))
    sh["_small"] = sp
    gb = np.stack([np.stack([np.asarray(inp["ln_in_g"], f), np.asarray(inp["ln_in_b"], f)]),
                   np.stack([np.asarray(inp["ln1_g"][0], f), np.asarray(inp["ln1_b"][0], f)]),
                   np.stack([np.asarray(inp["ln2_g"][0], f), np.asarray(inp["ln2_b"][0], f)])])
    sh["gb"] = np.ascontiguousarray(gb.reshape(3, 1, 2 * D))
    ii = np.arange(128)
    sh["ident_f"] = np.eye(128, dtype=f)
    tri = (ii[:, None] <= ii[None, :]).astype(f)
    sh["masks_b"] = np.ascontiguousarray(
        np.stack([np.eye(128, dtype=f), tri, 1.0 - tri, np.ones((128, 128), f)], axis=1)).astype(NPBF)
    negc = np.stack([np.tile(-30000.0 * (1.0 - tri), (1, 4)), np.tile(-30000.0 * tri, (1, 4))], axis=1)
    sh["negc"] = np.ascontiguousarray(negc).astype(NPBF)
    eec = np.zeros((128, 32, 128), f)
    for r in range(2):
        for j in range(32):
            for hf in range(2):
                eec[64 * r + 2 * j + hf, j, hf * 64:(hf + 1) * 64] = 1.0
    sh["eec"] = eec.astype(NPBF)
    n = np.arange(cfg.NCT * 128)
    blk = np.arange(cfg.NBLK)
    cover = ((16 * n[:, None] <= 64 * blk[None, :] + 63) & (16 * n[:, None] + 31 >= 64 * blk[None, :])).astype(f)
    cover[n >= cfg.NCMP - 1] = 0.0
    c1 = np.concatenate([cover, np.ones((cfg.NCT * 128, 1), f)], axis=1)
    sh["cover1"] = np.ascontiguousarray(c1.reshape(cfg.NCT, 128, cfg.NBLK + 1).transpose(1, 0, 2)).astype(NPBF)
    return sh


def prep_core(cfg, inp, sh, c):
    f = np.float32
    b, j = c // 4, c % 4
    off = (3 - j) * cfg.CH
    m = dict(sh)
    xv = np.zeros((cfg.SEQ, D), f)
    xv[off:] = np.asarray(inp["x"][b, 0:(j + 1) * cfg.CH], f)
    m["xv"] = xv
    m["memx"] = np.ascontiguousarray(np.asarray(inp["mem"][b], f))
    lay, ns = small_layout(cfg)
    sp = sh["_small"].copy()
    o, n = lay["gflag"]
    sp[:, o:o + n] = (np.arange(cfg.NG) * GT >= off).astype(f)[None, :]
    o, n = lay["fbias"]
    fb = np.zeros(cfg.NBLK, f)
    fb[off // 64] = 1e9
    sp[:, o:o + n] = fb[None, :]
    m["smallp"] = sp
    del m["_small"]
    cm = np.zeros((cfg.NOT, 128, cfg.NCT, 128), f)
    for ti in range(cfg.NOT):
        vq = cfg.T_HALO + ti
        t = 128 * vq + np.arange(128)
        for nt in range(cfg.NCT):
            ng = nt * 128 + np.arange(128)
            ok = (16 * ng[:, None] + 31 <= t[None, :]) & (16 * ng[:, None] >= off) & (ng[:, None] < cfg.NCMP - 1)
            cm[ti, :, nt, :] = ok
    m["cmask"] = cm.astype(NPBF)
    return m


P_WIN, P_WOUT, P_WUP, P_WDN, P_W1K, P_W1V, NPIECE = 0, 30, 46, 134, 178, 180, 182
NB_POOL = 5
LOOKAHEAD = 4


class DryTracker(Tracker):
    def __init__(self):
        self.n_wait = 0
        self.n_ins = 0

    def op(self, en, fn, reads=(), writes=()):
        return None

    def dma(self, qn, out, in_, reads=(), writes=(), slots="main", **kw):
        return None

    def wait_events(self, en, events):
        pass

    def barrier(self, names=()):
        pass


class Prog:
    def __init__(self, cfg):
        self.cfg = cfg
        self.nc = bass.Bass("TRN2", target_bir_lowering=False)
        self.alloc()

    def sb(self, name, shape, dt):
        return self.nc.alloc_sbuf_tensor(name, list(shape), dt)

    def din(self, name, shape, dt):
        return self.nc.dram_tensor(name, list(shape), dt, kind="ExternalInput").ap()

    def alloc(self):
        cfg, nc = self.cfg, self.nc
        lay, ns = small_layout(cfg)
        self.lay = lay
        self.xv = self.din("xv", [cfg.SEQ, D], F32)
        self.memx = self.din("memx", [256, D], F32)
        self.w_in_t = self.din("w_in_t", [30, 128, KC, 128], F32)
        self.w_gate = self.din("w_gate", [128, KC, 24], F32)
        self.w_up_t = self.din("w_up_t", [88, 128, KC, 128], F32)
        self.w_out_t = self.din("w_out_t", [4, 4, 128, 4, 512], F32)
        self.w_dn_t = self.din("w_dn_t", [4, 11, 128, 4, 512], F32)
        self.w_mk_t = self.din("w_mk_t", [4, 128, KC, 128], F32)
        self.w_mv_t = self.din("w_mv_t", [4, 128, 4, 512], F32)
        self.w_ax = self.din("w_ax", [128, 2, 8, 128], F32)
        self.w1k = self.din("w1k", [2, 128, 16, 128], F32)
        self.w1v = self.din("w1v", [2, 128, 16, 128], F32)
        self.smallp = self.din("smallp", [128, ns], F32)
        self.gb = self.din("gb", [3, 1, 2 * D], F32)
        self.ident_f_d = self.din("ident_f", [128, 128], F32)
        self.masks_b_d = self.din("masks_b", [128, 4, 128], BF16)
        self.eec_d = self.din("eec", [128, 32, 128], BF16)
        self.negc_d = self.din("negc", [128, 2, 512], BF16)
        self.cover1_d = self.din("cover1", [128, cfg.NCT, cfg.NBLK + 1], BF16)
        self.cmask_d = self.din("cmask", [cfg.NOT, 128, cfg.NCT, 128], BF16)
        self.y = nc.dram_tensor("y", [cfg.CH, D], F32, kind="ExternalOutput").ap()
        self.wsc = nc.dram_tensor("wsc", [NPIECE, 128, 2048], BF16, kind="Internal").ap()
        if cfg.debug:
            self.d_mixed = nc.dram_tensor("d_mixed", [cfg.OWN_G + 1, 128, KC, 512], BF16, kind="ExternalOutput").ap()
            self.d_h1 = nc.dram_tensor("d_h1", [cfg.OWN_G + 1, 128, 4, D], F32, kind="ExternalOutput").ap()
            self.d_deriv = nc.dram_tensor("d_deriv", [128, 64], F32, kind="ExternalOutput").ap()
            self.d_hlru = nc.dram_tensor("d_hlru", [128, 8, 512], BF16, kind="ExternalOutput").ap()
            self.d_hT = nc.dram_tensor("d_hT", [128, KC, 512], BF16, kind="ExternalOutput").ap()
        self.SM = self.sb("SM", [128, ns], F32)
        self.ksT = self.sb("ksT", [128, cfg.SEQ], BF16)
        self.kwT = self.sb("kwT", [128, cfg.NWT * 128], BF16)
        self.V1s = self.sb("V1s", [128, cfg.NT, 2, 65], BF16)
        self.V1w = self.sb("V1w", [128, cfg.NWT, 2, 65], BF16)
        self.eec = self.sb("eec_s", [128, 32, 128], BF16)
        self.kcmpT = self.sb("kcmpT", [128, cfg.NCT * 128], BF16)
        self.vcmp1 = self.sb("vcmp1", [128, cfg.NCT, 2, 65], BF16)
        self.cover1 = self.sb("cover1_s", [128, cfg.NCT, cfg.NBLK + 1], BF16)
        self.kmemT = self.sb("kmemT", [128, 4, 256], BF16)
        self.vmem = self.sb("vmem", [128, 2, 4, 128], BF16)
        self.ones_bf = self.sb("ones_bf", [128, 128], BF16)
        self.masks = self.sb("masks_s", [128, 4, 128], BF16)
        self.ident_f = self.sb("ident_fs", [128, 128], F32)
        self.wg_bf = self.sb("wg_bf", [128, KC, 24], BF16)
        self.wax_sb = self.sb("wax_sb", [128, 2, 8, 128], BF16)
        self.smb = self.sb("smb", [128, 320], BF16)
        self.negc = self.sb("negc_s", [128, 2, 512], BF16)
        self.negm2 = self.sb("negm2", [128, 2, 512], BF16)
        self.vcmpT = self.sb("vcmpT", [128, cfg.NCT * 128], BF16)
        self.deriv = self.sb("deriv", [128, 64], F32)
        self.state = self.sb("lru_state", [128, 8], F32)
        self.xtail = self.sb("xtail", [128, 8, 3], F32)
        self.ftail = self.sb("ftail", [128, 88, 2], F32)
        self.kc_loc = self.sb("kc_loc", [128, 528], BF16)
        self.vc_loc = self.sb("vc_loc", [128, 528], BF16)
        self.hflag = self.sb("hflag", [128, cfg.NG], F32)
        self.xt = self.sb("xt", [128, 4, D], F32)
        self.bufM = self.sb("bufM", [128, KC, 512], BF16)
        self.hT = self.sb("hT", [128, KC, 512], BF16)
        self.bufQ = self.sb("bufQ", [128, 8, 512], BF16)
        self.bufA = self.sb("bufA", [128, 16, 512], BF16)
        self.wpool = self.sb("wpool", [128, NB_POOL, 2048], BF16)
        self.g_sb = self.sb("g_sb", [128, 4, 24], F32)
        self.stats = self.sb("stats", [128, 4, 4, 6], F32)
        self.mv = self.sb("mv", [128, 4, 2], F32)
        self.lnt = self.sb("lnt", [128, 3, 4], F32)
        self.bufS = self.sb("bufS", [128, 5248], F32)
        self.ps = [nc.alloc_psum_tensor("ps%d" % i, [128, 512], F32) for i in range(8)]
        print("sbuf bytes remaining/partition:", nc.sbuf_bytes_remaining)

    def mk_regions(self):
        cfg = self.cfg
        R = lambda n, excl=False: Region(n, excl)
        self.R_sm = R("sm")
        self.R_const = R("const")
        self.R_ks = [R("ks%d" % i) for i in range(cfg.NT)]
        self.R_kw = [R("kw%d" % i) for i in range(cfg.NWT)]
        self.R_vs = [R("vs%d" % i) for i in range(cfg.NT)]
        self.R_vw = [R("vw%d" % i) for i in range(cfg.NWT)]
        self.R_kcmp = R("kcmp")
        self.R_vcmp = R("vcmp")
        self.R_kmem = R("kmem")
        self.R_vmem = R("vmem")
        self.R_wg = R("wg")
        self.R_negm2 = R("negm2")
        self.R_wax = R("wax")
        self.R_smb = R("smb")
        self.R_deriv = R("deriv")
        self.R_state = [R("state%d" % i) for i in range(8)]
        self.R_xtail = [R("xtail%d" % i) for i in range(8)]
        self.R_ftail = [R("ftail%d" % i) for i in range(88)]
        self.R_kcl = R("kc_loc")
        self.R_vcl = R("vc_loc")
        self.R_xt = [R("xt%d" % i) for i in range(4)]
        self.R_M = [R("M%d" % i) for i in range(KC)]
        self.R_hT = [R("hT%d" % i) for i in range(KC)]
        self.R_Q = [R("Q%d" % i) for i in range(8)]
        self.R_A = [R("A%d" % i) for i in range(16)]
        self.R_wp = [R("wp%d" % i) for i in range(NB_POOL)]
        self.R_wsc = [R("wsc%d" % i) for i in range(NPIECE)]
        self.R_g = R("g_sb")
        self.R_stats = [R("stats%d" % i) for i in range(4)]
        self.R_mv = R("mv")
        self.R_lnt = R("lnt")
        self.R_S = {}
        self.R_ps = [R("ps%d" % i, True) for i in range(8)]
        self.R_y = R("y")
        self.R_dbg = R("dbg")

    def RS(self, name):
        if name not in self.R_S:
            self.R_S[name] = Region("S_" + name)
        return self.R_S[name]

    def sm(self, name, i=None, n=1):
        o, cnt = self.lay[name]
        if i is None:
            return self.SM[:, o:o + cnt]
        return self.SM[:, o + i:o + i + n]

    def piece_src(self, pid):
        if pid < P_WOUT:
            return self.w_in_t[pid].rearrange("p a b -> p (a b)")
        if pid < P_WUP:
            k = pid - P_WOUT
            return self.w_out_t[k // 4, k % 4].rearrange("p a b -> p (a b)")
        if pid < P_WDN:
            return self.w_up_t[pid - P_WUP].rearrange("p a b -> p (a b)")
        if pid < P_W1K:
            k = pid - P_WDN
            return self.w_dn_t[k // 11, k % 11].rearrange("p a b -> p (a b)")
        if pid < P_W1V:
            return self.w1k[pid - P_W1K].rearrange("p a b -> p (a b)")
        return self.w1v[pid - P_W1V].rearrange("p a b -> p (a b)")

    def stage_cast(self, src_ap, dst_ap, dst_regions, nelem=2048, eng="pool"):
        self.tr.dma("pool", dst_ap, src_ap, writes=dst_regions, slots="cast")

    def issue_casts(self, n):
        if self.dry:
            return
        while n > 0 and self.cast_ptr < len(self.cast_order):
            pid = self.cast_order[self.cast_ptr]
            self.cast_ptr += 1
            self.tr.dma("pool", self.wsc[pid], self.piece_src(pid), writes=[self.R_wsc[pid]], slots="cast")
            self.cast_done.add(pid)
            n -= 1

    def _issue_fetch(self, pid):
        tr = self.tr
        b = self.pool_rr
        self.pool_rr = (self.pool_rr + 1) % NB_POOL
        buf = self.wpool[:, b, :]
        while pid not in self.cast_done:
            self.issue_casts(1)
        tr.dma("sp", buf, self.wsc[pid], reads=[self.R_wsc[pid]], writes=[self.R_wp[b]])
        return b

    def fetch(self, pid):
        if self.dry:
            self.order.append(pid)
            return self.wpool[:, 0, :], self.R_wp[0]
        assert self.order[self.optr] == pid, (self.optr, self.order[self.optr], pid)
        while self.issued < min(len(self.order), self.optr + LOOKAHEAD + 1):
            self.inflight[self.issued] = self._issue_fetch(self.order[self.issued])
            self.issued += 1
        b = self.inflight.pop(self.optr)
        self.optr += 1
        return self.wpool[:, b, :], self.R_wp[b]

    def bank_mm(self):
        i = self.mm_rr % self.mm_nb
        self.mm_rr = (i + 1) % self.mm_nb
        return self.ps[i], self.R_ps[i]

    def bfv(self, bank):
        return bank[:, :].bitcast(BF16)

    def emit(self, dry):
        self.dry = dry
        self.mk_regions()
        if dry:
            self.tr = DryTracker()
            self.order = []
        else:
            self.tr = Tracker(self.nc)
            self.optr = 0
            self.issued = 0
            self.inflight = {}
        self.cast_done = set()
        self.cast_ptr = 0
        if not dry:
            seen = set()
            self.cast_order = [p for p in self.order if not (p in seen or seen.add(p))]
        self.pool_rr = 0
        self.stage_rr = 0
        self.mm_rr = 0
        self.mm_nb = 4
        self.x_loaded = set()
        self.a1_done = set()
        cfg = self.cfg
        self.setup()
        for G in range(cfg.NG):
            self.group(G)
        if not dry:
            tr = self.tr
            tr.wait_events("sp", [s[3] for s in tr.dma_sems])
            tr.wait_events("pool", [s[3] for s in tr.dma_sems])
            print("instructions:", tr.n_ins, "waits:", tr.n_wait)

    def setup(self):
        cfg, tr = self.cfg, self.tr
        SM = self.SM
        tr.dma("sp", SM[:, :], self.smallp[:, :], writes=[self.R_sm])
        tr.dma("sp", self.ident_f[:, :], self.ident_f_d[:, :], writes=[self.R_const])
        tr.dma("sp", self.masks[:, :, :], self.masks_b_d[:, :, :], writes=[self.R_const])
        tr.dma("sp", self.eec[:, :, :], self.eec_d[:, :, :], writes=[self.R_const])
        tr.dma("sp", self.negc[:, :, :], self.negc_d[:, :, :], writes=[self.R_const])
        tr.dma("sp", self.cover1[:, :, :], self.cover1_d[:, :, :], writes=[self.R_const])
        dv = self.deriv
        RD = self.R_deriv
        x_ = dv[:, 32:40]
        xs = dv[:, 40:48]
        t_ = dv[:, 48:56]
        yl = dv[:, 56:64]
        tr.op("act", lambda e: e.activation(out=x_, in_=self.sm("lam"), func=AF.Exp, scale=-1.0), reads=[self.R_sm], writes=[RD])
        tr.op("act", lambda e: e.activation(out=yl, in_=x_, func=AF.Ln, bias=1.0), reads=[RD], writes=[RD])
        tr.op("dve", lambda e: e.tensor_scalar(out=xs, in0=x_, scalar1=0.05, scalar2=None, op0=ALU.min), reads=[RD], writes=[RD])
        tr.op("dve", lambda e: e.tensor_scalar(out=t_, in0=xs, scalar1=-0.25, scalar2=1.0 / 3.0, op0=ALU.mult, op1=ALU.add), reads=[RD], writes=[RD])
        tr.op("dve", lambda e: e.tensor_tensor(out=t_, in0=t_, in1=xs, op=ALU.mult), reads=[RD], writes=[RD])
        tr.op("dve", lambda e: e.tensor_scalar(out=t_, in0=t_, scalar1=-0.5, scalar2=None, op0=ALU.add), reads=[RD], writes=[RD])
        tr.op("dve", lambda e: e.tensor_tensor(out=t_, in0=t_, in1=xs, op=ALU.mult), reads=[RD], writes=[RD])
        tr.op("dve", lambda e: e.tensor_scalar(out=t_, in0=t_, scalar1=1.0, scalar2=None, op0=ALU.add), reads=[RD], writes=[RD])
        tr.op("dve", lambda e: e.tensor_tensor(out=t_, in0=t_, in1=xs, op=ALU.mult), reads=[RD], writes=[RD])
        tr.op("dve", lambda e: e.tensor_tensor(out=t_, in0=t_, in1=yl, op=ALU.subtract), reads=[RD], writes=[RD])
        tr.op("dve", lambda e: e.tensor_scalar(out=xs, in0=x_, scalar1=0.05, scalar2=None, op0=ALU.is_lt), reads=[RD], writes=[RD])
        tr.op("dve", lambda e: e.tensor_tensor(out=t_, in0=t_, in1=xs, op=ALU.mult), reads=[RD], writes=[RD])
        tr.op("dve", lambda e: e.tensor_tensor(out=t_, in0=t_, in1=yl, op=ALU.add), reads=[RD], writes=[RD])
        tr.op("dve", lambda e: e.tensor_scalar(out=dv[:, 0:8], in0=t_, scalar1=-4.0, scalar2=None, op0=ALU.mult), reads=[RD], writes=[RD])
        tr.op("dve", lambda e: e.tensor_scalar(out=dv[:, 8:16], in0=self.sm("ba"), scalar1=0.5, scalar2=None, op0=ALU.mult), reads=[self.R_sm, RD], writes=[RD])
        tr.op("dve", lambda e: e.tensor_scalar(out=dv[:, 16:24], in0=self.sm("bx"), scalar1=0.5, scalar2=None, op0=ALU.mult), reads=[self.R_sm, RD], writes=[RD])
        tr.op("dve", lambda e: e.tensor_scalar(out=self.hflag[:, :], in0=self.sm("gflag"), scalar1=0.5, scalar2=None, op0=ALU.mult), reads=[self.R_sm, RD], writes=[RD])
        tr.op("dve", lambda e: e.memset(dv[:, 26:27], 1.0), reads=[RD], writes=[RD])
        tr.op("dve", lambda e: e.memset(dv[:, 27:28], LN_EPS), reads=[RD], writes=[RD])
        self.one_ap = dv[:, 26:27]
        self.eps_ap = dv[:, 27:28]
        self.phaseA1(0)
        o = self.lay["w2k"][0]
        tr.op("pool", lambda e: e.tensor_copy(out=self.smb[:, :], in_=SM[:, o:o + 320]), reads=[self.R_sm], writes=[self.R_smb])
        tr.op("pool", lambda e: e.memset(self.ones_bf[:, :], 1.0), writes=[self.R_const])
        for c in range(8):
            tr.op("pool", lambda e: e.memset(self.state[:, c:c + 1], 0.0), writes=[self.R_state[c]])
            tr.op("pool", lambda e: e.memset(self.xtail[:, c, :], 0.0), writes=[self.R_xtail[c]])
        for c in range(88):
            tr.op("pool", lambda e: e.memset(self.ftail[:, c, :], 0.0), writes=[self.R_ftail[c]])
        tr.op("pool", lambda e: e.memset(self.kc_loc[:, :], 0.0), writes=[self.R_kcl])
        tr.op("pool", lambda e: e.memset(self.vc_loc[:, :], 0.0), writes=[self.R_vcl])
        tr.op("pool", lambda e: e.memset(self.vcmp1[:, :, :, :], 0.0), writes=[self.R_vcmp])
        tr.op("pool", lambda e: e.memset(self.vcmp1[:, :, :, 64:65], 1.0), writes=[self.R_vcmp])
        tr.op("pool", lambda e: e.memset(self.negm2[:, :, :], 0.0), writes=[self.R_negm2])
        tr.op("pool", lambda e: e.memset(self.kcmpT[:, :], 0.0), writes=[self.R_kcmp])
        tr.op("pool", lambda e: e.memset(self.vcmpT[:, :], 0.0), writes=[self.R_vcmp])
        self.issue_casts(20)
        self.stage_cast(self.w_gate.rearrange("p a b -> p (a b)"), self.wg_bf[:, :, :].rearrange("p a b -> p (a b)"),
                        [self.R_wg], nelem=KC * 24)
        self.stage_cast(self.w_ax.rearrange("p a b c -> p (a b c)"), self.wax_sb[:, :, :, :].rearrange("p a b c -> p (a b c)"),
                        [self.R_wax])
        self.mem_kv()
        for kind, pid, pcol, dcol in (("k", P_W1K, 256, 24), ("v", P_W1V, 288, 25)):
            bank, rbk = self.bank_mm()
            pe2 = self.bufS[:, 64 * (dcol - 24):64 * (dcol - 24) + 64].bitcast(BF16)
            R_pe2 = self.RS("pe2%d" % dcol)
            tr.op("pool", lambda e: e.tensor_copy(out=pe2[:, 0:64].rearrange("p (l t) -> p l t", t=2),
                                                  in_=self.smb[:, pcol:pcol + 32].unsqueeze(2).broadcast_to([128, 32, 2])),
                  reads=[self.R_smb], writes=[R_pe2])
            for half in range(2):
                buf, rb = self.fetch(pid + half)
                w1 = buf.rearrange("p (l e) -> p l e", e=128)
                for li in range(16):
                    l = half * 16 + li
                    tr.op("pe", lambda e: e.matmul(bank[:, 0:2], lhsT=w1[:, li, :], rhs=pe2[:, 2 * l:2 * l + 2],
                                                   start=(l == 0), stop=(l == 31)),
                          reads=[rb, R_pe2], writes=[rbk])
            tr.op("dve", lambda e: e.tensor_copy(out=dv[:, dcol:dcol + 1], in_=bank[:, 0:1]), reads=[rbk, RD], writes=[RD])
        tr.barrier()

    def mem_kv(self):
        cfg, tr = self.cfg, self.tr
        memT = self.hT
        mbv = self.bufQ[:, :, :].rearrange("p a b -> p (a b)").rearrange("p (t d) -> p t d", d=D)
        for mt in range(2):
            mf = self.bufS[:, 512 + mt * D:512 + (mt + 1) * D]
            R_mf = self.RS("memf%d" % mt)
            tr.dma("sp", mf, self.memx[mt * 128:(mt + 1) * 128, :], writes=[R_mf])
            tr.op("pool", lambda e: e.tensor_copy(out=mbv[:, mt, :], in_=mf), reads=[R_mf], writes=self.R_Q[4 * mt:4 * mt + 4])
        for kc in range(KC):
            bank, rbk = self.bank_mm()
            bb = self.bfv(bank)
            for mt in range(2):
                tr.op("pe", lambda e: e.transpose(out=bb[:, mt * 128:(mt + 1) * 128], in_=mbv[:, mt, kc * 128:(kc + 1) * 128],
                                                  identity=self.masks[:, 0, :]),
                      reads=self.R_Q[4 * mt:4 * mt + 4] + [self.R_const], writes=[rbk])
            tr.op("dve", lambda e: e.tensor_copy(out=memT[:, kc, 0:256], in_=bb[:, 0:256]), reads=[rbk], writes=[self.R_hT[kc]])
        for h in range(4):
            wb = self.bufA[:, 0:4, :].rearrange("p a b -> p (a b)")
            self.stage_cast(self.w_mk_t[h].rearrange("p a b -> p (a b)"), wb, self.R_A[0:4])
            wv = wb.rearrange("p (a b) -> p a b", b=128)
            bank, rbk = self.bank_mm()
            for kc in range(KC):
                tr.op("pe", lambda e: e.matmul(bank[:, 0:256], lhsT=wv[:, kc, :], rhs=memT[:, kc, 0:256], start=(kc == 0), stop=(kc == KC - 1)),
                      reads=self.R_A[0:4] + [self.R_hT[kc]], writes=[rbk])
            tr.op("act", lambda e: e.copy(out=self.kmemT[:, h, :], in_=bank[:, 0:256]), reads=[rbk], writes=[self.R_kmem])
        banks = [self.bank_mm() for _ in range(2)]
        for kq in range(4):
            wb = self.bufA[:, 4:8, :].rearrange("p a b -> p (a b)")
            self.stage_cast(self.w_mv_t[kq].rearrange("p a b -> p (a b)"), wb, self.R_A[4:8])
            wv = wb.rearrange("p (a b) -> p a b", b=512)
            for mt in range(2):
                bank, rbk = banks[mt]
                for kci in range(4):
                    kc = kq * 4 + kci
                    tr.op("pe", lambda e: e.matmul(bank[:, 0:512], lhsT=memT[:, kc, mt * 128:(mt + 1) * 128], rhs=wv[:, kci, :],
                                                   start=(kc == 0), stop=(kc == KC - 1)),
                          reads=self.R_A[4:8] + [self.R_hT[kc]], writes=[rbk])
        for mt in range(2):
            bank, rbk = banks[mt]
            tr.op("act", lambda e: e.copy(out=self.vmem[:, mt, :, :].rearrange("p h d -> p (h d)"), in_=bank[:, 0:512]),
                  reads=[rbk], writes=[self.R_vmem])

    def ln_stats(self, tiles):
        tr = self.tr
        for t in tiles:
            for k in range(4):
                tr.op("dve", lambda e: e.bn_stats(out=self.stats[:, t, k, :], in_=self.xt[:, t, k * 512:(k + 1) * 512]),
                      reads=[self.R_xt[t]], writes=[self.R_stats[t]])
            tr.op("dve", lambda e: e.bn_aggr(out=self.mv[:, t, :], in_=self.stats[:, t, :, :].rearrange("p a b -> p (a b)")),
                  reads=[self.R_stats[t]], writes=[self.R_mv])
        t0, t1 = tiles[0], tiles[-1] + 1
        tr.op("act", lambda e: e.activation(out=self.lnt[:, 0, t0:t1], in_=self.mv[:, t0:t1, 1], func=AF.Sqrt, bias=self.eps_ap, scale=1.0),
              reads=[self.R_mv, self.R_lnt, self.R_deriv], writes=[self.R_lnt])
        tr.op("dve", lambda e: e.reciprocal(out=self.lnt[:, 1, t0:t1], in_=self.lnt[:, 0, t0:t1]), reads=[self.R_lnt], writes=[self.R_lnt])
        tr.op("dve", lambda e: e.scalar_tensor_tensor(out=self.lnt[:, 2, t0:t1], in0=self.mv[:, t0:t1, 0], scalar=-1.0,
                                                       in1=self.lnt[:, 1, t0:t1], op0=ALU.mult, op1=ALU.mult),
              reads=[self.R_mv, self.R_lnt], writes=[self.R_lnt])

    def load_gb(self, which):
        gbuf = self.bufA[:, :, :].rearrange("p a b -> p (a b)").bitcast(F32)
        self.tr.dma("sp", gbuf, self.gb[which, 0:1, :].partition_broadcast(128), writes=self.R_A[0:16])
        return gbuf

    def ln_apply(self, t, gbuf, g_name, b_name, want_tok, want_T=True):
        tr = self.tr
        xnv = self.bufM[:, :, :].rearrange("p a b -> p (a b)").rearrange("p (t d) -> p t d", d=D)
        if want_T:
            tr.op("act", lambda e: e.activation(out=xnv[:, t, :], in_=self.xt[:, t, :], func=AF.Identity,
                                                scale=self.lnt[:, 1, t:t + 1], bias=self.lnt[:, 2, t:t + 1]),
                  reads=[self.R_xt[t], self.R_lnt], writes=self.R_M[4 * t:4 * t + 4])
        if want_tok:
            tr.op("dve", lambda e: e.scalar_tensor_tensor(out=self.xt[:, t, :], in0=self.xt[:, t, :], scalar=self.mv[:, t, 0:1], in1=gbuf[:, 0:D],
                                                           op0=ALU.subtract, op1=ALU.mult),
                  reads=[self.R_xt[t], self.R_mv] + self.R_A[0:8], writes=[self.R_xt[t]])
            tr.op("dve", lambda e: e.scalar_tensor_tensor(out=self.xt[:, t, :], in0=self.xt[:, t, :], scalar=self.lnt[:, 1, t:t + 1], in1=gbuf[:, D:2 * D],
                                                           op0=ALU.mult, op1=ALU.add),
                  reads=[self.R_xt[t], self.R_lnt] + self.R_A[8:16], writes=[self.R_xt[t]])

    def ln_transpose(self, tiles, g_name, b_name):
        tr = self.tr
        xnv = self.bufM[:, :, :].rearrange("p a b -> p (a b)").rearrange("p (t d) -> p t d", d=D)
        c0, c1 = tiles[0] * 128, (tiles[-1] + 1) * 128
        for kc in range(KC):
            bank, rbk = self.bank_mm()
            bb = self.bfv(bank)
            for t in tiles:
                tr.op("pe", lambda e: e.transpose(out=bb[:, t * 128:(t + 1) * 128], in_=xnv[:, t, kc * 128:(kc + 1) * 128],
                                                  identity=self.masks[:, 0, :]),
                      reads=self.R_M[4 * t:4 * t + 4] + [self.R_const], writes=[rbk])
            if kc % 2 == 0:
                tr.op("act", lambda e: e.activation(out=self.hT[:, kc, c0:c1], in_=bb[:, c0:c1], func=AF.Identity,
                                                    scale=self.sm(g_name, kc), bias=self.sm(b_name, kc)),
                      reads=[rbk, self.R_sm], writes=[self.R_hT[kc]])
            else:
                tr.op("dve", lambda e: e.tensor_scalar(out=self.hT[:, kc, c0:c1], in0=bb[:, c0:c1], scalar1=self.sm(g_name, kc),
                                                        scalar2=self.sm(b_name, kc), op0=ALU.mult, op1=ALU.add),
                      reads=[rbk, self.R_sm], writes=[self.R_hT[kc]])

    def proj(self, pid, c0, c1):
        tr = self.tr
        buf, rb = self.fetch(pid)
        wv = buf.rearrange("p (a b) -> p a b", b=128)
        bank, rbk = self.bank_mm()
        n = c1 - c0
        for kc in range(KC):
            tr.op("pe", lambda e: e.matmul(bank[:, 0:n], lhsT=wv[:, kc, :], rhs=self.hT[:, kc, c0:c1], start=(kc == 0), stop=(kc == KC - 1)),
                  reads=[rb, self.R_hT[kc]], writes=[rbk])
        return bank, rbk

    def phaseA1(self, G):
        tr = self.tr
        for t in range(4):
            if (G, t) not in self.x_loaded:
                tok0 = G * GT + t * 128
                tr.dma("sp", self.xt[:, t, :], self.xv[tok0:tok0 + 128, :], writes=[self.R_xt[t]])
                self.x_loaded.add((G, t))
        self.ln_stats([0, 1, 2, 3])
        for t in range(4):
            self.ln_apply(t, None, "g_in", "b_in", want_tok=False)
        self.a1_done.add(G)

    def group(self, G):
        cfg, tr = self.cfg, self.tr
        own = G >= cfg.G0
        halo = G == cfg.G0 - 1
        TR = (0, 4) if own else ((3, 4) if halo else None)
        if not self.dry:
            self.issue_casts((len(self.cast_order) + cfg.G0 - 2) // max(1, cfg.G0 - 1))
        self.mm_nb = 8
        if G not in self.a1_done:
            self.phaseA1(G)
        self.ln_transpose([0, 1, 2, 3], "g_in", "b_in")
        if G + 1 < cfg.NG:
            free_t = [0, 1, 2, 3] if TR is None else ([0, 1, 2] if TR == (3, 4) else [])
            for t in free_t:
                tok0 = (G + 1) * GT + t * 128
                tr.dma("sp", self.xt[:, t, :], self.xv[tok0:tok0 + 128, :], writes=[self.R_xt[t]])
                self.x_loaded.add((G + 1, t))
        if cfg.debug and G == cfg.G0:
            tr.dma("pool", self.d_hT, self.hT[:, :, :], reads=self.R_hT, writes=[self.R_dbg])
        self.lru(G, TR)
        if cfg.debug and G == cfg.G0:
            tr.dma("pool", self.d_hlru, self.bufQ[:, :, :], reads=self.R_Q, writes=[self.R_dbg])
            tr.dma("pool", self.d_deriv, self.deriv[:, :], reads=[self.R_deriv], writes=[self.R_dbg])
        self.kv(G)
        self.compress(G)
        if TR is None:
            return
        self.mm_nb = 4
        c0, c1 = TR[0] * 128, TR[1] * 128
        qT = self.bufQ
        for c in range(4):
            bank, rbk = self.proj(P_WIN + 16 + c, c0, c1)
            tr.op("act", lambda e: e.activation(out=qT[:, c, c0:c1], in_=bank[:, 0:c1 - c0], func=AF.Copy, scale=0.125),
                  reads=[rbk], writes=[self.R_Q[c]])
        for h in range(4):
            bank, rbk = self.proj(P_WIN + 26 + h, c0, c1)
            tr.op("act", lambda e: e.activation(out=qT[:, 4 + h, c0:c1], in_=bank[:, 0:c1 - c0], func=AF.Copy, scale=128.0 ** -0.5),
                  reads=[rbk], writes=[self.R_Q[4 + h]])
        bank, rbk = self.bank_mm()
        for t in range(TR[0], TR[1]):
            for kc in range(KC):
                tr.op("pe", lambda e: e.matmul(bank[:, t * 24:(t + 1) * 24], lhsT=self.hT[:, kc, t * 128:(t + 1) * 128], rhs=self.wg_bf[:, kc, :],
                                               start=(kc == 0), stop=(kc == KC - 1)),
                      reads=[self.R_hT[kc], self.R_wg], writes=[rbk])
        gs = self.g_sb[:, :, :].rearrange("p a b -> p (a b)")
        tr.op("act", lambda e: e.activation(out=gs[:, TR[0] * 24:TR[1] * 24], in_=bank[:, TR[0] * 24:TR[1] * 24], func=AF.Tanh, scale=0.5),
              reads=[rbk], writes=[self.R_g])
        tr.op("dve", lambda e: e.tensor_scalar(out=gs[:, TR[0] * 24:TR[1] * 24], in0=gs[:, TR[0] * 24:TR[1] * 24], scalar1=0.5, scalar2=0.5,
                                                op0=ALU.mult, op1=ALU.add),
              reads=[self.R_g], writes=[self.R_g])
        tr.barrier()
        self.mm_nb = 3
        tr.op("dve", lambda e: e.memset(self.bufA[:, 6:8, :], 0.0), writes=self.R_A[6:8])
        units = [(t, g) for t in range(TR[0], TR[1]) for g in range(2)]
        U = [self.nsa_unit(G, t, g, k) for k, (t, g) in enumerate(units)]
        U[0]["h0"]()
        U[0]["h1"]()
        U[0]["h2"]()
        for k in range(len(units)):
            n = U[k]["n"]
            inj = {}
            if k >= 1:
                inj.setdefault(min(6, n - 1), []).append(U[k - 1]["EP"])
            if k + 1 < len(units):
                inj.setdefault(min(10, n - 1), []).append(U[k + 1]["h0"])
                inj.setdefault(min(16, n - 1), []).append(U[k + 1]["h1"])
                inj.setdefault(min(30, n + 1), []).append(U[k + 1]["h2"])
            U[k]["run"](inj)
        U[-1]["EP"]()
        self.mm_nb = 4
        self.mem_attn(c0, c1)
        if cfg.debug:
            gi = G - cfg.G0 + 1
            tr.dma("pool", self.d_mixed[gi], self.bufM[:, :, :], reads=self.R_M, writes=[self.R_dbg])
        tr.barrier()
        self.out_proj_ln1(G, TR)
        if cfg.debug:
            gi = G - cfg.G0 + 1
            tr.dma("pool", self.d_h1[gi], self.xt[:, :, :], reads=self.R_xt, writes=[self.R_dbg])
        self.ffn(G, TR)
        tr.barrier()

    def lru(self, G, TR):
        cfg, tr = self.cfg, self.tr
        need_h = TR is not None
        S = self.bufS
        dv = self.deriv
        xr = [S[:, b * 516:(b + 1) * 516] for b in range(2)]
        xc = [S[:, 1032 + b * 512:1032 + (b + 1) * 512] for b in range(2)]
        th = [S[:, 2056 + b * 512:2056 + (b + 1) * 512] for b in range(2)]
        hf = [S[:, 3080 + b * 512:3080 + (b + 1) * 512] for b in range(2)]
        xcb = [S[:, 4104 + b * 256:4104 + (b + 1) * 256].bitcast(BF16) for b in range(2)]
        gy = S[:, 4616:4872].bitcast(BF16)
        R_xr = [self.RS("xr%d" % b) for b in range(2)]
        R_xc = [self.RS("xc%d" % b) for b in range(2)]
        R_th = [self.RS("th%d" % b) for b in range(2)]
        R_hf = [self.RS("hf%d" % b) for b in range(2)]
        R_xcb = [self.RS("xcb%d" % b) for b in range(2)]
        R_gy = self.RS("gy")
        Af = self.bufA[:, :, :].rearrange("p a b -> p (a b)").bitcast(F32)
        A_a = [Af[:, s * 512:(s + 1) * 512] for s in range(2)]
        A_om = [Af[:, 1024 + s * 512:1024 + (s + 1) * 512] for s in range(2)]
        A_ix = [Af[:, 2048 + s * 512:2048 + (s + 1) * 512] for s in range(2)]
        RA_a = [self.R_A[2 * s:2 * s + 2] for s in range(2)]
        RA_om = [self.R_A[4 + 2 * s:4 + 2 * s + 2] for s in range(2)]
        RA_ix = [self.R_A[8 + 2 * s:8 + 2 * s + 2] for s in range(2)]
        wax, r_ax = self.wax_sb, self.R_wax
        gfl = self.sm("gflag", G)
        hfl = self.hflag[:, G:G + 1]
        lcw = lambda c, k: self.sm("lcw", c * 4 + k)
        w3g = S[:, 5160:5168]
        R_w3g = self.RS("w3g")
        o_l = self.lay["lcw"][0]
        tr.op("dve", lambda e: e.tensor_scalar(out=w3g, in0=self.SM[:, o_l + 3:o_l + 32:4], scalar1=gfl, scalar2=None, op0=ALU.mult),
              reads=[self.R_sm], writes=[R_w3g])
        def S1(bt):
            info = []
            for s in range(2):
                c = bt * 2 + s
                b = s
                bank, rbk = self.proj(P_WIN + c, 0, 512)
                info.append((c, b))
                tr.op("pool", lambda e: e.tensor_copy(out=xr[b][:, 0:3], in_=self.xtail[:, c, :]), reads=[self.R_xtail[c]], writes=[R_xr[b]])
                tr.op("act", lambda e: e.activation(out=xr[b][:, 3:515], in_=bank[:, 0:512], func=AF.Identity, scale=gfl),
                      reads=[rbk, self.R_sm], writes=[R_xr[b]])
                tr.op("pool", lambda e: e.tensor_copy(out=self.xtail[:, c, :], in_=xr[b][:, 512:515]), reads=[R_xr[b]], writes=[self.R_xtail[c]])
                tr.op("act", lambda e: e.activation(out=xc[b], in_=bank[:, 0:512], func=AF.Identity, scale=w3g[:, c:c + 1], bias=self.sm("lcb", c)),
                      reads=[rbk, self.R_sm, R_w3g], writes=[R_xc[b]])
            for (c, b) in info:
                for k in (2, 1, 0):
                    tr.op("dve", lambda e: e.scalar_tensor_tensor(out=xc[b], in0=xr[b][:, k:k + 512], scalar=lcw(c, k), in1=xc[b],
                                                                   op0=ALU.mult, op1=ALU.add),
                          reads=[R_xr[b], R_xc[b], self.R_sm], writes=[R_xc[b]])
                tr.op("act", lambda e: e.copy(out=xcb[b], in_=xc[b]), reads=[R_xc[b]], writes=[R_xcb[b]])

        def S2(bt):
            banks = []
            for s in range(2):
                c = bt * 2 + s
                b = s
                bank_r, rbr = self.bank_mm()
                tr.op("pe", lambda e: e.matmul(bank_r[:, 0:512], lhsT=wax[:, 0, c, :], rhs=xcb[b], start=True, stop=True),
                      reads=[r_ax, R_xcb[b]], writes=[rbr])
                bank_i, rbi = self.bank_mm()
                tr.op("pe", lambda e: e.matmul(bank_i[:, 0:512], lhsT=wax[:, 1, c, :], rhs=xcb[b], start=True, stop=True),
                      reads=[r_ax, R_xcb[b]], writes=[rbi])
                banks.append((bank_r, rbr, bank_i, rbi))
            for s in range(2):
                c = bt * 2 + s
                b = s
                bank_r, rbr, bank_i, rbi = banks[s]
                tr.op("act", lambda e: e.activation(out=th[b], in_=bank_r[:, 0:512], func=AF.Tanh, scale=0.5, bias=dv[:, 8 + c:9 + c]),
                      reads=[rbr, self.R_deriv], writes=[R_th[b]])
                tr.op("act", lambda e: e.activation(out=A_a[s], in_=th[b], func=AF.Exp, scale=dv[:, c:c + 1], bias=dv[:, c:c + 1]),
                      reads=[R_th[b], self.R_deriv], writes=RA_a[s])
                tr.op("act", lambda e: e.activation(out=th[b], in_=bank_i[:, 0:512], func=AF.Tanh, scale=0.5, bias=dv[:, 16 + c:17 + c]),
                      reads=[rbi, self.R_deriv], writes=[R_th[b]])
                tr.op("dve", lambda e: e.scalar_tensor_tensor(out=A_ix[s], in0=th[b], scalar=1.0, in1=xc[b], op0=ALU.add, op1=ALU.mult),
                      reads=[R_th[b], R_xc[b]], writes=RA_ix[s])
                tr.op("act", lambda e: e.activation(out=A_om[s], in_=A_a[s], func=AF.Square), reads=RA_a[s], writes=RA_om[s])

        def S3(bt):
            tr.op("act", lambda e: e.activation(out=Af[:, 1024:2048], in_=Af[:, 1024:2048], func=AF.Sqrt, scale=-1.0, bias=self.one_ap),
                  reads=self.R_A[4:8] + [self.R_deriv], writes=self.R_A[4:8])
            for s in range(2):
                c = bt * 2 + s
                b = s
                tr.op("dve", lambda e: e.scalar_tensor_tensor(out=A_ix[s], in0=A_ix[s], scalar=hfl, in1=A_om[s], op0=ALU.mult, op1=ALU.mult),
                      reads=RA_ix[s] + RA_om[s] + [self.R_deriv], writes=RA_ix[s])
                tr.op("dve", lambda e: e.tensor_tensor_scan(out=hf[b], data0=A_a[s], data1=A_ix[s], initial=self.state[:, c:c + 1],
                                                             op0=ALU.mult, op1=ALU.add),
                      reads=RA_a[s] + RA_ix[s] + [self.R_state[c]], writes=[R_hf[b]])
                tr.op("pool", lambda e: e.tensor_copy(out=self.state[:, c:c + 1], in_=hf[b][:, 511:512]), reads=[R_hf[b]], writes=[self.R_state[c]])
                if need_h:
                    tr.op("pool", lambda e: e.tensor_copy(out=self.bufQ[:, c, :], in_=hf[b]), reads=[R_hf[b]], writes=[self.R_Q[c]])

        S1(0)
        for bt in range(4):
            S2(bt)
            if bt < 3:
                S1(bt + 1)
            S3(bt)
        if need_h:
            c0, c1 = TR[0] * 128, TR[1] * 128
            n = c1 - c0
            for c in range(8):
                bank, rbk = self.proj(P_WIN + 8 + c, c0, c1)
                tr.op("act", lambda e: e.activation(out=gy[:, 0:n], in_=bank[:, 0:n], func=AF.Gelu_apprx_tanh), reads=[rbk], writes=[R_gy])
                tr.op("dve", lambda e: e.tensor_tensor(out=self.bufM[:, c, c0:c1], in0=gy[:, 0:n], in1=self.bufQ[:, c, c0:c1], op=ALU.mult),
                      reads=[R_gy, self.R_Q[c]], writes=[self.R_M[c]])

    def kv(self, G):
        cfg, tr = self.cfg, self.tr
        S = self.bufS
        vt = S[:, 4872:5128].bitcast(BF16)
        R_vt = self.RS("vt")
        gfl = self.sm("gflag", G)
        for pid, loc, rl in ((P_WIN + 20, self.kc_loc, self.R_kcl), (P_WIN + 21, self.vc_loc, self.R_vcl)):
            bank, rbk = self.proj(pid, 0, 512)
            tr.op("pool", lambda e: e.tensor_copy(out=loc[:, 0:16], in_=loc[:, 512:528]), reads=[rl], writes=[rl])
            tr.op("act", lambda e: e.copy(out=loc[:, 16:528], in_=bank[:, 0:512]), reads=[rbk], writes=[rl])
        bank, rbk = self.proj(P_WIN + 22, 0, 512)
        tr.op("act", lambda e: e.copy(out=self.ksT[:, G * GT:(G + 1) * GT], in_=bank[:, 0:512]), reads=[rbk], writes=self.R_ks[4 * G:4 * G + 4])

        def store_v(pid, V1, RV, tbase, ta):
            bank, rbk = self.proj(pid, 0, 512)
            tr.op("act", lambda e: e.activation(out=vt, in_=bank[:, 0:512], func=AF.Identity, scale=gfl), reads=[rbk, self.R_sm], writes=[R_vt])
            bank2, rb2 = self.bank_mm()
            bb = self.bfv(bank2)
            for t in range(ta, 4):
                tr.op("pe", lambda e: e.transpose(out=bb[:, t * 128:(t + 1) * 128], in_=vt[:, t * 128:(t + 1) * 128], identity=self.masks[:, 0, :]),
                      reads=[R_vt, self.R_const], writes=[rb2])
            nt_ = 4 - ta
            i0 = 4 * G + ta - tbase
            tr.op("dve", lambda e: e.tensor_copy(out=V1[:, i0:i0 + nt_, :, 0:64],
                                                 in_=bb[:, ta * 128:512].rearrange("p (t g d) -> p t g d", t=nt_, g=2)),
                  reads=[rb2], writes=RV[i0:i0 + nt_])
            tr.op("pool", lambda e: e.tensor_scalar(out=V1[:, i0:i0 + nt_, :, 64],
                                                     in0=self.ones_bf[:, 0:2 * nt_].rearrange("p (a b) -> p a b", a=nt_),
                                                     scalar1=gfl, scalar2=None, op0=ALU.mult),
                  reads=[self.R_const, self.R_sm], writes=RV[i0:i0 + nt_])

        store_v(P_WIN + 23, self.V1s, self.R_vs, 0, 0)
        if 4 * G + 3 >= cfg.WT0:
            ta = max(0, cfg.WT0 - 4 * G)
            bank, rbk = self.proj(P_WIN + 24, 0, 512)
            w0 = 4 * G + ta - cfg.WT0
            tr.op("act", lambda e: e.copy(out=self.kwT[:, w0 * 128:(w0 + 4 - ta) * 128], in_=bank[:, ta * 128:512]),
                  reads=[rbk], writes=self.R_kw[w0:w0 + 4 - ta])
            store_v(P_WIN + 25, self.V1w, self.R_vw, cfg.WT0, ta)

    def compress(self, G):
        cfg, tr = self.cfg, self.tr
        S = self.bufS
        dv = self.deriv
        m0 = 1 if G == 0 else 0
        col0 = 32 * G - 1 + m0
        ncol = 32 - m0
        for kind in range(2):
            pid = P_W1K if kind == 0 else P_W1V
            loc, rl = (self.kc_loc, self.R_kcl) if kind == 0 else (self.vc_loc, self.R_vcl)
            hid = S[:, 5128 + kind * 16:5128 + (kind + 1) * 16].bitcast(BF16)
            R_hid = self.RS("hid%d" % kind)
            bank, rbk = self.bank_mm()
            for half in range(2):
                buf, rb = self.fetch(pid + half)
                w1 = buf.rearrange("p (l e) -> p l e", e=128)
                for li in range(16):
                    l = half * 16 + li
                    tr.op("pe", lambda e: e.matmul(bank[:, 0:32], lhsT=w1[:, li, :], rhs=loc[:, l:l + 497:16], start=(l == 0), stop=(l == 31)),
                          reads=[rb, rl], writes=[rbk])
            tr.op("act", lambda e: e.activation(out=hid, in_=bank[:, 0:32], func=AF.Gelu_apprx_tanh, bias=dv[:, 24 + kind:25 + kind]),
                  reads=[rbk, self.R_deriv], writes=[R_hid])
            bank2, rb2 = self.bank_mm()
            tr.op("pe", lambda e: e.matmul(bank2[:, 0:32], lhsT=self.smb[:, 128 * kind:128 * kind + 128], rhs=hid, start=True, stop=True),
                  reads=[self.R_smb, R_hid], writes=[rb2])
            if kind == 0:
                tr.op("act", lambda e: e.copy(out=self.kcmpT[:, col0:col0 + ncol], in_=bank2[:, m0:32]), reads=[rb2], writes=[self.R_kcmp])
            else:
                tr.op("act", lambda e: e.copy(out=self.vcmpT[:, col0:col0 + ncol], in_=bank2[:, m0:32]), reads=[rb2], writes=[self.R_vcmp])
                for nt in sorted({col0 // 128, (col0 + ncol - 1) // 128}):
                    bankT, rbT = self.bank_mm()
                    bb = self.bfv(bankT)
                    tr.op("pe", lambda e: e.transpose(out=bb[:, 0:128], in_=self.vcmpT[:, nt * 128:(nt + 1) * 128], identity=self.masks[:, 0, :]),
                          reads=[self.R_vcmp, self.R_const], writes=[rbT])
                    tr.op("dve", lambda e: e.tensor_copy(out=self.vcmp1[:, nt, :, 0:64], in_=bb[:, 0:128].rearrange("p (g d) -> p g d", g=2)),
                          reads=[rbT], writes=[self.R_vcmp])

    def nsa_unit(self, G, t, g, uidx):
        cfg, tr = self.cfg, self.tr
        vq = 4 * G + t
        ti = vq - cfg.T_HALO
        S = self.bufS
        A = self.bufA
        NCT, NB = cfg.NCT, cfg.NBLK
        p = uidx % 2
        bf = lambda o, n: S[:, o:o + n].bitcast(BF16)
        RS = self.RS
        cm = bf(0, NCT * 64).rearrange("p (n q) -> p n q", q=128)
        R_cm = RS("cm")
        if p == 0:
            Pc = [bf(256 + n * 256, 256) for n in range(NCT)]
            R_Pc = [RS("Pc%d" % n) for n in range(NCT)]
            qz, R_qz = bf(2304, 256), RS("qz")
            negm, R_negm = self.negm2, [self.R_negm2]
            OT0, R_OT0 = S[:, 2904:2904 + 512], [RS("OT0")]
        else:
            Pc = [A[:, 2 + n, :] for n in range(NCT)]
            R_Pc = [self.R_A[2 + n] for n in range(NCT)]
            qz, R_qz = A[:, 1, :], self.R_A[1]
            negm, R_negm = A[:, 6:8, :], self.R_A[6:8]
            OT0, R_OT0 = A[:, 8:10, :].rearrange("p a b -> p (a b)").bitcast(F32), self.R_A[8:10]
        E4 = [bf(1280 + i * 256, 256) for i in range(3)] + [A[:, 0, :]]
        R_E4 = [RS("E%d" % i) for i in range(3)] + [self.R_A[0]]
        imp = S[:, 2560:2560 + NB]
        impw = S[:, 2688:2688 + NB]
        m8 = S[:, 2816:2832]
        zc = S[:, 2832:2836]
        rz = S[:, 2836:2840]
        sel = bf(2840, 64)[:, 0:NB]
        OT = [OT0, S[:, 2904 + 512:2904 + 1024], S[:, 2904 + 1024:2904 + 1536]]
        R_OT = [R_OT0, [RS("OT1")], [RS("OT2")]]
        etmp = S[:, 4440:4824]
        otok = bf(4824, 256)
        coef = S[:, 5080:5086]
        zz = S[:, 5086:5092]
        R_imp, R_m8, R_sel = RS("imp"), RS("m8"), RS("sel")
        R_et, R_otok, R_coef = RS("etmp"), RS("otok"), RS("coef")
        qT = self.bufQ
        tq = slice(t * 128, (t + 1) * 128)
        ident_b = self.masks[:, 0, :]
        v4 = lambda ap: ap.rearrange("p (h q) -> p h q", h=4)
        bc4 = lambda ap: ap.unsqueeze(1).broadcast_to([128, 4, 128])
        rq = self.R_Q[0:4]
        bOc, rOc = self.ps[4], self.R_ps[4]
        bOs, rOs = self.ps[5], self.R_ps[5]
        bI = [(self.ps[6], self.R_ps[6]), (self.ps[3], self.R_ps[3])]
        bOw, rOw = self.ps[7], self.R_ps[7]
        SKEW = 2

        def h0():
            if g == 0:
                tr.dma("sp", cm, self.cmask_d[ti], writes=[R_cm])
            tr.op("act", lambda e: e.activation(out=v4(qz), in_=qT[:, 0:4, tq], func=AF.Identity, scale=self.sm("hm", g)),
                  reads=rq + [self.R_sm], writes=[R_qz])
            for nt in range(NCT):
                bank, rbk = self.bank_mm()
                tr.op("pe", lambda e: e.matmul(bank[:, 0:512], lhsT=self.kcmpT[:, nt * 128:(nt + 1) * 128], rhs=qz, start=True, stop=True),
                      reads=[self.R_kcmp, R_qz], writes=[rbk])
                tr.op("act", lambda e: e.activation(out=Pc[nt], in_=bank[:, 0:512], func=AF.Exp), reads=[rbk], writes=[R_Pc[nt]])
                tr.op("dve", lambda e: e.tensor_tensor(out=v4(Pc[nt]), in0=v4(Pc[nt]), in1=bc4(cm[:, nt, :]), op=ALU.mult),
                      reads=[R_Pc[nt], R_cm], writes=[R_Pc[nt]])

        def h1():
            for nt in range(NCT):
                tr.op("pe", lambda e: e.matmul(bOc[0:65, 0:512], lhsT=self.vcmp1[:, nt, g, :], rhs=Pc[nt], start=(nt == 0), stop=(nt == NCT - 1)),
                      reads=[self.R_vcmp, R_Pc[nt]], writes=[rOc])
            for hh in range(2):
                bk, rk = bI[hh]
                for h2_ in range(2):
                    h = 2 * hh + h2_
                    for nt in range(NCT):
                        tr.op("pe", lambda e: e.matmul(bk[:, h2_ * (NB + 1):(h2_ + 1) * (NB + 1)], lhsT=Pc[nt][:, h * 128:(h + 1) * 128],
                                                       rhs=self.cover1[:, nt, :], start=(nt == 0), stop=(nt == NCT - 1)),
                              reads=[R_Pc[nt], self.R_const], writes=[rk])
            for hh in range(2):
                bk, rk = bI[hh]
                tr.op("dve", lambda e: e.tensor_scalar(out=zc[:, 2 * hh:2 * hh + 2], in0=bk[:, NB:2 * (NB + 1):NB + 1], scalar1=TINY, scalar2=None,
                                                        op0=ALU.max),
                      reads=[rk], writes=[R_m8])
            tr.op("dve", lambda e: e.reciprocal(out=rz, in_=zc), reads=[R_m8], writes=[R_m8])
            for h in range(4):
                bk, rk = bI[h // 2]
                src = bk[:, (h % 2) * (NB + 1):(h % 2) * (NB + 1) + NB]
                if h == 0:
                    tr.op("dve", lambda e: e.scalar_tensor_tensor(out=imp, in0=src, scalar=rz[:, 0:1], in1=self.sm("fbias"), op0=ALU.mult, op1=ALU.add),
                          reads=[rk, R_m8, self.R_sm], writes=[R_imp])
                else:
                    tr.op("dve", lambda e: e.scalar_tensor_tensor(out=imp, in0=src, scalar=rz[:, h:h + 1], in1=imp, op0=ALU.mult, op1=ALU.add),
                          reads=[rk, R_m8, R_imp], writes=[R_imp])
            lo0 = max(0, 2 * vq - 1)
            tr.op("dve", lambda e: e.memset(imp[0:64, lo0:2 * vq + 1], 1e9), reads=[R_imp], writes=[R_imp])
            tr.op("dve", lambda e: e.memset(imp[64:128, 2 * vq:2 * vq + 2], 1e9), reads=[R_imp], writes=[R_imp])
            tr.op("dve", lambda e: e.max(out=m8[:, 0:8], in_=imp), reads=[R_imp], writes=[R_m8])
            tr.op("dve", lambda e: e.match_replace(out=impw, in_to_replace=m8[:, 0:8], in_values=imp, imm_value=-1e30),
                  reads=[R_imp, R_m8], writes=[R_sel])
            tr.op("dve", lambda e: e.max(out=m8[:, 8:16], in_=impw), reads=[R_sel], writes=[R_m8])
            tr.op("dve", lambda e: e.tensor_scalar(out=sel, in0=imp, scalar1=m8[:, 15:16], scalar2=None, op0=ALU.is_ge),
                  reads=[R_imp, R_m8, R_sel], writes=[R_sel])

        def h2():
            bankT, rbT = bI[0]
            bbT = self.bfv(bankT)
            tr.op("pe", lambda e: e.transpose(out=bbT[0:NB, 0:128], in_=sel, identity=ident_b), reads=[R_sel, self.R_const], writes=[rbT])
            for hb in range(2):
                if 64 * hb >= NB:
                    continue
                rows = slice(64 * hb, min(NB, 64 * hb + 64))
                nr = rows.stop - rows.start
                tr.op("dve", lambda e: e.tensor_scalar(out=v4(negm[:, hb, :])[rows],
                                                        in0=bbT[rows, 0:128].unsqueeze(1).broadcast_to([nr, 4, 128]),
                                                        scalar1=30000.0, scalar2=-30000.0, op0=ALU.mult, op1=ALU.add),
                      reads=[rbT], writes=R_negm)
            tr.op("act", lambda e: e.copy(out=OT[0][0:65, :], in_=bOc[0:65, 0:512]), reads=[rOc], writes=R_OT[0])

        steps = []
        kts = [kt for kt in range(vq - 4, vq + 1) if kt >= 0]
        for i, kt in enumerate(kts):
            def wf(j, i=i, kt=kt):
                w = kt - cfg.WT0
                masked = (kt == vq or kt == vq - 4)
                bank, rbk = self.bank_mm()
                tr.op("pe", lambda e: e.matmul(bank[:, 0:512], lhsT=self.kwT[:, w * 128:(w + 1) * 128], rhs=qz, start=True, stop=(not masked)),
                      reads=[self.R_kw[w], R_qz], writes=[rbk])
                if masked:
                    mi = 0 if kt == vq else 1
                    tr.op("pe", lambda e: e.matmul(bank[:, 0:512], lhsT=ident_b, rhs=self.negc[:, mi, :], start=False, stop=True),
                          reads=[self.R_const], writes=[rbk])
                tr.op("act", lambda e: e.activation(out=E4[j % 4], in_=bank[:, 0:512], func=AF.Exp), reads=[rbk], writes=[R_E4[j % 4]])

            def wb(j, i=i, kt=kt):
                w = kt - cfg.WT0
                tr.op("pe", lambda e: e.matmul(bOw[0:65, 0:512], lhsT=self.V1w[:, w, g, :], rhs=E4[j % 4], start=(i == 0), stop=(i == len(kts) - 1)),
                      reads=[self.R_vw[w], R_E4[j % 4]], writes=[rOw])
            steps.append((wf, wb))
        nstep = vq + 1
        for kt in range(nstep):
            def sf(j, kt=kt):
                bank, rbk = self.bank_mm()
                tr.op("pe", lambda e: e.matmul(bank[:, 0:512], lhsT=self.ksT[:, kt * 128:(kt + 1) * 128], rhs=qz, start=True, stop=False),
                      reads=[self.R_ks[kt], R_qz], writes=[rbk])
                r = (2 * kt) // 64
                jj = kt % 32
                tr.op("pe", lambda e: e.matmul(bank[:, 0:512], lhsT=self.eec[:, jj, :], rhs=negm[:, r, :],
                                               start=False, stop=(kt != vq)),
                      reads=[self.R_const] + R_negm, writes=[rbk])
                if kt == vq:
                    tr.op("pe", lambda e: e.matmul(bank[:, 0:512], lhsT=ident_b, rhs=self.negc[:, 0, :], start=False, stop=True),
                          reads=[self.R_const], writes=[rbk])
                tr.op("act", lambda e: e.activation(out=E4[j % 4], in_=bank[:, 0:512], func=AF.Exp), reads=[rbk], writes=[R_E4[j % 4]])

            def sb(j, kt=kt):
                tr.op("pe", lambda e: e.matmul(bOs[0:65, 0:512], lhsT=self.V1s[:, kt, g, :], rhs=E4[j % 4], start=(kt == 0), stop=(kt == nstep - 1)),
                      reads=[self.R_vs[kt], R_E4[j % 4]], writes=[rOs])
            steps.append((sf, sb))

        def run(inject):
            n = len(steps)
            for i in range(n + SKEW):
                if i < n:
                    steps[i][0](i)
                if i >= SKEW:
                    steps[i - SKEW][1](i - SKEW)
                for fn in inject.get(i, ()):
                    fn()
            for i in sorted(k for k in inject if k >= n + SKEW):
                for fn in inject[i]:
                    fn()
            tr.op("dve", lambda e: e.tensor_copy(out=OT[1][0:65, :], in_=bOs[0:65, 0:512]), reads=[rOs], writes=R_OT[1])
            tr.op("act", lambda e: e.copy(out=OT[2][0:65, :], in_=bOw[0:65, 0:512]), reads=[rOw], writes=R_OT[2])

        def EP():
            for hh in range(2):
                bankE, rbE = bI[hh]
                for c2 in range(2):
                    c = 2 * hh + c2
                    for b in range(3):
                        k = c2 * 3 + b
                        tr.op("pe", lambda e: e.transpose(out=bankE[:, k * 65:(k + 1) * 65], in_=OT[b][0:65, c * 128:(c + 1) * 128],
                                                          identity=self.ident_f[0:65, 0:65]),
                              reads=R_OT[b] + [self.R_const], writes=[rbE])
                tr.op("dve", lambda e: e.tensor_scalar(out=zz, in0=bankE[:, 64:390:65], scalar1=TINY, scalar2=None, op0=ALU.max),
                      reads=[rbE], writes=[R_coef])
                tr.op("dve", lambda e: e.reciprocal(out=zz, in_=zz), reads=[R_coef], writes=[R_coef])
                tr.op("dve", lambda e: e.tensor_tensor(out=coef, in0=zz, in1=self.g_sb[:, t, 12 * g + 6 * hh:12 * g + 6 * hh + 6], op=ALU.mult),
                      reads=[R_coef, self.R_g], writes=[R_coef])
                tr.op("dve", lambda e: e.tensor_tensor(out=etmp.rearrange("p (k d) -> p k d", d=64),
                                                       in0=bankE[:, 0:390].rearrange("p (k d) -> p k d", d=65)[:, :, 0:64],
                                                       in1=coef.unsqueeze(2).broadcast_to([128, 6, 64]), op=ALU.mult),
                      reads=[rbE, R_coef], writes=[R_et])
                o0 = 256 * g + 128 * hh
                with self.nc.allow_low_precision(reason="fp32 reduce, bf16 store"):
                    tr.op("dve", lambda e: e.tensor_reduce(out=otok[:, o0:o0 + 128].rearrange("p (c d) -> p c d", c=2),
                                                           in_=etmp.rearrange("p (c b d) -> p c d b", c=2, b=3), axis=AX.X, op=ALU.add),
                          reads=[R_et], writes=[R_otok])
            if g == 1:
                bankF, rbF = self.bank_mm()
                bbF = self.bfv(bankF)
                for cc in range(4):
                    tr.op("pe", lambda e: e.transpose(out=bbF[:, cc * 128:(cc + 1) * 128], in_=otok[:, cc * 128:(cc + 1) * 128], identity=ident_b),
                          reads=[R_otok, self.R_const], writes=[rbF])
                tr.op("act", lambda e: e.copy(out=self.bufM[:, 8:12, tq], in_=bbF[:, 0:512].rearrange("p (c q) -> p c q", c=4)),
                      reads=[rbF], writes=self.R_M[8:12])

        return dict(h0=h0, h1=h1, h2=h2, run=run, EP=EP, n=len(steps))

    def mem_attn(self, c0, c1):
        cfg, tr = self.cfg, self.tr
        S = self.bufS
        n = c1 - c0
        E = [S[:, 1280 + i * 256:1280 + (i + 1) * 256].bitcast(BF16) for i in range(3)]
        R_E = [self.RS("E%d" % i) for i in range(3)]
        rzm = S[:, 2904:2904 + 512]
        R_rz = self.RS("OT0")
        for h in range(4):
            for mt in range(2):
                bank, rbk = self.bank_mm()
                tr.op("pe", lambda e: e.matmul(bank[:, 0:n], lhsT=self.kmemT[:, h, mt * 128:(mt + 1) * 128], rhs=self.bufQ[:, 4 + h, c0:c1],
                                               start=True, stop=True),
                      reads=[self.R_kmem, self.R_Q[4 + h]], writes=[rbk])
                tr.op("act", lambda e: e.activation(out=E[mt][:, 0:n], in_=bank[:, 0:n], func=AF.Exp), reads=[rbk], writes=[R_E[mt]])
            bO, rO = self.ps[4], self.R_ps[4]
            bZ, rZ = self.ps[5], self.R_ps[5]
            for mt in range(2):
                tr.op("pe", lambda e: e.matmul(bO[:, 0:n], lhsT=self.vmem[:, mt, h, :], rhs=E[mt][:, 0:n], start=(mt == 0), stop=(mt == 1)),
                      reads=[self.R_vmem, R_E[mt]], writes=[rO])
            for mt in range(2):
                tr.op("pe", lambda e: e.matmul(bZ[:, 0:n], lhsT=self.ones_bf[:, :], rhs=E[mt][:, 0:n], start=(mt == 0), stop=(mt == 1)),
                      reads=[self.R_const, R_E[mt]], writes=[rZ])
            tr.op("dve", lambda e: e.reciprocal(out=rzm[:, 0:n], in_=bZ[:, 0:n]), reads=[rZ], writes=[R_rz])
            tr.op("dve", lambda e: e.tensor_tensor(out=self.bufM[:, 12 + h, c0:c1], in0=bO[:, 0:n], in1=rzm[:, 0:n], op=ALU.mult),
                  reads=[rO, R_rz], writes=[self.R_M[12 + h]])

    def out_proj_ln1(self, G, TR):
        cfg, tr = self.cfg, self.tr
        tiles = list(range(TR[0], TR[1]))
        acc = {t: (self.ps[4 + i], self.R_ps[4 + i]) for i, t in enumerate(tiles)}
        gbuf0 = self.load_gb(0)
        for t in tiles:
            tr.op("dve", lambda e: e.scalar_tensor_tensor(out=self.xt[:, t, :], in0=self.xt[:, t, :], scalar=self.mv[:, t, 0:1], in1=gbuf0[:, 0:D],
                                                           op0=ALU.subtract, op1=ALU.mult),
                  reads=[self.R_xt[t], self.R_mv] + self.R_A[0:8], writes=[self.R_xt[t]])
            tr.op("dve", lambda e: e.scalar_tensor_tensor(out=self.xt[:, t, :], in0=self.xt[:, t, :], scalar=self.lnt[:, 1, t:t + 1], in1=gbuf0[:, D:2 * D],
                                                           op0=ALU.mult, op1=ALU.add),
                  reads=[self.R_xt[t], self.R_lnt] + self.R_A[8:16], writes=[self.R_xt[t]])
        for cb in range(4):
            for kq in range(4):
                buf, rb = self.fetch(P_WOUT + cb * 4 + kq)
                wv = buf.rearrange("p (a b) -> p a b", b=512)
                for t in tiles:
                    bk, rk = acc[t]
                    for kci in range(4):
                        kc = kq * 4 + kci
                        tr.op("pe", lambda e: e.matmul(bk[:, 0:512], lhsT=self.bufM[:, kc, t * 128:(t + 1) * 128], rhs=wv[:, kci, :],
                                                       start=(kc == 0), stop=(kc == KC - 1)),
                              reads=[rb, self.R_M[kc]], writes=[rk])
            for t in tiles:
                bk, rk = acc[t]
                xs = self.xt[:, t, cb * 512:(cb + 1) * 512]
                tr.op("dve", lambda e: e.scalar_tensor_tensor(out=xs, in0=xs, scalar=ALPHA, in1=bk[:, 0:512], op0=ALU.mult, op1=ALU.add),
                      reads=[rk, self.R_xt[t]], writes=[self.R_xt[t]])
        gbuf = self.load_gb(1)
        self.ln_stats(tiles)
        for t in tiles:
            self.ln_apply(t, gbuf, "g1", "b1", want_tok=True)
        self.ln_transpose(tiles, "g1", "b1")

    def ffn(self, G, TR):
        cfg, tr = self.cfg, self.tr
        S = self.bufS
        c0, c1 = TR[0] * 128, TR[1] * 128
        n = c1 - c0
        if TR != (0, 4):
            for c in range(88):
                bank, rbk = self.proj(P_WUP + c, c1 - 2, c1)
                tr.op("dve", lambda e: e.tensor_scalar(out=self.ftail[:, c, :], in0=bank[:, 0:2], scalar1=self.sm("gflag", G), scalar2=None,
                                                        op0=ALU.mult),
                      reads=[rbk, self.R_sm], writes=[self.R_ftail[c]])
            return
        ext = [[S[:, (2 * w + b) * 516:(2 * w + b + 1) * 516] for b in range(2)] for w in range(2)]
        cv = [[S[:, 2064 + (2 * w + b) * 512:2064 + (2 * w + b + 1) * 512] for b in range(2)] for w in range(2)]
        ga = [S[:, 4112 + b * 512:4112 + (b + 1) * 512] for b in range(2)]
        R_ext = [[self.RS("ext%d%d" % (w, b)) for b in range(2)] for w in range(2)]
        R_cv = [[self.RS("cv%d%d" % (w, b)) for b in range(2)] for w in range(2)]
        R_ga = [self.RS("ga%d" % b) for b in range(2)]
        actT = self.bufA
        fcw = lambda cc, k: self.sm("fcw", cc * 3 + k)
        acc = [(self.ps[4 + t], self.R_ps[4 + t]) for t in range(4)]
        for third, (lo, hi) in enumerate(((0, 16), (16, 32), (32, 44))):
            for c in range(lo, hi):
                b = c % 2
                for w, cc in ((0, c), (1, 44 + c)):
                    bank, rbk = self.proj(P_WUP + cc, 0, 512)
                    ex, rex = ext[w][b], R_ext[w][b]
                    cvv, rcv = cv[w][b], R_cv[w][b]
                    tr.op("pool", lambda e: e.tensor_copy(out=ex[:, 0:2], in_=self.ftail[:, cc, :]), reads=[self.R_ftail[cc]], writes=[rex])
                    tr.op("act", lambda e: e.copy(out=ex[:, 2:514], in_=bank[:, 0:512]), reads=[rbk], writes=[rex])
                    tr.op("pool", lambda e: e.tensor_copy(out=self.ftail[:, cc, :], in_=ex[:, 512:514]), reads=[rex], writes=[self.R_ftail[cc]])
                    tr.op("pool", lambda e: e.tensor_scalar(out=cvv, in0=ex[:, 2:514], scalar1=fcw(cc, 2), scalar2=self.sm("fcb", cc),
                                                             op0=ALU.mult, op1=ALU.add),
                          reads=[rex, self.R_sm], writes=[rcv])
                    for k in (1, 0):
                        tr.op("dve", lambda e: e.scalar_tensor_tensor(out=cvv, in0=ex[:, k:k + 512], scalar=fcw(cc, k), in1=cvv,
                                                                       op0=ALU.mult, op1=ALU.add),
                              reads=[rex, rcv, self.R_sm], writes=[rcv])
                tr.op("act", lambda e: e.activation(out=ga[b], in_=cv[0][b], func=AF.Gelu_apprx_tanh), reads=[R_cv[0][b]], writes=[R_ga[b]])
                tr.op("dve", lambda e: e.tensor_tensor(out=actT[:, c - lo, :], in0=ga[b], in1=cv[1][b], op=ALU.mult),
                      reads=[R_ga[b], R_cv[1][b]], writes=[self.R_A[c - lo]])
            for cb in range(4):
                for fq in range(lo // 4, hi // 4):
                    buf, rb = self.fetch(P_WDN + cb * 11 + fq)
                    wv = buf.rearrange("p (a b) -> p a b", b=512)
                    for t in range(4):
                        bk, rk = acc[t]
                        for fi in range(4):
                            ffc = fq * 4 + fi
                            tr.op("pe", lambda e: e.matmul(bk[:, 0:512], lhsT=actT[:, ffc - lo, t * 128:(t + 1) * 128], rhs=wv[:, fi, :],
                                                           start=(ffc == lo), stop=(ffc == hi - 1)),
                                  reads=[rb, self.R_A[ffc - lo]], writes=[rk])
                for t in range(4):
                    bk, rk = acc[t]
                    xs = self.xt[:, t, cb * 512:(cb + 1) * 512]
                    if third == 0:
                        tr.op("dve", lambda e: e.scalar_tensor_tensor(out=xs, in0=xs, scalar=ALPHA, in1=bk[:, 0:512], op0=ALU.mult, op1=ALU.add),
                              reads=[rk, self.R_xt[t]], writes=[self.R_xt[t]])
                    else:
                        tr.op("dve", lambda e: e.tensor_tensor(out=xs, in0=xs, in1=bk[:, 0:512], op=ALU.add),
                              reads=[rk, self.R_xt[t]], writes=[self.R_xt[t]])
        gbuf = self.load_gb(2)
        self.ln_stats([0, 1, 2, 3])
        for t in range(4):
            self.ln_apply(t, gbuf, "g2", "b2", want_tok=True, want_T=False)
            r0 = (G - cfg.G0) * GT + t * 128
            tr.dma("pool", self.y[r0:r0 + 128, :], self.xt[:, t, :], reads=[self.R_xt[t]], writes=[self.R_y])


def kernel(**inputs):
    cfg = Cfg(SEQ=np.asarray(inputs["x"]).shape[1])
    prog = Prog(cfg)
    prog.emit(dry=True)
    prog.emit(dry=False)
    sh = prep_shared(cfg, inputs)
    in_maps = [prep_core(cfg, inputs, sh, c) for c in range(8)]
    res = run_bass_kernel_spmd(prog.nc, in_maps, core_ids=list(range(8)))
    B = 2
    out = np.zeros((B, cfg.SEQ, D), np.float32)
    for c in range(8):
        b, j = c // 4, c % 4
        out[b, j * cfg.CH:(j + 1) * cfg.CH] = np.asarray(res.results[c]["y"], np.float32)
    return out
```
